# Optimizing a Trainium2 kernel written in Bass

```python
import math
import jax, jax.numpy as jnp
from jax import lax
import numpy as np

D_MODEL = 1024
BATCH = 8
SEQ = 2048
DEPTH = 2
DEC_BATCH = 16
DEC_SEQ = 32
PAST_LEN = 4096

CHUNK = 64
CONV_W = 4
EPS = 1e-6
LRU_HEADS = 4
LRU_WIDTH = 256
LRU_BLOCK = LRU_WIDTH // LRU_HEADS
LRU_C = 8.0
FOX_HEADS = 8
FOX_HEAD_DIM = 64
FOX_WIDTH = FOX_HEADS * FOX_HEAD_DIM
Q_BLOCK = 128
SSD_HEADS = 4
SSD_HEAD_DIM = 64
SSD_WIDTH = SSD_HEADS * SSD_HEAD_DIM
SSD_GROUPS = 2
SSD_HPG = SSD_HEADS // SSD_GROUPS
D_STATE = 128
SSD_CONV_DIM = SSD_WIDTH + 2 * SSD_GROUPS * D_STATE
SSD_BLOCK = CHUNK
D_MIX = LRU_WIDTH + FOX_WIDTH + SSD_WIDTH
IN_SIZES = (LRU_WIDTH, LRU_WIDTH, FOX_WIDTH, FOX_WIDTH, FOX_WIDTH, FOX_HEADS, SSD_WIDTH, SSD_CONV_DIM, SSD_HEADS)
IN_OFFSETS = tuple(int(v) for v in np.cumsum(IN_SIZES)[:-1])
D_IN = int(sum(IN_SIZES))
D_FF = 2816
N_SUB = 3
STATE_KEYS = ("fox_k", "fox_v", "fox_logf", "lru_conv", "lru_h", "ssd_conv", "ssd_h")

kernel_name = "hybrid_streaming_encoder_step"

F32 = jnp.float32


def rmsnorm(x, g):
    xf = x.astype(F32)
    r = lax.rsqrt(jnp.mean(xf * xf, axis=-1, keepdims=True) + EPS)
    return (xf * r).astype(x.dtype) * g


def causal_conv(x, prev, w, b):
    L = x.shape[1]
    xp = jnp.concatenate([prev.astype(x.dtype), x], axis=1)
    y = b + w[0] * xp[:, 0:L]
    for k in range(1, CONV_W):
        y = y + w[k] * xp[:, k:k + L]
    return y, xp[:, -(CONV_W - 1):]


def swiglu(h, wg, wu, wd):
    return (jax.nn.silu(h @ wg) * (h @ wu)) @ wd


def rg_lru(x, h0, wa, ba, wx, bx, lam):
    b_, L, _ = x.shape
    xf = x.astype(F32)
    xb = xf.reshape(b_, L, LRU_HEADS, LRU_BLOCK)
    r = jax.nn.sigmoid(jnp.einsum('blhi,hij->blhj', xb, wa.astype(F32)).reshape(b_, L, LRU_WIDTH) + ba)
    i = jax.nn.sigmoid(jnp.einsum('blhi,hij->blhj', xb, wx.astype(F32)).reshape(b_, L, LRU_WIDTH) + bx)
    log_a = -LRU_C * r * jax.nn.softplus(-lam.astype(F32))
    a = jnp.exp(log_a)
    u = jnp.sqrt(-jnp.expm1(2.0 * log_a)) * (i * xf)

    def comb(left, right):
        al, bl = left
        ar, br = right
        return al * ar, ar * bl + br

    a_cum, b_cum = lax.associative_scan(comb, (a, u), axis=1)
    h = a_cum * h0.astype(F32)[:, None] + b_cum
    return h, h[:, -1]


def fox_block(q, k, v, fq, fk, pos_q, pos_k):
    s = jnp.einsum('bqhd,bkhd->bhqk', q, k, preferred_element_type=F32) * (FOX_HEAD_DIM ** -0.5)
    s = s + (jnp.transpose(fq, (0, 2, 1))[:, :, :, None] - jnp.transpose(fk, (0, 2, 1))[:, :, None, :])
    mask = pos_k[None, :] <= pos_q[:, None]
    s = jnp.where(mask, s, -1e30)
    p = jax.nn.softmax(s, axis=-1)
    return jnp.einsum('bhqk,bkhd->bqhd', p.astype(v.dtype), v)


def fox_prompt(q, k, v, logf):
    b_, S, H, dh = q.shape
    F = jnp.cumsum(logf, axis=1)
    nblk = S // Q_BLOCK
    qb = jnp.transpose(q.reshape(b_, nblk, Q_BLOCK, H, dh), (1, 0, 2, 3, 4))
    fb = jnp.transpose(F.reshape(b_, nblk, Q_BLOCK, H), (1, 0, 2, 3))
    pos_q = jnp.arange(S, dtype=jnp.int32).reshape(nblk, Q_BLOCK)
    pos_k = jnp.arange(S, dtype=jnp.int32)
    out = lax.map(lambda a: fox_block(a[0], k, v, a[1], F, a[2], pos_k), (qb, fb, pos_q))
    return jnp.transpose(out, (1, 0, 2, 3, 4)).reshape(b_, S, H, dh)


def fox_sample(q, k, v, logf, ck, cv, clogf):
    P = ck.shape[1]
    T = q.shape[1]
    k_all = jnp.concatenate([ck.astype(k.dtype), k], axis=1)
    v_all = jnp.concatenate([cv.astype(v.dtype), v], axis=1)
    F = jnp.cumsum(jnp.concatenate([clogf.astype(F32), logf], axis=1), axis=1)
    pos_k = jnp.arange(P + T, dtype=jnp.int32)
    pos_q = P + jnp.arange(T, dtype=jnp.int32)
    return fox_block(q, k_all, v_all, F[:, P:], F, pos_q, pos_k)


def ssd(x, dt, A, Bm, Cm, Dp, h0, block):
    b_, L = x.shape[:2]
    nc = L // block
    xg = x.reshape(b_, nc, block, SSD_GROUPS, SSD_HPG, SSD_HEAD_DIM)
    dtg = dt.reshape(b_, nc, block, SSD_GROUPS, SSD_HPG)
    Bc = Bm.reshape(b_, nc, block, SSD_GROUPS, D_STATE)
    Cc = Cm.reshape(b_, nc, block, SSD_GROUPS, D_STATE)
    cum = jnp.cumsum(dtg * A.reshape(SSD_GROUPS, SSD_HPG), axis=2)
    seg = cum[:, :, :, None] - cum[:, :, None, :]
    causal = jnp.tril(jnp.ones((block, block), dtype=bool))[:, :, None, None]
    Lmat = jnp.exp(jnp.where(causal, seg, -jnp.inf))
    CB = jnp.einsum('bcqgn,bcsgn->bcqsg', Cc, Bc)
    dx = dtg[..., None] * xg
    y_diag = jnp.einsum('bcqsgh,bcsghp->bcqghp', CB[..., None] * Lmat, dx)
    decay_end = jnp.exp(cum[:, :, -1:] - cum)
    states = jnp.einsum('bcsgn,bcsghp->bcghpn', Bc, decay_end[..., None] * dx)
    chunk_decay = jnp.exp(cum[:, :, -1])

    def step(h, inp):
        st, dec = inp
        return dec[..., None, None] * h + st, h

    h0g = h0.astype(F32).reshape(b_, SSD_GROUPS, SSD_HPG, SSD_HEAD_DIM, D_STATE)
    h_last, h_starts = lax.scan(step, h0g, (jnp.transpose(states, (1, 0, 2, 3, 4, 5)),
                                            jnp.transpose(chunk_decay, (1, 0, 2, 3))))
    h_starts = jnp.transpose(h_starts, (1, 0, 2, 3, 4, 5))
    y_off = jnp.einsum('bcqgn,bcghpn->bcqghp', Cc, h_starts) * jnp.exp(cum)[..., None]
    y = y_diag + y_off + Dp.astype(F32).reshape(SSD_GROUPS, SSD_HPG)[..., None] * xg
    return y.reshape(b_, L, SSD_HEADS, SSD_HEAD_DIM), h_last.reshape(b_, SSD_HEADS, SSD_HEAD_DIM, D_STATE)


def token_mix(h, lp, prev, ssd_block):
    b_, L, _ = h.shape
    proj = h @ lp["w_in"]
    lru_x, lru_g, q, k, v, f_raw, z, xbc, dt_raw = jnp.split(proj, IN_OFFSETS, axis=-1)
    u, lru_conv_new = causal_conv(lru_x, prev["lru_conv"], lp["lru_conv_w"], lp["lru_conv_b"])
    hA, lru_h_new = rg_lru(u, prev["lru_h"], lp["lru_wa"], lp["lru_ba"], lp["lru_wx"], lp["lru_bx"], lp["lru_lambda"])
    yA = hA.astype(h.dtype) * jax.nn.gelu(lru_g)
    q = q.reshape(b_, L, FOX_HEADS, FOX_HEAD_DIM)
    k = k.reshape(b_, L, FOX_HEADS, FOX_HEAD_DIM)
    v = v.reshape(b_, L, FOX_HEADS, FOX_HEAD_DIM)
    logf = jax.nn.log_sigmoid((f_raw + lp["fox_f_bias"]).astype(F32))
    if prev["fox_k"] is None:
        o = fox_prompt(q, k, v, logf)
    else:
        o = fox_sample(q, k, v, logf, prev["fox_k"], prev["fox_v"], prev["fox_logf"])
    yB = o.reshape(b_, L, FOX_WIDTH).astype(h.dtype)
    xbc, ssd_conv_new = causal_conv(xbc, prev["ssd_conv"], lp["ssd_conv_w"], lp["ssd_conv_b"])
    xbc = jax.nn.silu(xbc).astype(F32)
    xs, Bm, Cm = jnp.split(xbc, (SSD_WIDTH, SSD_WIDTH + SSD_GROUPS * D_STATE), axis=-1)
    dt = jax.nn.softplus((dt_raw + lp["ssd_dt_bias"]).astype(F32))
    A = -jnp.exp(lp["ssd_a_log"].astype(F32))
    yc, ssd_h_new = ssd(xs.reshape(b_, L, SSD_HEADS, SSD_HEAD_DIM), dt, A,
                        Bm.reshape(b_, L, SSD_GROUPS, D_STATE), Cm.reshape(b_, L, SSD_GROUPS, D_STATE),
                        lp["ssd_d"], prev["ssd_h"], ssd_block)
    yC = rmsnorm(yc.reshape(b_, L, SSD_WIDTH).astype(h.dtype) * jax.nn.silu(z), lp["ssd_norm_w"])
    y = jnp.concatenate([yA, yB, yC], axis=-1) @ lp["w_out"]
    new = {"fox_k": k, "fox_v": v, "fox_logf": logf, "lru_conv": lru_conv_new, "lru_h": lru_h_new,
           "ssd_conv": ssd_conv_new, "ssd_h": ssd_h_new}
    return y, new


def trunk(x, c, caches, params):
    b_, L, _ = x.shape
    ssd_block = SSD_BLOCK if caches is None else L
    new_states = {name: [] for name in STATE_KEYS}
    for l in range(DEPTH):
        lp = {name: arr[l] for name, arr in params.items()}
        if caches is None:
            prev = {"fox_k": None, "fox_v": None, "fox_logf": None,
                    "lru_conv": jnp.zeros((b_, CONV_W - 1, LRU_WIDTH), x.dtype),
                    "lru_h": jnp.zeros((b_, LRU_WIDTH), F32),
                    "ssd_conv": jnp.zeros((b_, CONV_W - 1, SSD_CONV_DIM), x.dtype),
                    "ssd_h": jnp.zeros((b_, SSD_HEADS, SSD_HEAD_DIM, D_STATE), F32)}
        else:
            prev = {name: arr[l] for name, arr in caches.items()}
        mod = (jax.nn.silu(c) @ lp["w_mod"] + lp["b_mod"]).reshape(b_, N_SUB, 3, D_MODEL)
        shift, scale, gate = mod[:, :, 0, None], mod[:, :, 1, None], mod[:, :, 2, None]

        def pre(x_, j):
            return rmsnorm(x_, lp["norm_pre"][j]) * (1.0 + scale[:, j]) + shift[:, j]

        def post(x_, y_, j, w):
            return x_ + w * gate[:, j] * rmsnorm(y_, lp["norm_post"][j])

        y = swiglu(pre(x, 0), lp["ffn_w_gate"][0], lp["ffn_w_up"][0], lp["ffn_w_down"][0])
        x = post(x, y, 0, 0.5)
        y, st = token_mix(pre(x, 1), lp, prev, ssd_block)
        x = post(x, y, 1, 1.0)
        y = swiglu(pre(x, 2), lp["ffn_w_gate"][1], lp["ffn_w_up"][1], lp["ffn_w_down"][1])
        x = post(x, y, 2, 0.5)
        for name in STATE_KEYS:
            new_states[name].append(st[name])
    return x, {name: jnp.stack(v, axis=0) for name, v in new_states.items()}


def setup_inputs(seed: int = 0) -> dict:
    key = jax.random.key(seed)
    ks = iter(jax.random.split(key, 48))

    def nrm(shape, s=1.0):
        return s * jax.random.normal(next(ks), shape, F32)

    def unif(shape, lo, hi):
        return jax.random.uniform(next(ks), shape, F32, lo, hi)

    a_root = unif((DEPTH, LRU_WIDTH), 0.9, 0.999) ** (1.0 / LRU_C)
    lru_lambda = jnp.log(a_root) - jnp.log1p(-a_root)
    dt0 = jnp.exp(unif((DEPTH, SSD_HEADS), math.log(1e-3), math.log(1e-1)))
    ssd_dt_bias = dt0 + jnp.log(-jnp.expm1(-dt0))
    return {
        "x_prompt": nrm((BATCH, SEQ, D_MODEL)),
        "x_sample": nrm((DEC_BATCH, DEC_SEQ, D_MODEL)),
        "c_prompt": nrm((BATCH, D_MODEL)),
        "c_sample": nrm((DEC_BATCH, D_MODEL)),
        "cache_fox_k": nrm((DEPTH, DEC_BATCH, PAST_LEN, FOX_HEADS, FOX_HEAD_DIM)),
        "cache_fox_v": nrm((DEPTH, DEC_BATCH, PAST_LEN, FOX_HEADS, FOX_HEAD_DIM)),
        "cache_fox_logf": jax.nn.log_sigmoid(nrm((DEPTH, DEC_BATCH, PAST_LEN, FOX_HEADS)) + 3.0),
        "state_lru_conv": nrm((DEPTH, DEC_BATCH, CONV_W - 1, LRU_WIDTH)),
        "state_lru_h": nrm((DEPTH, DEC_BATCH, LRU_WIDTH), 0.5),
        "state_ssd_conv": nrm((DEPTH, DEC_BATCH, CONV_W - 1, SSD_CONV_DIM)),
        "state_ssd_h": nrm((DEPTH, DEC_BATCH, SSD_HEADS, SSD_HEAD_DIM, D_STATE), 0.1),
        "w_mod": nrm((DEPTH, D_MODEL, N_SUB * 3 * D_MODEL), 0.5 * D_MODEL ** -0.5),
        "b_mod": nrm((DEPTH, N_SUB * 3 * D_MODEL), 0.1),
        "norm_pre": 1.0 + nrm((DEPTH, N_SUB, D_MODEL), 0.05),
        "norm_post": 1.0 + nrm((DEPTH, N_SUB, D_MODEL), 0.05),
        "ffn_w_gate": nrm((DEPTH, 2, D_MODEL, D_FF), D_MODEL ** -0.5),
        "ffn_w_up": nrm((DEPTH, 2, D_MODEL, D_FF), D_MODEL ** -0.5),
        "ffn_w_down": nrm((DEPTH, 2, D_FF, D_MODEL), D_FF ** -0.5),
        "w_in": nrm((DEPTH, D_MODEL, D_IN), D_MODEL ** -0.5),
        "w_out": nrm((DEPTH, D_MIX, D_MODEL), D_MIX ** -0.5),
        "lru_conv_w": nrm((DEPTH, CONV_W, LRU_WIDTH), CONV_W ** -0.5),
        "lru_conv_b": nrm((DEPTH, LRU_WIDTH), 0.02),
        "lru_wa": nrm((DEPTH, LRU_HEADS, LRU_BLOCK, LRU_BLOCK), LRU_BLOCK ** -0.5),
        "lru_ba": nrm((DEPTH, LRU_WIDTH), 0.02),
        "lru_wx": nrm((DEPTH, LRU_HEADS, LRU_BLOCK, LRU_BLOCK), LRU_BLOCK ** -0.5),
        "lru_bx": nrm((DEPTH, LRU_WIDTH), 0.02),
        "lru_lambda": lru_lambda,
        "fox_f_bias": unif((DEPTH, FOX_HEADS), 1.0, 4.0),
        "ssd_conv_w": nrm((DEPTH, CONV_W, SSD_CONV_DIM), CONV_W ** -0.5),
        "ssd_conv_b": nrm((DEPTH, SSD_CONV_DIM), 0.02),
        "ssd_dt_bias": ssd_dt_bias,
        "ssd_a_log": jnp.log(unif((DEPTH, SSD_HEADS), 1.0, 16.0)),
        "ssd_d": 1.0 + nrm((DEPTH, SSD_HEADS), 0.1),
        "ssd_norm_w": 1.0 + nrm((DEPTH, SSD_WIDTH), 0.05),
    }


def reference(x_prompt, x_sample, c_prompt, c_sample, cache_fox_k, cache_fox_v, cache_fox_logf,
              state_lru_conv, state_lru_h, state_ssd_conv, state_ssd_h,
              w_mod, b_mod, norm_pre, norm_post, ffn_w_gate, ffn_w_up, ffn_w_down, w_in, w_out,
              lru_conv_w, lru_conv_b, lru_wa, lru_ba, lru_wx, lru_bx, lru_lambda, fox_f_bias,
              ssd_conv_w, ssd_conv_b, ssd_dt_bias, ssd_a_log, ssd_d, ssd_norm_w):
    params = {"w_mod": w_mod, "b_mod": b_mod, "norm_pre": norm_pre, "norm_post": norm_post,
              "ffn_w_gate": ffn_w_gate, "ffn_w_up": ffn_w_up, "ffn_w_down": ffn_w_down,
              "w_in": w_in, "w_out": w_out, "lru_conv_w": lru_conv_w, "lru_conv_b": lru_conv_b,
              "lru_wa": lru_wa, "lru_ba": lru_ba, "lru_wx": lru_wx, "lru_bx": lru_bx, "lru_lambda": lru_lambda,
              "fox_f_bias": fox_f_bias, "ssd_conv_w": ssd_conv_w, "ssd_conv_b": ssd_conv_b,
              "ssd_dt_bias": ssd_dt_bias, "ssd_a_log": ssd_a_log, "ssd_d": ssd_d, "ssd_norm_w": ssd_norm_w}
    caches = {"fox_k": cache_fox_k, "fox_v": cache_fox_v, "fox_logf": cache_fox_logf,
              "lru_conv": state_lru_conv, "lru_h": state_lru_h, "ssd_conv": state_ssd_conv, "ssd_h": state_ssd_h}
    y_prompt, sp = trunk(x_prompt, c_prompt, None, params)
    y_sample, ss = trunk(x_sample, c_sample, caches, params)
    return (y_prompt, y_sample,
            sp["fox_k"], sp["fox_v"], sp["fox_logf"], sp["lru_conv"], sp["lru_h"], sp["ssd_conv"], sp["ssd_h"],
            ss["fox_k"], ss["fox_v"], ss["fox_logf"], ss["lru_conv"], ss["lru_h"], ss["ssd_conv"], ss["ssd_h"])
```

```python
import os
import math
import numpy as np
from contextlib import ExitStack
import concourse.bass as bass
import concourse.mybir as mybir
from concourse.bass_utils import run_bass_kernel_spmd

F32 = mybir.dt.float32
BF16 = mybir.dt.bfloat16
AF = mybir.ActivationFunctionType
ALU = mybir.AluOpType

NCORES = 8
D = 1024
NPR = 2048
NSQ = 32
T = NPR + 2 * NSQ
DFF = 2816
DIN = 3084
PAST = 4096
EPS = 1e-6
TILES = [(0, 512), (512, 512), (1024, 512), (1536, 512), (2048, 64)]
SEGS = [(0, 2048, 0), (2048, 32, 1), (2080, 32, 2)]
NSMALL = 170
ENGS = ("pe", "act", "dve", "pool", "sp")
SEM_LIMIT = 30000
SWDGE_DEPTH = int(os.environ.get("MK_SWDGE", "3"))
STAGE = int(os.environ.get("MK_STAGE", "99"))
STAGE_SUB = int(os.environ.get("MK_SUB", "99"))
DBG = int(os.environ.get("MK_DBG", "0"))
FOXS = int(os.environ.get("MK_FOX", "99"))
TOG = os.environ.get("MK_TOG", "")
NRUN = int(os.environ.get("MK_CORES", "8"))


class Res:
    __slots__ = ("name", "last_w", "readers")

    def __init__(self, name):
        self.name = name
        self.last_w = None
        self.readers = []


class Op:
    __slots__ = ("idx", "eng", "fn", "deps", "is_dma", "key", "sig", "sem", "val", "epoch", "bar")

    def __init__(self, idx, eng, fn, is_dma, key):
        self.idx = idx
        self.eng = eng
        self.fn = fn
        self.deps = set()
        self.is_dma = is_dma
        self.key = key
        self.sig = False
        self.sem = None
        self.val = 0


class Prog:
    def __init__(self, nc):
        self.nc = nc
        self.ops = []
        self.barrier_deps = {e: set() for e in ENGS}
        self.last_on_eng = {e: None for e in ENGS}
        self.dma_last = {}
        self.epoch = 0

    def op(self, eng, fn, reads=(), writes=(), dma=False, key=None):
        idx = len(self.ops)
        o = Op(idx, eng, fn, dma, key)
        o.epoch = self.epoch
        o.bar = set()
        if self.barrier_deps[eng]:
            o.deps |= self.barrier_deps[eng]
            o.bar = set(self.barrier_deps[eng])
            self.barrier_deps[eng] = set()
        for r in reads:
            if r.last_w is not None:
                o.deps.add(r.last_w)
        for w in writes:
            if w.last_w is not None:
                o.deps.add(w.last_w)
            for rd in w.readers:
                o.deps.add(rd)
        for r in reads:
            r.readers.append(idx)
        for w in writes:
            w.last_w = idx
            w.readers = []
        o.deps.discard(idx)
        if dma:
            assert key is not None
            self.dma_last[key] = idx
        self.ops.append(o)
        self.last_on_eng[eng] = idx
        return o

    def barrier(self):
        s = set()
        for e in ENGS:
            if self.last_on_eng[e] is not None:
                s.add(self.last_on_eng[e])
        for k, v in self.dma_last.items():
            s.add(v)
        for e in ENGS:
            self.barrier_deps[e] |= s
        self.dma_last = {}
        self.epoch += 1
        for e in ENGS:
            self.op(e, lambda eng: eng.nop())

    def emit(self, stack):
        nc = self.nc
        ops = self.ops
        for o in ops:
            for d in o.deps:
                ops[d].sig = True
            if o.is_dma and o.eng == "pool":
                o.sig = True
        eng_sem, eng_cnt = {}, {}
        nsem = [0]

        def new_sem(nm):
            nsem[0] += 1
            return stack.enter_context(nc.semaphore(nm + str(nsem[0])))

        sw_keys = set()
        for o in ops:
            if o.is_dma and o.eng == "pool":
                sw_keys.add((o.key, o.epoch))
        pools = {True: [], False: []}
        limbo = {True: [], False: []}
        active, cur_epoch = {}, 0
        for o in ops:
            if o.epoch != cur_epoch:
                for sw in (True, False):
                    pools[sw].extend(limbo[sw])
                    limbo[sw] = []
                for (k, sw), sc in active.items():
                    limbo[sw].append(sc)
                active = {}
                cur_epoch = o.epoch
            if o.is_dma:
                sw = (o.key, o.epoch) in sw_keys
                if (o.key, sw) not in active:
                    active[(o.key, sw)] = pools[sw].pop() if pools[sw] else [new_sem("ds" if sw else "dh"), 0]
                sc = active[(o.key, sw)]
                sc[1] += 16
                o.sem, o.val, o.sig = sc[0], sc[1], True
            elif o.sig:
                e = o.eng
                if e not in eng_sem or eng_cnt[e] >= SEM_LIMIT:
                    eng_sem[e] = new_sem("e" + e)
                    eng_cnt[e] = 0
                eng_cnt[e] += 1
                o.sem = eng_sem[e]
                o.val = eng_cnt[e]
        self.n_sems = nsem[0]
        per_eng = {e: [o for o in ops if o.eng == e] for e in ENGS}
        block = stack.enter_context(nc.Block())

        def make(e):
            lst = per_eng[e]

            def body(eng):
                waited = {}
                issued = []
                for o in lst:
                    need = {}
                    if e == "pool" and o.is_dma:
                        if len(issued) >= SWDGE_DEPTH:
                            p = issued[-SWDGE_DEPTH]
                            if waited.get(id(p.sem), 0) < p.val:
                                need[id(p.sem)] = (p.sem, p.val)
                        issued.append(o)
                    for d in o.deps:
                        p = ops[d]
                        if e == "pe" and p.eng == "pe" and not p.is_dma:
                            continue
                        if p.epoch < o.epoch and d not in o.bar:
                            continue
                        sid = id(p.sem)
                        if waited.get(sid, 0) >= p.val:
                            continue
                        if sid not in need or need[sid][1] < p.val:
                            need[sid] = (p.sem, p.val)
                    for sid, (sem, val) in need.items():
                        eng.wait_ge(sem, val)
                        waited[sid] = val
                    ins = o.fn(eng)
                    if o.sig:
                        ins.then_inc(o.sem, 16 if o.is_dma else 1)
            return body

        if per_eng["pe"]:
            block.tensor(make("pe"))
        if per_eng["act"]:
            block.scalar(make("act"))
        if per_eng["dve"]:
            block.vector(make("dve"))
        if per_eng["pool"]:
            block.gpsimd(make("pool"))
        if per_eng["sp"]:
            block.sync(make("sp"))


class SB:
    LO = 16512
    HI = 229376

    CNT = [0]

    def __init__(self, nc, lo=None, hi=None):
        self.nc = nc
        self.lo = SB.LO if lo is None else lo
        self.hi = SB.HI if hi is None else hi
        self.top = self.lo
        self.peak = 0

    def alloc(self, name, shape, dtype):
        esz = 2 if dtype == BF16 else 4
        nb = esz
        for s in shape[1:]:
            nb *= s
        nb = (nb + 63) // 64 * 64
        off = self.top
        self.top += nb
        self.peak = max(self.peak, self.top)
        assert self.top <= self.hi, f"SBUF overflow allocating {name}: {self.top} > {self.hi}"
        SB.CNT[0] += 1
        return self.nc.alloc_sbuf_tensor_at(f"{name}_{SB.CNT[0]}", list(shape), dtype, offset=off)

    def mark(self):
        return self.top

    def release(self, m):
        self.top = m


def build_program():
    nc = bass.Bass("TRN2", target_bir_lowering=False)

    def din(name, shape):
        return nc.dram_tensor(name, list(shape), F32, kind="ExternalInput").ap()

    def dout(name, shape):
        return nc.dram_tensor(name, list(shape), F32, kind="ExternalOutput").ap()

    xT_d = din("xT", [128, 8, T])
    cT_d = din("cT", [128, 8, 3])
    small_d = din("smallp", [2, 128, NSMALL])
    headp_d = din("headp", [2, 8, 4])
    wmod_d = din("w_mod", [2, D, 9 * D])
    wg_d = din("ffn_w_gate", [2, 2, D, DFF])
    wu_d = din("ffn_w_up", [2, 2, D, DFF])
    wd_d = din("ffn_w_down", [2, 2, DFF, D])
    win_d = din("w_in", [2, D, DIN])
    wout_d = din("w_out", [2, D, D])
    lwa_d = din("lru_wa", [2, 4, 64, 64])
    lwx_d = din("lru_wx", [2, 4, 64, 64])
    ckT_d = din("ckT", [2, 2, 8, 64, PAST])
    cv_d = din("cvh", [2, 2, 8, 128, 2048])
    clfT_d = din("clfT", [2, 2, 8, PAST])
    lconv_d = din("lconvT", [2, 128, 2, 2, 3])
    lh_d = din("lhT", [2, 128, 2, 2])
    sconv_d = din("sconvT", [2, 128, 2, 6, 3])
    sh0_d = din("sh0T", [2, 2, 4, 128, 64])
    sh0f_d = din("sh0f", [2, 2, 4, 64, 128])
    yT_o = dout("yT", [128, 8, T])
    kT_o = dout("fk_out", [2, 128, 4, T])
    v_o = dout("fv_out", [2, T, 512])
    lf_o = dout("logfT", [2, 8, T])
    lconv_o = dout("lconv_o", [2, 128, 3, 2, 3])
    lh_o = dout("lh_o", [2, 128, 3, 2])
    sconv_o = dout("sconv_o", [2, 128, 3, 6, 3])
    sh_o = dout("sh_o", [2, 3, 4, 64, 128])
    xspill = nc.dram_tensor("xspill", [128, 8, T], F32).ap()
    dbg_o = dout("dbg", [128, 8, T]) if DBG else None

    st = ExitStack()
    P = Prog(nc)
    sb = SB(nc)

    PS = [nc.alloc_psum_tensor(f"psb{i}", [128, 512], F32) for i in range(8)]
    RPS = [Res(f"ps{i}") for i in range(8)]

    X = sb.alloc("X", [128, 8, T], F32)
    RX = [Res(f"X{t}") for t in range(5)]
    ones_bf = sb.alloc("ones_bf", [128, 128], BF16)
    ident_bf = sb.alloc("ident_bf", [128, 128], BF16)
    ident_f = sb.alloc("ident_f", [128, 128], F32)
    maskneg = sb.alloc("maskneg", [128, 128], BF16)
    eps_t = sb.alloc("eps_t", [128, 1], F32)
    one_t = sb.alloc("one_t", [128, 1], F32)
    csil = sb.alloc("csil", [128, 8, 3], BF16)
    cin = sb.alloc("cin", [128, 8, 3], F32)
    CUR = [0]

    class Sel:
        def __init__(self, items):
            self.items = items

        def __getitem__(self, k):
            return self.items[CUR[0]][k]

    class ResSel:
        def __init__(self, items):
            object.__setattr__(self, "items", items)

        def __getattr__(self, n):
            return getattr(self.items[CUR[0]], n)

        def __setattr__(self, n, v):
            setattr(self.items[CUR[0]], n, v)

    small = Sel([sb.alloc("small", [128, NSMALL], F32) for _ in range(2)])
    headp = Sel([sb.alloc("headp", [8, 4], F32) for _ in range(2)])
    modsb = Sel([sb.alloc("modsb", [128, 72, 3], F32) for _ in range(2)])
    gs_t = Sel([sb.alloc("gs_t", [128, 3, 8, 3], F32) for _ in range(2)])
    gp_t = Sel([sb.alloc("gp_t", [128, 3, 8, 3], F32) for _ in range(2)])
    Rconst = Res("const")
    Rsmall = ResSel([Res("small0"), Res("small1")])
    Rmod_lj = [[Res(f"mod{l}_{j}") for j in range(3)] for l in range(2)]

    def RM(j):
        return Rmod_lj[CUR[0]][j]
    Rcs = Res("csil")
    dum = ones_bf
    ARENA = sb.mark()

    C_NPRE, C_NPOST, C_BMOD = 0, 24, 48
    C_LCW, C_LCB, C_LBA, C_LBX, C_LLAM = 120, 128, 130, 132, 134
    C_SCW, C_SCB, C_SD, C_SNW = 136, 160, 166, 168

    def dma(eng, out, in_, reads=(), writes=(), key=None, slow=False):
        if slow:
            return P.op(eng, lambda e: e.dma_start(out=out, in_=in_, allow_slow_non_contiguous=True),
                        reads=reads, writes=writes, dma=True, key=key)
        return P.op(eng, lambda e: e.dma_start(out=out, in_=in_), reads=reads, writes=writes, dma=True, key=key)

    def mm_group(out, pairs, reads, writes):
        n = len(pairs)

        def fn(e):
            ins = None
            for i, (l, r) in enumerate(pairs):
                ins = e.matmul(out, lhsT=l, rhs=r, start=(i == 0), stop=(i == n - 1))
            return ins
        return P.op("pe", fn, reads=reads, writes=writes)

    def act(out, in_, func, reads, writes, bias=None, scale=None):
        kw = {}
        if bias is not None:
            kw["bias"] = bias
        if scale is not None:
            kw["scale"] = scale
        return P.op("act", lambda e: e.activation(out=out, in_=in_, func=func, **kw), reads=reads, writes=writes)

    def dve(fn, reads, writes):
        return P.op("dve", fn, reads=reads, writes=writes)

    def tt(out, in0, in1, op, reads, writes, eng="dve"):
        return P.op(eng, lambda e: e.tensor_tensor(out=out, in0=in0, in1=in1, op=op), reads=reads, writes=writes)

    def stt(out, in0, scalar, in1, op0, op1, reads, writes):
        return P.op("dve", lambda e: e.scalar_tensor_tensor(out=out, in0=in0, scalar=scalar, in1=in1, op0=op0, op1=op1),
                    reads=reads, writes=writes)

    def tsc(out, in0, s1, s2, op0, op1, reads, writes, eng="dve"):
        if s2 is None:
            return P.op(eng, lambda e: e.tensor_scalar(out=out, in0=in0, scalar1=s1, scalar2=None, op0=op0),
                        reads=reads, writes=writes)
        return P.op(eng, lambda e: e.tensor_scalar(out=out, in0=in0, scalar1=s1, scalar2=s2, op0=op0, op1=op1),
                    reads=reads, writes=writes)

    def recip(out, in_, reads, writes):
        return P.op("dve", lambda e: e.reciprocal(out=out, in_=in_), reads=reads, writes=writes)

    def cp(out, in_, reads, writes, eng="dve"):
        return P.op(eng, lambda e: e.tensor_copy(out=out, in_=in_), reads=reads, writes=writes)

    def scan(out, d0, d1, init, reads, writes):
        return P.op("dve", lambda e: e.tensor_tensor_scan(out=out, data0=d0, data1=d1, initial=init,
                                                          op0=ALU.mult, op1=ALU.add), reads=reads, writes=writes)

    def memset(ap, val, writes, eng="dve"):
        return P.op(eng, lambda e: e.memset(ap, val), writes=writes)

    P.op("dve", lambda e: e.memset(ones_bf[:], 1.0), writes=[Rconst])
    P.op("dve", lambda e: e.memset(eps_t[:], EPS), writes=[Rconst])
    P.op("dve", lambda e: e.memset(one_t[:], 1.0), writes=[Rconst])
    P.op("dve", lambda e: e.memset(ident_f[:], 1.0), writes=[Rconst])
    P.op("pool", lambda e: e.affine_select(out=ident_f[:], in_=ident_f[:], pattern=[[-1, 128]],
                                           compare_op=ALU.is_equal, fill=0.0, base=0, channel_multiplier=1),
         reads=[Rconst], writes=[Rconst])
    P.op("dve", lambda e: e.tensor_copy(out=ident_bf[:], in_=ident_f[:]), reads=[Rconst], writes=[Rconst])
    P.op("dve", lambda e: e.memset(maskneg[:], 0.0), writes=[Rconst])
    P.op("pool", lambda e: e.affine_select(out=maskneg[:], in_=maskneg[:], pattern=[[1, 128]],
                                           compare_op=ALU.is_ge, fill=-30000.0, base=0, channel_multiplier=-1),
         reads=[Rconst], writes=[Rconst])

    NWARM = int(os.environ.get("MK_WARM", "0"))

    def keep_warm(bank, n):
        if n <= 0:
            return

        def fn(e):
            ins = None
            for _ in range(n):
                ins = e.matmul(PS[bank][:, 0:128], lhsT=ident_bf[:], rhs=dum[:], start=True, stop=True)
            return ins
        P.op("pe", fn, reads=[Rconst], writes=[RPS[bank]])

    for t, (t0, n) in enumerate(TILES):
        dma("sp", X[:, :, t0:t0 + n], xT_d[:, :, t0:t0 + n], writes=[RX[t]], key=RX[t])
    dma("sp", cin[:], cT_d, writes=[Rcs], key=Rcs)
    for l_ in range(2):
        dma("sp", small.items[l_][:], small_d[l_], writes=[Rsmall.items[l_]], key=Rsmall.items[l_])
        dma("sp", headp.items[l_][:], headp_d[l_], writes=[Rsmall.items[l_]], key=Rsmall.items[l_])
    act(csil[:], cin[:], AF.Silu, reads=[Rcs], writes=[Rcs])

    def mod_thunks(l, parts, wm, Rwm):
        th = []
        wv = wmod_d[l].rearrange("(kc p) n -> p kc n", p=128)
        psm = PS[7]
        sm, msb, gst, gpt = small.items[l], modsb.items[l], gs_t.items[l], gp_t.items[l]
        Rsm = Rsmall.items[l]
        for j in parts:
            for s_ in range(6 * j, 6 * j + 6):
                def slab(s_=s_):
                    b = s_ % 2
                    dma("pool", wm[b][:], wv[:, :, s_ * 512:(s_ + 1) * 512], writes=[Rwm[b]], key=Rwm[b])
                    for m in range(4):
                        mc = s_ * 4 + m
                        mm_group(psm[:, mc * 3:mc * 3 + 3],
                                 [(wm[b][:, kc, m * 128:(m + 1) * 128], csil[:, kc, :]) for kc in range(8)],
                                 reads=[Rwm[b], Rcs], writes=[RPS[7]])
                th.append(slab)

            def fin(j=j):
                Rm = Rmod_lj[l][j]
                psv = psm[:, 72 * j:72 * j + 72].rearrange("p (m s) -> p m s", s=3)
                w_j = 1.0 if j == 1 else 0.5
                for s_ in range(3):
                    tt(msb[:, 24 * j:24 * j + 24, s_], psv[:, :, s_], sm[:, C_BMOD + 24 * j:C_BMOD + 24 * j + 24], ALU.add,
                       reads=[RPS[7], Rsm], writes=[Rm])
                for s_ in range(3):
                    stt(gst[:, j, :, s_], msb[:, j * 24 + 8:j * 24 + 16, s_], 1.0,
                        sm[:, C_NPRE + j * 8:C_NPRE + j * 8 + 8], ALU.add, ALU.mult, reads=[Rm, Rsm], writes=[Rm])
                    stt(gpt[:, j, :, s_], msb[:, j * 24 + 16:j * 24 + 24, s_], w_j,
                        sm[:, C_NPOST + j * 8:C_NPOST + j * 8 + 8], ALU.mult, ALU.mult, reads=[Rm, Rsm], writes=[Rm])
            th.append(fin)
        return th

    def rms_stats(src_fn, ti, n, sq, Rsq, rs, Rrs, srcres):
        act(sq[:, :, :n], src_fn(), AF.Square, reads=srcres, writes=[Rsq])
        mm_group(PS[6][:, :n], [(ones_bf[:], sq[:, c, :n]) for c in range(8)], reads=[Rsq, Rconst], writes=[RPS[6]])
        act(rs[:, :n], PS[6][:, :n], AF.Sqrt, reads=[RPS[6], Rconst], writes=[Rrs], bias=eps_t[:], scale=1.0 / D)
        recip(rs[:, :n], rs[:, :n], reads=[Rrs], writes=[Rrs])

    def segs_in(t0, n):
        out = []
        for (s0, sl, s) in SEGS:
            a, b = max(s0, t0), min(s0 + sl, t0 + n)
            if a < b:
                out.append((a, b - a, s))
        return out

    def norm_pre_steps(j, tis, xn, Rxn, xoff, scr):
        steps = []
        for k_, ti in enumerate(tis):
            sq, Rsq, rs, Rrs, tmp, Rtmp = scr[k_ % len(scr)] if isinstance(scr, list) else scr
            t0, n = TILES[ti]
            steps.append(lambda ti=ti, t0=t0, n=n, sq=sq, Rsq=Rsq, rs=rs, Rrs=Rrs: rms_stats(
                lambda: X[:, :, t0:t0 + n], ti, n, sq, Rsq, rs, Rrs, [RX[ti]]))
            for c in range(8):
                def st(ti=ti, t0=t0, n=n, c=c, rs=rs, Rrs=Rrs, tmp=tmp, Rtmp=Rtmp):
                    b = c % 2
                    tt(tmp[b][:, :n], X[:, c, t0:t0 + n], rs[:, :n], ALU.mult, reads=[RX[ti], Rrs], writes=[Rtmp[b]])
                    for (a, ln, s) in segs_in(t0, n):
                        act(xn[:, c, a - xoff:a - xoff + ln], tmp[b][:, a - t0:a - t0 + ln], AF.Identity,
                            reads=[Rtmp[b], RM(j)], writes=[Rxn[ti]],
                            bias=modsb[:, j * 24 + c, s:s + 1], scale=gs_t[:, j, c, s:s + 1])
                steps.append(st)
        return steps

    def norm_pre(j, tis, xn, Rxn, xoff, scr):
        for st in norm_pre_steps(j, tis, xn, Rxn, xoff, scr):
            st()

    def post_norm_steps(j, tis, yacc, Ry, yoff, scr):
        sq, Rsq, rs, Rrs, tmp, Rtmp = scr
        steps = []
        for ti in tis:
            t0, n = TILES[ti]
            steps.append(lambda ti=ti, t0=t0, n=n: rms_stats(lambda: yacc[:, :, t0 - yoff:t0 - yoff + n], ti, n, sq, Rsq, rs, Rrs,
                                                             [Ry[ti]]))
            for c in range(8):
                def st(ti=ti, t0=t0, n=n, c=c):
                    b = c % 2
                    tt(tmp[b][:, :n], yacc[:, c, t0 - yoff:t0 - yoff + n], rs[:, :n], ALU.mult,
                       reads=[Ry[ti], Rrs], writes=[Rtmp[b]])
                    for (a, ln, s) in segs_in(t0, n):
                        stt(X[:, c, a:a + ln], tmp[b][:, a - t0:a - t0 + ln], gp_t[:, j, c, s:s + 1],
                            X[:, c, a:a + ln], ALU.mult, ALU.add,
                            reads=[Rtmp[b], RM(j), RX[ti]], writes=[RX[ti]])
                steps.append(st)
        return steps

    def post_norm(j, tis, yacc, Ry, yoff, scr):
        for st in post_norm_steps(j, tis, yacc, Ry, yoff, scr):
            st()

    def skewed(steps, per=9):
        groups = [steps[i:i + per] for i in range(0, len(steps), per)]
        out = []
        for g, grp in enumerate(groups):
            if g == 0:
                out.append(grp[0])
            if g + 1 < len(groups):
                out.append(groups[g + 1][0])
            out.extend(grp[1:])
        return out

    def interleave(main, side):
        nm, ns = len(main), len(side)
        k = 0
        for i, m in enumerate(main):
            m()
            tgt = (i + 1) * ns // max(nm, 1)
            while k < tgt:
                side[k]()
                k += 1
        while k < ns:
            side[k]()
            k += 1

    def alloc_norm_scratch():
        sq = sb.alloc("sq", [128, 8, 512], BF16)
        rs = sb.alloc("rs", [128, 512], F32)
        tmp = [sb.alloc("ntmp", [128, 512], F32) for _ in range(2)]
        return (sq, Res("sq"), rs, Res("rs"), tmp, [Res("ntmp0"), Res("ntmp1")])

    def ffn(l, wi, j):
        m0 = sb.mark()
        halves = [[0, 1], [2, 3, 4]]
        NT = 1088
        scr = alloc_norm_scratch()
        xn = sb.alloc("xn", [128, 8, NT], BF16)
        yacc = sb.alloc("yacc", [128, 8, NT], F32)
        hb = [sb.alloc("hb", [128, 4, NT], BF16) for _ in range(2)]
        wg = [sb.alloc("wg", [128, 8, 512], BF16) for _ in range(2)]
        wu = [sb.alloc("wu", [128, 8, 512], BF16) for _ in range(2)]
        wd = [sb.alloc("wd", [128, 4, 1024], BF16) for _ in range(2)]
        sg = [sb.alloc("sg", [128, 512], BF16) for _ in range(2)]
        Rxs_ = [Res(f"xn_s{p}") for p in range(3)]
        Rys_ = [Res(f"y_s{p}") for p in range(3)]
        Rhs_ = [[Res(f"h{b}_s{p}") for p in range(3)] for b in range(2)]
        Rwgu = [Res("wgu0"), Res("wgu1")]
        Rwd = [Res("wd0"), Res("wd1")]
        Rsg = [Res("sg0"), Res("sg1")]
        wgv = wg_d[l, wi].rearrange("(kc p) n -> p kc n", p=128)
        wuv = wu_d[l, wi].rearrange("(kc p) n -> p kc n", p=128)
        wdv = wd_d[l, wi].rearrange("(jc p) n -> p jc n", p=128)
        mch = [4, 4, 4, 4, 4, 2]
        t_lo = [TILES[h[0]][0] for h in halves]
        Rxn = [{ti: Rxs_[p] for p, ti in enumerate(h)} for h in halves]
        Ry = [{ti: Rys_[p] for p, ti in enumerate(h)} for h in halves]

        def load_gu(v):
            s_, b = v % 6, v % 2
            w = mch[s_] * 128
            dma("pool", wg[b][:, :, 0:w], wgv[:, :, s_ * 512:s_ * 512 + w], writes=[Rwgu[b]], key=Rwgu[b])
            dma("pool", wu[b][:, :, 0:w], wuv[:, :, s_ * 512:s_ * 512 + w], writes=[Rwgu[b]], key=Rwgu[b])

        def load_d(v):
            s_, b = v % 6, v % 2
            dma("pool", wd[b][:, 0:mch[s_], :], wdv[:, s_ * 4:s_ * 4 + mch[s_], :], writes=[Rwd[b]], key=Rwd[b])

        cnt = [0]
        dcnt = [0]

        def gu_units(v):
            hf, s_, b = v // 6, v % 6, v % 2
            units = []
            for p, ti in enumerate(halves[hf]):
                t0, n = TILES[ti]
                o = t0 - t_lo[hf]
                for m in range(mch[s_]):
                    def unit(p=p, n=n, o=o, m=m, b=b):
                        g = cnt[0] % 2
                        cnt[0] += 1
                        pg, pu = PS[g], PS[2 + g]
                        mm_group(pg[:, :n], [(wg[b][:, kc, m * 128:(m + 1) * 128], xn[:, kc, o:o + n]) for kc in range(8)],
                                 reads=[Rwgu[b], Rxs_[p]], writes=[RPS[g]])
                        mm_group(pu[:, :n], [(wu[b][:, kc, m * 128:(m + 1) * 128], xn[:, kc, o:o + n]) for kc in range(8)],
                                 reads=[Rwgu[b], Rxs_[p]], writes=[RPS[2 + g]])
                        act(sg[g][:, :n], pg[:, :n], AF.Silu, reads=[RPS[g]], writes=[Rsg[g]])
                        tt(hb[b][:, m, o:o + n], sg[g][:, :n], pu[:, :n], ALU.mult,
                           reads=[Rsg[g], RPS[2 + g]], writes=[Rhs_[b][p]])
                    units.append(unit)
            return units

        def down_units(v):
            hf, s_, b = v // 6, v % 6, v % 2
            units = []
            for p, ti in enumerate(halves[hf]):
                t0, n = TILES[ti]
                o = t0 - t_lo[hf]
                for oc in range(8):
                    def unit(p=p, n=n, o=o, oc=oc, b=b, s_=s_):
                        g = dcnt[0] % 2
                        dcnt[0] += 1
                        pd = PS[4 + g]
                        mm_group(pd[:, :n], [(wd[b][:, m, oc * 128:(oc + 1) * 128], hb[b][:, m, o:o + n]) for m in range(mch[s_])],
                                 reads=[Rwd[b], Rhs_[b][p]], writes=[RPS[4 + g]])
                        if s_ == 0:
                            act(yacc[:, oc, o:o + n], pd[:, :n], AF.Identity, reads=[RPS[4 + g]], writes=[Rys_[p]])
                        else:
                            tt(yacc[:, oc, o:o + n], yacc[:, oc, o:o + n], pd[:, :n], ALU.add,
                               reads=[RPS[4 + g], Rys_[p]], writes=[Rys_[p]])
                    units.append(unit)
            return units

        def run(lst):
            for u in lst:
                u()

        load_gu(0)
        load_d(0)
        load_gu(1)
        load_d(1)
        norm_pre(j, halves[0], xn, Rxn[0], t_lo[0], scr)
        NV = 12
        for v in range(NV):
            if v == 6:
                interleave(gu_units(v), post_norm_steps(j, halves[0], yacc, Ry[0], t_lo[0], scr))
            else:
                run(gu_units(v))
            if v + 2 < NV:
                load_gu(v + 2)
            if v == 5:
                pre_b = norm_pre_steps(j, halves[1], xn, Rxn[1], t_lo[1], scr)
                h1 = len(pre_b) // 2
                interleave(down_units(4), pre_b[:h1])
                load_d(6)
                interleave(down_units(5), pre_b[h1:])
                load_d(7)
            elif v >= 1 and v != 6:
                run(down_units(v - 1))
                if v + 1 < NV:
                    load_d(v + 1)
        run(down_units(NV - 1))
        post_norm(j, halves[1], yacc, Ry[1], t_lo[1], scr)
        P.barrier()
        sb.release(m0)

    def fox(l, ymix, Rym, xn, Rxn, winv):
        ms, mx = sb.mark(), sbx.mark()
        NB = 18
        blocks = [(tb * 128, 128) for tb in range(16)] + [(2048, 32), (2080, 32)]
        vaug = sb.alloc("vaug", [128, NB, 8, 66], BF16)
        Rva = Res("vaug")
        memset(vaug[:, :, :, 64:65], 1.0, writes=[Rva])
        ones_r = sb.alloc("ones_r", [128, 64], F32)
        memset(ones_r[:], 1.0, writes=[Rva])
        qs = sb.alloc("qs", [70, 8, 64], BF16)
        ks = sb.alloc("ks", [70, 8, 64], BF16)
        Rqs = Res("qs")
        fend = sb.alloc("fend", [8, 2], F32)
        nfb = sb.alloc("nfb", [8, 1], F32)
        Rfend = Res("fend")
        m_f = sb.mark()
        LG = sb.alloc("LG", [8, T], F32)
        FT = sb.alloc("FT", [128, T], F32)
        SP1 = sb.alloc("SP1", [128, T], BF16)
        SP2 = sb.alloc("SP2", [128, T], BF16)
        CLF = sb.alloc("CLF", [8, 2050], F32)
        RLG, RFT, RFR, RSP1, RSP2 = Res("LG"), Res("FT"), Res("FR"), Res("SP1"), Res("SP2")
        RCLF = Res("CLF")
        qa = [sbx.alloc("qa", [70, T], BF16) for _ in range(4)]
        ka = [sbx.alloc("ka", [70, T], BF16) for _ in range(4)]
        Rqa = [Res(f"qa{i}") for i in range(4)]
        Rka = [Res(f"ka{i}") for i in range(4)]
        wq = sbx.alloc("wq", [128, 8, 256], BF16)
        wk = sbx.alloc("wk", [128, 8, 256], BF16)
        wv = sbx.alloc("wv", [128, 8, 512], BF16)
        wf = sbx.alloc("wf", [128, 8, 8], BF16)
        Rwq, Rwk, Rwv, Rwf = Res("wq"), Res("wk"), Res("wv"), Res("wf")
        kst = [sbx.alloc("kst", [128, 512], F32) for _ in range(2)]
        vst = [sbx.alloc("vst", [128, 512], F32) for _ in range(2)]
        Rkst = [Res("kst0"), Res("kst1")]
        Rvst = [Res("vst0"), Res("vst1")]
        pbuf = [sbx.alloc("pbuf", [128, 512], BF16) for _ in range(4)]
        Rpb = [Res(f"pb{i}") for i in range(4)]
        rcb = sbx.alloc("rcb", [128, 512], F32)
        bcs = sbx.alloc("bcs", [64, 512], F32)
        Rrcb, Rbcs = Res("rcb"), Res("bcs")

        if FOXS <= -2:
            P.barrier()
            sb.release(ms)
            sbx.release(mx)
            return
        dma("pool", wf[:], winv[:, :, 2048:2056], writes=[Rwf], key=Rwf)
        tsc(nfb[:], headp[0:8, 0:1], -1.0, None, ALU.mult, None, reads=[Rsmall], writes=[Rfend])
        for ti, (t0, n) in enumerate(TILES):
            mm_group(PS[3][0:8, :n], [(wf[:, kc, :], xn[:, kc, t0:t0 + n]) for kc in range(8)],
                     reads=[Rwf, Rxn[ti]], writes=[RPS[3]])
            act(LG[0:8, t0:t0 + n], PS[3][0:8, :n], AF.Exp, reads=[RPS[3], Rfend], writes=[RLG], bias=nfb[:], scale=-1.0)
        act(LG[:], LG[:], AF.Ln, reads=[RLG, Rconst], writes=[RLG], bias=one_t[0:8])
        tsc(LG[:], LG[:], -1.0, None, ALU.mult, None, reads=[RLG], writes=[RLG])
        dma("sp", lf_o[l], LG[:], reads=[RLG], key=RLG)
        ones8 = lambda n: one_t[0:8, 0:1].to_broadcast([8, n])
        scan(FT[0:8, 0:NPR], ones8(NPR), LG[0:8, 0:NPR], 0.0, reads=[RLG, Rconst], writes=[RFT])
        for si in range(2):
            for hf in range(2):
                dma("sp", CLF[:, 0:2048], clfT_d[l, si, :, hf * 2048:(hf + 1) * 2048], writes=[RCLF], key=RCLF)
                P.op("dve", (lambda hf=hf: (lambda e: e.tensor_reduce(
                    out=CLF[:, 2048 + hf:2049 + hf], in_=CLF[:, 0:2048], axis=mybir.AxisListType.X, op=ALU.add)))(),
                    reads=[RCLF], writes=[RCLF])
            tt(fend[:, si:si + 1], CLF[:, 2048:2049], CLF[:, 2049:2050], ALU.add, reads=[RCLF], writes=[Rfend])
            scan(FT[0:8, NPR + si * 32:NPR + si * 32 + 32], ones8(32), LG[0:8, NPR + si * 32:NPR + si * 32 + 32],
                 fend[:, si:si + 1], reads=[RLG, Rconst, Rfend], writes=[RFT])
        split3(FT[0:8, :], FT[32:40, :], FT[64:72, :], SP1[0:8, :], SP1[32:40, :], SP1[64:72, :], RFT, RFR, RSP1)
        for q_ in (0, 32, 64):
            tsc(SP2[q_:q_ + 8, :], SP1[q_:q_ + 8, :], -1.0, None, ALU.mult, None, reads=[RSP1], writes=[RSP2])

        if FOXS <= -1:
            P.barrier()
            sb.release(ms)
            sbx.release(mx)
            return
        scnt = [0]
        ocnt = [0]
        deferred = []
        for hg in range(1 if "f" in TOG else 2):
            dma("pool", wq[:], winv[:, :, 512 + hg * 256:512 + hg * 256 + 256], writes=[Rwq], key=Rwq)
            dma("pool", wk[:], winv[:, :, 1024 + hg * 256:1024 + hg * 256 + 256], writes=[Rwk], key=Rwk)
            if hg == 0 and "e" not in TOG:
                dma("pool", wv[:], winv[:, :, 1536:2048], writes=[Rwv], key=Rwv)
            for hl in range(0 if "a" in TOG else 4):
                memset(qa[hl][64:70, :], 1.0, writes=[Rqa[hl]])
                memset(ka[hl][64:70, :], 1.0, writes=[Rka[hl]])
            for m in range(0 if "d" in TOG else 2):
                for ti, (t0, n) in enumerate(TILES):
                    psq, psk = PS[3], PS[7]
                    b = ti % 2
                    mm_group(psq[:, :n], [(wq[:, kc, m * 128:(m + 1) * 128], xn[:, kc, t0:t0 + n]) for kc in range(8)],
                             reads=[Rwq, Rxn[ti]], writes=[RPS[3]])
                    act(qa[2 * m][0:64, t0:t0 + n], psq[0:64, :n], AF.Identity, reads=[RPS[3]], writes=[Rqa[2 * m]], scale=0.125)
                    tsc(qa[2 * m + 1][0:64, t0:t0 + n], psq[64:128, :n], 0.125, None, ALU.mult, None,
                        reads=[RPS[3]], writes=[Rqa[2 * m + 1]])
                    mm_group(psk[:, :n], [(wk[:, kc, m * 128:(m + 1) * 128], xn[:, kc, t0:t0 + n]) for kc in range(8)],
                             reads=[Rwk, Rxn[ti]], writes=[RPS[7]])
                    act(ka[2 * m][0:64, t0:t0 + n], psk[0:64, :n], AF.Identity, reads=[RPS[7]], writes=[Rka[2 * m]])
                    cp(ka[2 * m + 1][0:64, t0:t0 + n], psk[64:128, :n], reads=[RPS[7]], writes=[Rka[2 * m + 1]])
                    if "c" not in TOG:
                        cp(kst[b][:, :n], psk[:, :n], reads=[RPS[7]], writes=[Rkst[b]])
                        if "g" in TOG:
                            dma("sp", dbg_o[:, hg * 2 + m, t0:t0 + n], kst[b][:, :n], reads=[Rkst[b]], key=Rkst[b])
                        else:
                            dma("sp", kT_o[l, :, hg * 2 + m, t0:t0 + n], kst[b][:, :n], reads=[Rkst[b]], key=Rkst[b])
            if hg == 0 and FOXS >= 0 and "b" not in TOG:
                for tb, (k0, nb) in enumerate(blocks):
                    b = tb % 2
                    psv = PS[6]
                    mm_group(psv[0:nb, :], [(xn[:, kc, k0:k0 + nb], wv[:, kc, :]) for kc in range(8)],
                             reads=[Rwv] + [Rxn[i] for i in range(5)], writes=[RPS[6]])
                    if "i" not in TOG:
                        cp(vaug[0:nb, tb, :, 0:64], psv[0:nb, :].rearrange("p (h d) -> p h d", d=64),
                           reads=[RPS[6]], writes=[Rva])
                    if "h" not in TOG:
                        cp(vst[b][0:nb, :], psv[0:nb, :], reads=[RPS[6]], writes=[Rvst[b]])
                        dma("sp", v_o[l, k0:k0 + nb, :], vst[b][0:nb, :], reads=[Rvst[b]], key=Rvst[b])
            for hl in range(4 if FOXS >= 1 else 0):
                h = hg * 4 + hl
                dma("sp", qa[hl][64:67, :], SP1[h:h + 65:32, :], reads=[RSP1], writes=[Rqa[hl]], key=Rqa[hl])
                dma("sp", ka[hl][67:70, :], SP2[h:h + 65:32, :], reads=[RSP2], writes=[Rka[hl]], key=Rka[hl])
            for hl in range(4 if FOXS >= 2 else 0):
                h = hg * 4 + hl
                for Q in range(4):
                    og = 4 + (ocnt[0] % 2)
                    ocnt[0] += 1
                    po = PS[og]
                    nkb = 4 * Q + 4
                    LOOK = 2
                    pend = []
                    for kb in range(nkb + LOOK):
                        if kb < nkb:
                            d = kb - 4 * Q
                            col0 = max(d, 0) * 128
                            g = scnt[0] % 4
                            scnt[0] += 1
                            pS = PS[g]

                            def fnS(e, pS=pS, col0=col0, d=d, hl=hl, kb=kb, Q=Q):
                                ins = e.matmul(pS[:, col0:512], lhsT=ka[hl][0:70, kb * 128:(kb + 1) * 128],
                                               rhs=qa[hl][0:70, Q * 512 + col0:Q * 512 + 512], start=True, stop=(d < 0))
                                if d >= 0:
                                    ins = e.matmul(pS[:, col0:col0 + 128], lhsT=ident_bf[:], rhs=maskneg[:], start=False, stop=True)
                                return ins
                            P.op("pe", fnS, reads=[Rka[hl], Rqa[hl], Rconst], writes=[RPS[g]])
                            act(pbuf[g][:, col0:512], pS[:, col0:512], AF.Exp, reads=[RPS[g]], writes=[Rpb[g]])
                            pend.append((kb, g, col0))
                            keep_warm(7, NWARM)
                            if kb == 1:
                                while deferred:
                                    deferred.pop(0)()
                        if kb >= LOOK:
                            kb_, g_, c0_ = pend.pop(0)

                            def fnV(e, kb_=kb_, g_=g_, c0_=c0_, po=po, h=h, nkb=nkb):
                                return e.matmul(po[0:65, c0_:512], lhsT=vaug[:, kb_, h, 0:65], rhs=pbuf[g_][:, c0_:512],
                                                start=(kb_ == 0), stop=(kb_ == nkb - 1))
                            P.op("pe", fnV, reads=[Rva, Rpb[g_]], writes=[RPS[og]])
                    recip(rcb[0:1, :], po[64:65, :], reads=[RPS[og]], writes=[Rrcb])

                    def epilogue(po=po, og=og, h=h, Q=Q):
                        mm_group(PS[6][0:64, :], [(ones_r[0:1, 0:64], rcb[0:1, :])], reads=[Rrcb, Rva], writes=[RPS[6]])
                        act(bcs[:, :], PS[6][0:64, :], AF.Identity, reads=[RPS[6]], writes=[Rbcs])
                        tt(ymix[(h % 2) * 64:(h % 2) * 64 + 64, 2 + h // 2, Q * 512:Q * 512 + 512], po[0:64, :], bcs[:, :],
                           ALU.mult, reads=[RPS[og], Rbcs], writes=[Rym[2 + h // 2]])
                    deferred.append(epilogue)
            while deferred:
                deferred.pop(0)()
            for hl in range(4 if FOXS >= 1 else 0):
                h = hg * 4 + hl
                cp(qs[0:70, h, :], qa[hl][0:70, NPR:T], reads=[Rqa[hl]], writes=[Rqs])
                cp(ks[0:70, h, :], ka[hl][0:70, NPR:T], reads=[Rka[hl]], writes=[Rqs])
        P.barrier()
        sb.release(m_f)
        if FOXS < 3:
            sb.release(ms)
            sbx.release(mx)
            return
        FC = sb.alloc("FC", [128, PAST], F32)
        SPC = sb.alloc("SPC", [128, PAST], BF16)
        RFC, RFCR, RSPC = Res("FC"), Res("FCR"), Res("SPC")
        kcb = [sb.alloc("kcb", [70, PAST], BF16) for _ in range(2)]
        vcb = [sb.alloc("vcb", [128, 2048], BF16) for _ in range(2)]
        rsum = sb.alloc("rsum", [1, 64], F32)
        Rrsum = Res("rsum")
        Rkc = [Res("kc0"), Res("kc1")]
        Rkcr = [Res("kcr0"), Res("kcr1")]
        Rvc = [Res("vc0"), Res("vc1")]
        for b in range(2):
            memset(kcb[b][64:70, :], 1.0, writes=[Rkc[b]])
        for si in range(2):
            dma("sp", FC[64:72, :], clfT_d[l, si], writes=[RFCR], key=RFCR)
            scan(FC[0:8, :], one_t[64:72, 0:1].to_broadcast([8, PAST]), FC[64:72, :], 0.0, reads=[RFCR, Rconst], writes=[RFC])
            split3(FC[0:8, :], FC[32:40, :], FC[64:72, :], SPC[0:8, :], SPC[32:40, :], SPC[64:72, :], RFC, RFCR, RSPC)
            for q_ in (0, 32, 64):
                tsc(SPC[q_:q_ + 8, :], SPC[q_:q_ + 8, :], -1.0, None, ALU.mult, None, reads=[RSPC], writes=[RSPC])
            for h in range(8):
                b = h % 2
                dma("pool", kcb[b][0:64, :], ckT_d[l, si, h], writes=[Rkc[b]], key=Rkc[b])
                dma("sp", kcb[b][67:70, :], SPC[h:h + 65:32, :], reads=[RSPC], writes=[Rkc[b]], key=Rkcr[b])
                dma("pool", vcb[b][:, :], cv_d[l, si, h], writes=[Rvc[b]], key=Rvc[b])
                qcol = qs[0:70, h, si * 32:si * 32 + 32]
                for hf in range(2):
                    pS = PS[hf]

                    def fnC(e, pS=pS, hf=hf, b=b, qcol=qcol):
                        ins = None
                        for bb in range(16):
                            kb = hf * 16 + bb
                            ins = e.matmul(pS[:, bb * 32:bb * 32 + 32], lhsT=kcb[b][0:70, kb * 128:(kb + 1) * 128], rhs=qcol,
                                           start=True, stop=True)
                        return ins
                    P.op("pe", fnC, reads=[Rkc[b], Rqs], writes=[RPS[hf]])
                    act(pbuf[hf][:, :], pS[:, :], AF.Exp, reads=[RPS[hf]], writes=[Rpb[hf]])
                kcol = ks[0:70, h, si * 32:si * 32 + 32]

                def fnN(e, kcol=kcol, qcol=qcol):
                    e.matmul(PS[2][0:32, 0:32], lhsT=kcol, rhs=qcol, start=True, stop=False)
                    return e.matmul(PS[2][0:32, 0:32], lhsT=ident_bf[0:32, 0:32], rhs=maskneg[0:32, 0:32], start=False, stop=True)
                P.op("pe", fnN, reads=[Rqs, Rconst], writes=[RPS[2]])
                act(pbuf[2][0:32, 0:32], PS[2][0:32, 0:32], AF.Exp, reads=[RPS[2]], writes=[Rpb[2]])
                og = 4 + h % 2
                po = PS[og]
                vc4 = vcb[b]

                def fnPV(e, po=po, vc4=vc4, h=h, si=si):
                    for kb in range(32):
                        e.matmul(po[0:64, 0:32], lhsT=vc4[:, kb * 64:(kb + 1) * 64], rhs=pbuf[kb // 16][:, (kb % 16) * 32:(kb % 16) * 32 + 32],
                                 start=(kb == 0), stop=False)
                    return e.matmul(po[0:64, 0:32], lhsT=vaug[0:32, 16 + si, h, 0:64], rhs=pbuf[2][0:32, 0:32], start=False, stop=True)
                P.op("pe", fnPV, reads=[Rvc[b], Rva, Rpb[0], Rpb[1], Rpb[2]], writes=[RPS[og]])

                def fnSum(e):
                    e.matmul(PS[3][0:1, :], lhsT=ones_bf[:, 0:1], rhs=pbuf[0][:, :], start=True, stop=False)
                    e.matmul(PS[3][0:1, :], lhsT=ones_bf[:, 0:1], rhs=pbuf[1][:, :], start=False, stop=True)
                    return e.matmul(PS[7][0:1, 0:32], lhsT=ones_bf[0:32, 0:1], rhs=pbuf[2][0:32, 0:32], start=True, stop=True)
                P.op("pe", fnSum, reads=[Rconst, Rpb[0], Rpb[1], Rpb[2]], writes=[RPS[3], RPS[7]])
                P.op("dve", lambda e: e.tensor_reduce(out=rsum[0:1, 0:32], in_=PS[3][0:1, :].rearrange("p (b q) -> p q b", q=32),
                                                      axis=mybir.AxisListType.X, op=ALU.add), reads=[RPS[3]], writes=[Rrsum])
                tt(rsum[0:1, 32:64], rsum[0:1, 0:32], PS[7][0:1, 0:32], ALU.add, reads=[Rrsum, RPS[7]], writes=[Rrsum])
                recip(rcb[0:1, 0:32], rsum[0:1, 32:64], reads=[Rrsum], writes=[Rrcb])
                mm_group(PS[6][0:64, 0:32], [(ones_r[0:1, 0:64], rcb[0:1, 0:32])], reads=[Rrcb, Rva], writes=[RPS[6]])
                act(bcs[:, 0:32], PS[6][0:64, 0:32], AF.Identity, reads=[RPS[6]], writes=[Rbcs])
                tt(ymix[(h % 2) * 64:(h % 2) * 64 + 64, 2 + h // 2, NPR + si * 32:NPR + si * 32 + 32], po[0:64, 0:32], bcs[:, 0:32],
                   ALU.mult, reads=[RPS[og], Rbcs], writes=[Rym[2 + h // 2]])
        P.barrier()
        sb.release(ms)
        sbx.release(mx)

    def ssd(l, ymix, Rym, xn, Rxn, winv):
        ms, mx = sb.mark(), sbx.mark()
        NB = 18
        blocks = [(tb * 128, 128) for tb in range(16)] + [(2048, 32), (2080, 32)]
        seq_blocks = {0: list(range(16)), 1: [16], 2: [17]}
        xs = sb.alloc("xs", [128, 2, T], F32)
        BT = sb.alloc("BT", [128, 2, T], BF16)
        CT = sb.alloc("CT", [128, 2, T], BF16)
        DT = sb.alloc("DT", [128, T], F32)
        ET = sb.alloc("ET", [128, T], F32)
        S1 = sb.alloc("S1", [128, T], BF16)
        dctok = sb.alloc("dctok", [128, NB, 12], F32)
        at = sb.alloc("at", [4, 2], F32)
        ecd = sb.alloc("ecd", [4, 2, 4], F32)
        ecb = sb.alloc("ecb", [64, 2, 4], F32)
        Rxs = [Res("xs0"), Res("xs1")]
        RBT, RCT, RDT, RET, RS1, Rdc, Rat = [Res(n) for n in "BT CT DT ET S1 dctok at".split()]
        mconv = sbx.mark()
        wx1 = sbx.alloc("wx1", [128, 8, 512], BF16)
        wx2 = sbx.alloc("wx2", [128, 8, 260], BF16)
        Rwx1, Rwx2 = Res("wx1"), Res("wx2")
        dma("pool", wx1[:], winv[:, :, 2312:2824], writes=[Rwx1], key=Rwx1)
        dma("pool", wx2[:], winv[:, :, 2824:3084], writes=[Rwx2], key=Rwx2)
        xp6 = [sbx.alloc("xp6", [128, XPW], F32) for _ in range(6)]
        mu6 = sb.mark()
        u6 = [sb.alloc("u6", [128, T], F32)] * 2
        Rxp6 = [Res(f"xp6_{i}") for i in range(6)]
        Ru6 = [Res("u6a")] * 2
        for ci in range(6):
            xp = xp6[ci]
            memset(xp[:, 0:3], 0.0, writes=[Rxp6[ci]])
            dma("sp", xp[:, 2051:2054], sconv_d[l, :, 0, ci, :], writes=[Rxp6[ci]], key=Rxp6[ci])
            dma("sp", xp[:, 2086:2089], sconv_d[l, :, 1, ci, :], writes=[Rxp6[ci]], key=Rxp6[ci])
            wsl, Rw, col = (wx1, Rwx1, ci * 128) if ci < 4 else (wx2, Rwx2, (ci - 4) * 128)
            for ti, (t0, n) in enumerate(TILES):
                g = (ci * 5 + ti) % 2
                mm_group(PS[g][:, :n], [(wsl[:, kc, col:col + 128], xn[:, kc, t0:t0 + n]) for kc in range(8)],
                         reads=[Rw, Rxn[ti]], writes=[RPS[g]])
                if ti < 4:
                    if ti % 2 == 0:
                        act(xp[:, 3 + t0:3 + t0 + n], PS[g][:, :n], AF.Identity, reads=[RPS[g]], writes=[Rxp6[ci]])
                    else:
                        cp(xp[:, 3 + t0:3 + t0 + n], PS[g][:, :n], reads=[RPS[g]], writes=[Rxp6[ci]])
                else:
                    act(xp[:, 2054:2086], PS[g][:, 0:32], AF.Identity, reads=[RPS[g]], writes=[Rxp6[ci]])
                    cp(xp[:, 2089:2121], PS[g][:, 32:64], reads=[RPS[g]], writes=[Rxp6[ci]])
        for ci in range(6):
            b = ci % 2
            xp, uu = xp6[ci], u6[b]
            cw = lambda k: small[:, C_SCW + ci * 4 + k:C_SCW + ci * 4 + k + 1]
            cb = small[:, C_SCB + ci:C_SCB + ci + 1]
            for (s0, sl, s) in SEGS:
                p0 = POFF[s]
                tsc(uu[:, s0:s0 + sl], xp[:, p0:p0 + sl], cw(0), cb, ALU.mult, ALU.add, reads=[Rxp6[ci], Rsmall], writes=[Ru6[b]])
                for k in range(1, 4):
                    stt(uu[:, s0:s0 + sl], xp[:, p0 + k:p0 + k + sl], cw(k), uu[:, s0:s0 + sl], ALU.mult, ALU.add,
                        reads=[Rxp6[ci], Rsmall, Ru6[b]], writes=[Ru6[b]])
                dma("sp", sconv_o[l, :, s, ci, :], xp[:, p0 + sl:p0 + sl + 3], reads=[Rxp6[ci]], key=Rxp6[ci])
            if ci < 2:
                act(xs[:, ci, :], uu[:], AF.Silu, reads=[Ru6[b]], writes=[Rxs[ci]])
            elif ci < 4:
                act(BT[:, ci - 2, :], uu[:], AF.Silu, reads=[Ru6[b]], writes=[RBT])
            else:
                act(CT[:, ci - 4, :], uu[:], AF.Silu, reads=[Ru6[b]], writes=[RCT])
        for ti, (t0, n) in enumerate(TILES):
            mm_group(PS[2][0:4, :n], [(wx2[:, kc, 256:260], xn[:, kc, t0:t0 + n]) for kc in range(8)],
                     reads=[Rwx2, Rxn[ti]], writes=[RPS[2]])
            act(DT[0:4, t0:t0 + n], PS[2][0:4, :n], AF.Exp, reads=[RPS[2], Rsmall], writes=[RDT], bias=headp[0:4, 1:2])
        act(DT[0:4, :], DT[0:4, :], AF.Ln, reads=[RDT, Rconst], writes=[RDT], bias=one_t[0:4])
        act(at[:, 0:1], headp[0:4, 2:3], AF.Exp, reads=[Rsmall], writes=[Rat])
        tsc(at[:, 1:2], at[:, 0:1], -1.0, None, ALU.mult, None, reads=[Rat], writes=[Rat])
        tsc(DT[32:36, :], DT[0:4, :], at[:, 1:2], None, ALU.mult, None, reads=[RDT, Rat], writes=[RDT])
        for (s0, sl, s) in SEGS:
            scan(DT[64:68, s0:s0 + sl], one_t[32:36, 0:1].to_broadcast([4, sl]), DT[32:36, s0:s0 + sl], 0.0,
                 reads=[RDT, Rconst], writes=[RDT])
            act(ET[64:68, s0:s0 + sl], DT[64:68, s0:s0 + sl], AF.Exp, reads=[RDT], writes=[RET],
                bias=DT[64:68, s0 + sl - 1:s0 + sl], scale=-1.0)
        split3(DT[64:68, :], ET[0:4, :], ET[32:36, :], S1[64:68, :], S1[0:4, :], S1[32:36, :], RDT, RET, RS1)
        for si in range(2):
            last = NPR + si * 32 + 31
            act(at[:, 0:1], DT[64:68, last:last + 1], AF.Exp, reads=[RDT, Rat], writes=[Rat])
            tsc(ecd[:, si, :], ident_f[0:4, 0:4], at[:, 0:1], None, ALU.mult, None, reads=[Rat, Rconst], writes=[Rat])
            mm_group(PS[3][0:64, si * 4:si * 4 + 4], [(ones_f4[0:4, 0:64], ecd[:, si, :])], reads=[Rat, Rconst], writes=[RPS[3]])
        cp(ecb[:].rearrange("p a b -> p (a b)"), PS[3][0:64, 0:8], reads=[RPS[3]], writes=[Rat])
        P.barrier()
        sbx.release(mconv)
        sb.release(mu6)
        xtok = sbx.alloc("xtok", [128, NB, 256], BF16)
        xwtok = sbx.alloc("xwtok", [128, NB, 256], BF16)
        Btok = sbx.alloc("Btok", [128, NB, 256], BF16)
        Rxt, Rxw, RBt = Res("xtok"), Res("xwtok"), Res("Btok")
        aq = [sbx.alloc("aq", [6, T], BF16) for _ in range(4)]
        ak = [sbx.alloc("ak", [6, T], BF16) for _ in range(4)]
        Raq = [Res(f"aq{h}") for h in range(4)]
        Rak = [Res(f"ak{h}") for h in range(4)]
        Lb = [sbx.alloc("Lb", [128, 512], BF16) for _ in range(3)]
        Gb = [sb.alloc("Gb", [128, 512], BF16) for _ in range(8)]
        RL = [Res(f"L{i}") for i in range(3)]
        RG = [Res(f"G{i}") for i in range(8)]
        for h in range(4):
            memset(aq[h][:], 1.0, writes=[Raq[h]])
            memset(ak[h][:], 1.0, writes=[Rak[h]])
            dma("sp", aq[h][3:6, :], S1[h:h + 65:32, :], reads=[RS1], writes=[Raq[h]], key=Raq[h])
            dma("sp", ak[h][0:3, :], S1[h:h + 65:32, :], reads=[RS1], writes=[Rak[h]], key=Rak[h])
            tsc(ak[h][0:3, :], ak[h][0:3, :], -1.0, None, ALU.mult, None, reads=[Rak[h]], writes=[Rak[h]])
        PSb = [PS[i][:].bitcast(BF16) for i in range(8)]
        for tb, (k0, nb) in enumerate(blocks):
            g = tb % 2
            P.op("pe", (lambda k0=k0, nb=nb, g=g: (lambda e: e.transpose(out=PS[g][0:nb, 0:68], in_=DT[0:68, k0:k0 + nb],
                                                                       identity=ident_f[0:68, 0:68])))(),
                 reads=[RDT, Rconst], writes=[RPS[g]])
            cp(dctok[0:nb, tb, 0:4], PS[g][0:nb, 0:4], reads=[RPS[g]], writes=[Rdc])
            P.op("pe", (lambda k0=k0, nb=nb, g=g: (lambda e: e.transpose(out=PS[g][0:nb, 128:196], in_=ET[0:68, k0:k0 + nb],
                                                                       identity=ident_f[0:68, 0:68])))(),
                 reads=[RET, Rconst], writes=[RPS[g]])
            cp(dctok[0:nb, tb, 4:8], PS[g][0:nb, 192:196], reads=[RPS[g]], writes=[Rdc])
            tt(dctok[0:nb, tb, 8:12], dctok[0:nb, tb, 0:4], dctok[0:nb, tb, 4:8], ALU.mult, reads=[Rdc], writes=[Rdc])
            for c in range(2):
                g2 = 2 + c
                P.op("pe", (lambda k0=k0, nb=nb, c=c, g2=g2: (lambda e: e.transpose(
                    out=PS[g2][0:nb, 0:128], in_=xs[:, c, k0:k0 + nb], identity=ident_f[:])))(),
                    reads=[Rxs[c], Rconst], writes=[RPS[g2]])
                act(xtok[0:nb, tb, c * 128:(c + 1) * 128], PS[g2][0:nb, 0:128], AF.Identity, reads=[RPS[g2]], writes=[Rxt])
                for hh_ in range(2):
                    h = 2 * c + hh_
                    tsc(xwtok[0:nb, tb, h * 64:(h + 1) * 64], PS[g2][0:nb, hh_ * 64:(hh_ + 1) * 64], dctok[0:nb, tb, 8 + h:9 + h],
                        None, ALU.mult, None, reads=[RPS[g2], Rdc], writes=[Rxw])
            for gI in range(2):
                g3 = 4 + gI
                P.op("pe", (lambda k0=k0, nb=nb, gI=gI, g3=g3: (lambda e: e.transpose(
                    out=PSb[g3][0:nb, 0:128], in_=BT[:, gI, k0:k0 + nb], identity=ident_bf[:])))(),
                    reads=[RBT, Rconst], writes=[RPS[g3]])
                cp(Btok[0:nb, tb, gI * 128:(gI + 1) * 128], PSb[g3][0:nb, 0:128], reads=[RPS[g3]], writes=[RBt])
        cnt = [0]
        for Q in range(4):
            nsb = 4 * Q + 4
            pend = []
            for sbk in range(nsb + 1):
                cur = []
                if sbk < nsb:
                    d = sbk - 4 * Q
                    col0 = max(d, 0) * 128
                    for gI in range(2):
                        pcb = PS[gI]
                        mm_group(pcb[:, col0:512], [(BT[:, gI, sbk * 128:(sbk + 1) * 128], CT[:, gI, Q * 512 + col0:Q * 512 + 512])],
                                 reads=[RBT, RCT], writes=[RPS[gI]])
                        for hh_ in range(2):
                            h = 2 * gI + hh_
                            pe_ = PS[2 + hh_]
                            i3 = cnt[0] % 3
                            i8 = cnt[0] % 8
                            cnt[0] += 1

                            def fnE(e, pe_=pe_, col0=col0, d=d, h=h, sbk=sbk, Q=Q):
                                ins = e.matmul(pe_[:, col0:512], lhsT=ak[h][0:6, sbk * 128:(sbk + 1) * 128],
                                               rhs=aq[h][0:6, Q * 512 + col0:Q * 512 + 512], start=True, stop=(d < 0))
                                if d >= 0:
                                    ins = e.matmul(pe_[:, col0:col0 + 128], lhsT=ident_bf[:], rhs=maskneg[:], start=False, stop=True)
                                return ins
                            P.op("pe", fnE, reads=[Rak[h], Raq[h], Rconst], writes=[RPS[2 + hh_]])
                            act(Lb[i3][:, col0:512], pe_[:, col0:512], AF.Exp, reads=[RPS[2 + hh_]], writes=[RL[i3]])
                            stt(Gb[i8][:, col0:512], pcb[:, col0:512], dctok[:, sbk, h:h + 1], Lb[i3][:, col0:512], ALU.mult, ALU.mult,
                                reads=[RPS[gI], Rdc, RL[i3]], writes=[RG[i8]])
                            cur.append((sbk, h, i8, col0))
                        keep_warm(7 if gI == 0 else 6, NWARM)
                for (sb_, h_, i3_, c0_) in pend:
                    og = 4 + h_ // 2
                    r0 = (h_ % 2) * 64

                    def fnV(e, sb_=sb_, h_=h_, i3_=i3_, c0_=c0_, og=og, r0=r0, nsb=nsb):
                        return e.matmul(PS[og][r0:r0 + 64, c0_:512], lhsT=xtok[:, sb_, h_ * 64:(h_ + 1) * 64], rhs=Gb[i3_][:, c0_:512],
                                        start=(sb_ == 0), stop=(sb_ == nsb - 1))
                    P.op("pe", fnV, reads=[Rxt, RG[i3_]], writes=[RPS[og]])
                pend = cur
            for c in range(2):
                stt(xs[:, c, Q * 512:(Q + 1) * 512], xs[:, c, Q * 512:(Q + 1) * 512], small[:, C_SD + c:C_SD + c + 1], PS[4 + c][:, :],
                    ALU.mult, ALU.add, reads=[Rxs[c], Rsmall, RPS[4 + c]], writes=[Rxs[c]])
        h0T = sb.alloc("h0T", [128, 2, 4, 64], BF16)
        h0f = sb.alloc("h0f", [64, 2, 4, 128], F32)
        hst = [sb.alloc("hst", [64, 128], F32) for _ in range(2)]
        Rh0T, Rh0f = Res("h0T"), Res("h0f")
        Rhst = [Res("hst0"), Res("hst1")]
        dma("pool", h0T[:], sh0_d[l].rearrange("s h n p -> n s h p"), writes=[Rh0T], key=Rh0T)
        dma("sp", h0f[:], sh0f_d[l].rearrange("s h p n -> p s h n"), writes=[Rh0f], key=Rh0f)
        for si in range(2):
            tk = slice(NPR + si * 32, NPR + si * 32 + 32)
            tb = 16 + si
            for h in range(4):
                gI = h // 2
                r0 = (h % 2) * 64
                og = 4 + gI
                mm_group(PS[2][:, 0:32], [(akv[0:6, :], aq[h][0:6, tk])], reads=[Rconst, Raq[h]], writes=[RPS[2]])
                act(Lb[0][:, 0:32], PS[2][:, 0:32], AF.Exp, reads=[RPS[2]], writes=[RL[0]])
                tt(Gb[0][:, 0:32], CT[:, gI, tk], Lb[0][:, 0:32], ALU.mult, reads=[RCT, RL[0]], writes=[RG[0]])
                mm_group(PS[0][0:32, 0:32], [(BT[:, gI, tk], CT[:, gI, tk])], reads=[RBT, RCT], writes=[RPS[0]])

                def fnE2(e, h=h, tk=tk):
                    e.matmul(PS[3][0:32, 0:32], lhsT=ak[h][0:6, tk], rhs=aq[h][0:6, tk], start=True, stop=False)
                    return e.matmul(PS[3][0:32, 0:32], lhsT=ident_bf[0:32, 0:32], rhs=maskneg[0:32, 0:32], start=False, stop=True)
                P.op("pe", fnE2, reads=[Rak[h], Raq[h], Rconst], writes=[RPS[3]])
                act(Lb[1][0:32, 0:32], PS[3][0:32, 0:32], AF.Exp, reads=[RPS[3]], writes=[RL[1]])
                stt(Gb[1][0:32, 0:32], PS[0][0:32, 0:32], dctok[0:32, tb, h:h + 1], Lb[1][0:32, 0:32], ALU.mult, ALU.mult,
                    reads=[RPS[0], Rdc, RL[1]], writes=[RG[1]])

                def fnV2(e, h=h, si=si, tb=tb, og=og, r0=r0):
                    e.matmul(PS[og][r0:r0 + 64, 0:32], lhsT=h0T[:, si, h, :], rhs=Gb[0][:, 0:32], start=True, stop=False)
                    return e.matmul(PS[og][r0:r0 + 64, 0:32], lhsT=xtok[0:32, tb, h * 64:(h + 1) * 64], rhs=Gb[1][0:32, 0:32],
                                    start=False, stop=True)
                P.op("pe", fnV2, reads=[Rh0T, Rxt, RG[0], RG[1]], writes=[RPS[og]])
                stt(xs[r0:r0 + 64, gI, tk], xs[r0:r0 + 64, gI, tk], small[r0:r0 + 64, C_SD + gI:C_SD + gI + 1], PS[og][r0:r0 + 64, 0:32],
                    ALU.mult, ALU.add, reads=[Rxs[gI], Rsmall, RPS[og]], writes=[Rxs[gI]])
        fcnt = 0
        for s in range(3):
            for h in range(4):
                g = 6 + fcnt % 2
                b = fcnt % 2
                fcnt += 1
                mm_group(PS[g][0:64, 0:128],
                         [(xwtok[0:blocks[tb][1], tb, h * 64:(h + 1) * 64], Btok[0:blocks[tb][1], tb, (h // 2) * 128:(h // 2) * 128 + 128])
                          for tb in seq_blocks[s]], reads=[Rxw, RBt], writes=[RPS[g]])
                if s == 0:
                    cp(hst[b][:], PS[g][0:64, 0:128], reads=[RPS[g]], writes=[Rhst[b]])
                else:
                    stt(hst[b][:], h0f[:, s - 1, h, :], ecb[:, s - 1, h:h + 1], PS[g][0:64, 0:128], ALU.mult, ALU.add,
                        reads=[Rh0f, Rat, RPS[g]], writes=[Rhst[b]])
                dma("sp", sh_o[l, s, h], hst[b][:], reads=[Rhst[b]], key=Rhst[b])
        P.barrier()
        sbx.release(mconv)
        wz = sbx.alloc("wz", [128, 8, 256], BF16)
        Rwz = Res("wz")
        dma("pool", wz[:], winv[:, :, 2056:2312], writes=[Rwz], key=Rwz)
        zt = [sbx.alloc("zt", [128, 512], F32) for _ in range(2)]
        Rzt = [Res("zt0"), Res("zt1")]
        sqz = sbx.alloc("sqz", [128, 2, 512], BF16)
        rsz = sbx.alloc("rsz", [128, 512], F32)
        Rsqz, Rrsz = Res("sqz"), Res("rsz")
        for ti, (t0, n) in enumerate(TILES):
            for c in range(2):
                mm_group(PS[c][:, :n], [(wz[:, kc, c * 128:(c + 1) * 128], xn[:, kc, t0:t0 + n]) for kc in range(8)],
                         reads=[Rwz, Rxn[ti]], writes=[RPS[c]])
                act(zt[c][:, :n], PS[c][:, :n], AF.Silu, reads=[RPS[c]], writes=[Rzt[c]])
                tt(xs[:, c, t0:t0 + n], xs[:, c, t0:t0 + n], zt[c][:, :n], ALU.mult, reads=[Rxs[c], Rzt[c]], writes=[Rxs[c]])
                act(sqz[:, c, :n], xs[:, c, t0:t0 + n], AF.Square, reads=[Rxs[c]], writes=[Rsqz])
            mm_group(PS[2][:, :n], [(ones_bf[:], sqz[:, c, :n]) for c in range(2)], reads=[Rsqz, Rconst], writes=[RPS[2]])
            act(rsz[:, :n], PS[2][:, :n], AF.Sqrt, reads=[RPS[2], Rconst], writes=[Rrsz], bias=eps_t[:], scale=1.0 / 256)
            recip(rsz[:, :n], rsz[:, :n], reads=[Rrsz], writes=[Rrsz])
            for c in range(2):
                stt(ymix[:, 6 + c, t0:t0 + n], xs[:, c, t0:t0 + n], small[:, C_SNW + c:C_SNW + c + 1], rsz[:, :n], ALU.mult, ALU.mult,
                    reads=[Rxs[c], Rsmall, Rrsz], writes=[Rym[6 + c]])
        P.barrier()
        sb.release(ms)
        sbx.release(mx)

    sbx = SB(nc, SB.LO, SB.LO + 8 * T * 4)
    Rspill = Res("xspill")
    POFF = {0: 0, 1: 2051, 2: 2086}
    XPW = 2121
    ones_f4 = sb.alloc("ones_f4", [4, 64], F32)
    memset(ones_f4[:], 1.0, writes=[Rconst])
    akv = sb.alloc("akv", [6, 128], BF16)
    memset(akv[:], 1.0, writes=[Rconst])
    P.op("pool", lambda e: e.affine_select(out=akv[:], in_=akv[:], pattern=[[0, 128]], compare_op=ALU.is_ge,
                                           fill=0.0, base=-3, channel_multiplier=1), reads=[Rconst], writes=[Rconst])
    ARENA2 = sb.mark()

    def split3(F_ap, R1_ap, R2_ap, hi_ap, mid_ap, lo_ap, RF, RR, RS):
        cp(hi_ap, F_ap, reads=[RF], writes=[RS])
        tt(R1_ap, F_ap, hi_ap, ALU.subtract, reads=[RF, RS], writes=[RR])
        cp(mid_ap, R1_ap, reads=[RR], writes=[RS])
        tt(R2_ap, R1_ap, mid_ap, ALU.subtract, reads=[RR, RS], writes=[RR])
        cp(lo_ap, R2_ap, reads=[RR], writes=[RS])

    def mix(l):
        j = 1
        m_arena = sb.mark()
        ymix = sb.alloc("ymix", [128, 8, T], BF16)
        Rym = [Res(f"ym{c}") for c in range(8)]
        xn = sb.alloc("xn", [128, 8, T], BF16)
        Rxn = {ti: Res(f"xn{ti}") for ti in range(5)}
        m1 = sb.mark()
        scrs = [alloc_norm_scratch(), alloc_norm_scratch()]
        for st_ in skewed(norm_pre_steps(1, list(range(5)), xn, Rxn, 0, scrs)):
            st_()
        for ti, (t0, n) in enumerate(TILES):
            dma("sp", xspill[:, :, t0:t0 + n], X[:, :, t0:t0 + n], reads=[RX[ti]], writes=[Rspill], key=RX[ti])
        P.barrier()
        sb.release(m1)
        winv = win_d[l].rearrange("(kc p) n -> p kc n", p=128)
        allxn = [Rxn[ti] for ti in range(5)]

        def lru():
            ms, mx = sb.mark(), sbx.mark()
            wl = sb.alloc("wl", [128, 8, 512], BF16)
            Rwl = Res("wl")
            dma("pool", wl[:], winv[:, :, 0:512], writes=[Rwl], key=Rwl)
            h0sb = sb.alloc("h0sb", [128, 2, 2], F32)
            Rh0 = Res("h0")
            dma("sp", h0sb[:], lh_d[l], writes=[Rh0], key=Rh0)
            bd = [[sb.alloc("bd", [128, 128], BF16) for c in range(2)] for g in range(2)]
            Rbd = [[Res("bd") for c in range(2)] for g in range(2)]
            for g, src in ((0, lwa_d), (1, lwx_d)):
                for c in range(2):
                    memset(bd[g][c][:], 0.0, writes=[Rbd[g][c]])
                    for hh_ in range(2):
                        dma("pool", bd[g][c][hh_ * 64:(hh_ + 1) * 64, hh_ * 64:(hh_ + 1) * 64], src[l, 2 * c + hh_],
                            writes=[Rbd[g][c]], key=Rbd[g][c])
            c1t = sb.alloc("c1t", [128, 2, 4], F32)
            Rc1 = Res("c1t")
            wmx = [sb.alloc("wmx", [128, 8, 512], BF16) for _ in range(2)]
            Rwmx = [Res("wmx0"), Res("wmx1")]
            extra = mod_thunks(l, [2], wmx, Rwmx)
            if l + 1 < 2:
                extra += mod_thunks(l + 1, [0, 1], wmx, Rwmx)

            def more(k=1):
                for _ in range(k):
                    if extra:
                        extra.pop(0)()
            xp = sbx.alloc("xp", [128, XPW], F32)
            u = sbx.alloc("u", [128, T], F32)
            ub = sbx.alloc("ub", [128, T], BF16)
            r = sbx.alloc("r", [128, T], F32)
            ig = sbx.alloc("ig", [128, T], F32)
            a = sbx.alloc("a", [128, T], F32)
            t1 = sbx.alloc("t1", [128, T], F32)
            hh = sb.alloc("hh", [128, T], F32)
            gg = sb.alloc("gg", [128, T], F32)
            Rxp, Ru, Rub, Rr, Rig, Ra, Rt1, Rhh, Rgg = [Res(n) for n in "xp u ub r ig a t1 hh gg".split()]
            for c in range(2):
                lam = small[:, C_LLAM + c:C_LLAM + c + 1]
                act(c1t[:, c, 0:1], lam, AF.Exp, reads=[Rsmall], writes=[Rc1], scale=-1.0)
                act(c1t[:, c, 1:2], c1t[:, c, 0:1], AF.Ln, reads=[Rc1, Rconst], writes=[Rc1], bias=one_t[:])
                tsc(c1t[:, c, 2:3], c1t[:, c, 1:2], -8.0, None, ALU.mult, None, reads=[Rc1], writes=[Rc1])
                tsc(c1t[:, c, 3:4], c1t[:, c, 1:2], -16.0, None, ALU.mult, None, reads=[Rc1], writes=[Rc1])
                memset(xp[:, 0:3], 0.0, writes=[Rxp])
                dma("sp", xp[:, 2051:2054], lconv_d[l, :, 0, c, :], writes=[Rxp], key=Rxp)
                dma("sp", xp[:, 2086:2089], lconv_d[l, :, 1, c, :], writes=[Rxp], key=Rxp)
                for ti, (t0, n) in enumerate(TILES):
                    pa, pb = PS[ti % 2], PS[2 + ti % 2]
                    mm_group(pa[:, :n], [(wl[:, kc, c * 128:(c + 1) * 128], xn[:, kc, t0:t0 + n]) for kc in range(8)],
                             reads=[Rwl, Rxn[ti]], writes=[RPS[ti % 2]])
                    mm_group(pb[:, :n], [(wl[:, kc, 256 + c * 128:256 + (c + 1) * 128], xn[:, kc, t0:t0 + n]) for kc in range(8)],
                             reads=[Rwl, Rxn[ti]], writes=[RPS[2 + ti % 2]])
                    if ti < 4:
                        act(xp[:, 3 + t0:3 + t0 + n], pa[:, :n], AF.Identity, reads=[RPS[ti % 2]], writes=[Rxp])
                    else:
                        act(xp[:, 2054:2086], pa[:, 0:32], AF.Identity, reads=[RPS[ti % 2]], writes=[Rxp])
                        act(xp[:, 2089:2121], pa[:, 32:64], AF.Identity, reads=[RPS[ti % 2]], writes=[Rxp])
                    cp(gg[:, t0:t0 + n], pb[:, :n], reads=[RPS[2 + ti % 2]], writes=[Rgg])
                    more(1)
                cw = lambda k: small[:, C_LCW + c * 4 + k:C_LCW + c * 4 + k + 1]
                cb = small[:, C_LCB + c:C_LCB + c + 1]
                for (s0, sl, s) in SEGS:
                    p0 = POFF[s]
                    tsc(u[:, s0:s0 + sl], xp[:, p0:p0 + sl], cw(0), cb, ALU.mult, ALU.add, reads=[Rxp, Rsmall], writes=[Ru])
                    for k in range(1, 4):
                        stt(u[:, s0:s0 + sl], xp[:, p0 + k:p0 + k + sl], cw(k), u[:, s0:s0 + sl], ALU.mult, ALU.add,
                            reads=[Rxp, Rsmall, Ru], writes=[Ru])
                    dma("sp", lconv_o[l, :, s, c, :], xp[:, p0 + sl:p0 + sl + 3], reads=[Rxp], key=Rxp)
                act(ub[:], u[:], AF.Identity, reads=[Ru], writes=[Rub])
                for ti, (t0, n) in enumerate(TILES):
                    pa, pb = PS[ti % 2], PS[2 + ti % 2]
                    mm_group(pa[:, :n], [(bd[0][c][:], ub[:, t0:t0 + n])], reads=[Rbd[0][c], Rub], writes=[RPS[ti % 2]])
                    mm_group(pb[:, :n], [(bd[1][c][:], ub[:, t0:t0 + n])], reads=[Rbd[1][c], Rub], writes=[RPS[2 + ti % 2]])
                    act(r[:, t0:t0 + n], pa[:, :n], AF.Sigmoid, reads=[RPS[ti % 2], Rsmall], writes=[Rr],
                        bias=small[:, C_LBA + c:C_LBA + c + 1])
                    act(ig[:, t0:t0 + n], pb[:, :n], AF.Sigmoid, reads=[RPS[2 + ti % 2], Rsmall], writes=[Rig],
                        bias=small[:, C_LBX + c:C_LBX + c + 1])
                    more(1)
                act(a[:], r[:], AF.Exp, reads=[Rr, Rc1], writes=[Ra], scale=c1t[:, c, 2:3])
                act(t1[:], r[:], AF.Exp, reads=[Rr, Rc1], writes=[Rt1], scale=c1t[:, c, 3:4])
                tsc(t1[:], t1[:], -1.0, 1.0, ALU.mult, ALU.add, reads=[Rt1], writes=[Rt1])
                tsc(t1[:], t1[:], 1e-30, None, ALU.max, None, reads=[Rt1], writes=[Rt1])
                act(t1[:], t1[:], AF.Sqrt, reads=[Rt1], writes=[Rt1])
                tt(ig[:], ig[:], u[:], ALU.mult, reads=[Rig, Ru], writes=[Rig])
                tt(ig[:], ig[:], t1[:], ALU.mult, reads=[Rig, Rt1], writes=[Rig])
                for (s0, sl, s) in SEGS:
                    init = 0.0 if s == 0 else h0sb[:, s - 1, c:c + 1]
                    scan(hh[:, s0:s0 + sl], a[:, s0:s0 + sl], ig[:, s0:s0 + sl], init, reads=[Ra, Rig, Rh0], writes=[Rhh])
                    dma("sp", lh_o[l, :, s, c:c + 1], hh[:, s0 + sl - 1:s0 + sl], reads=[Rhh], key=Rhh, slow=True)
                act(t1[:], gg[:], AF.Square, reads=[Rgg], writes=[Rt1])
                tsc(t1[:], t1[:], 0.044715, 1.0, ALU.mult, ALU.add, reads=[Rt1], writes=[Rt1])
                tt(t1[:], t1[:], gg[:], ALU.mult, reads=[Rt1, Rgg], writes=[Rt1])
                act(t1[:], t1[:], AF.Sigmoid, reads=[Rt1], writes=[Rt1], scale=1.5957691216057308)
                tt(gg[:], gg[:], t1[:], ALU.mult, reads=[Rgg, Rt1], writes=[Rgg])
                tt(ymix[:, c, :], hh[:], gg[:], ALU.mult, reads=[Rhh, Rgg], writes=[Rym[c]])
            more(100)
            P.barrier()
            sb.release(ms)
            sbx.release(mx)

        lru()
        if STAGE_SUB >= 2:
            ssd(l, ymix, Rym, xn, Rxn, winv)
        if STAGE_SUB >= 3:
            fox(l, ymix, Rym, xn, Rxn, winv)

        if DBG:
            md = sb.mark()
            stg = [sb.alloc("dstg", [128, 512], F32) for _ in range(2)]
            Rstg = [Res("dstg0"), Res("dstg1")]
            kk = 0
            for c in range(8):
                for ti, (t0, n) in enumerate(TILES):
                    b = kk % 2
                    kk += 1
                    cp(stg[b][:, :n], ymix[:, c, t0:t0 + n], reads=[Rym[c]], writes=[Rstg[b]])
                    dma("sp", dbg_o[:, c, t0:t0 + n], stg[b][:, :n], reads=[Rstg[b]], key=Rstg[b])
            P.barrier()
            sb.release(md)
        for ti, (t0, n) in enumerate(TILES):
            dma("sp", X[:, :, t0:t0 + n], xspill[:, :, t0:t0 + n], reads=[Rspill], writes=[RX[ti]], key=RX[ti])
        m2 = sb.mark()
        wo = sb.alloc("wo", [128, 8, 1024], BF16)
        Rwo = Res("wo")
        dma("pool", wo[:], wout_d[l].rearrange("(kc p) n -> p kc n", p=128), writes=[Rwo], key=Rwo)
        scr = alloc_norm_scratch()
        for half in ([0, 1], [2, 3, 4]):
            t_lo = TILES[half[0]][0]
            t_hi = TILES[half[-1]][0] + TILES[half[-1]][1]
            m3 = sb.mark()
            yacc = sb.alloc("yacc", [128, 8, t_hi - t_lo], F32)
            Ry = {ti: Res(f"y{ti}") for ti in half}
            cnt = 0
            for ti in half:
                t0, n = TILES[ti]
                for oc in range(8):
                    g = cnt % 2
                    cnt += 1
                    mm_group(PS[g][:, :n], [(wo[:, kc, oc * 128:(oc + 1) * 128], ymix[:, kc, t0:t0 + n]) for kc in range(8)],
                             reads=[Rwo] + Rym, writes=[RPS[g]])
                    if oc % 2 == 0:
                        act(yacc[:, oc, t0 - t_lo:t0 - t_lo + n], PS[g][:, :n], AF.Identity, reads=[RPS[g]], writes=[Ry[ti]])
                    else:
                        cp(yacc[:, oc, t0 - t_lo:t0 - t_lo + n], PS[g][:, :n], reads=[RPS[g]], writes=[Ry[ti]])
            post_norm(1, half, yacc, Ry, t_lo, scr)
            P.barrier()
            sb.release(m3)
        P.barrier()
        sb.release(m_arena)

    nsub = 0
    done = False
    m0_ = sb.mark()
    wm0 = [sb.alloc("wm", [128, 8, 512], BF16) for _ in range(2)]
    Rwm0 = [Res("wm0"), Res("wm1")]
    for th in mod_thunks(0, [0, 1], wm0, Rwm0):
        th()
    P.barrier()
    sb.release(m0_)
    for l in range(2):
        CUR[0] = l
        for kind in ("ffn0", "mix", "ffn1"):
            if nsub >= STAGE:
                done = True
                break
            if kind == "ffn0":
                ffn(l, 0, 0)
            elif kind == "mix":
                mix(l)
            else:
                ffn(l, 1, 2)
            nsub += 1
        if done:
            break

    for t, (t0, n) in enumerate(TILES):
        dma("sp", yT_o[:, :, t0:t0 + n], X[:, :, t0:t0 + n], reads=[RX[t]], key=RX[t])
    P.barrier()
    P.op("sp", lambda e: e.nop())
    P.emit(st)
    st.close()
    print(f"[mk] ops={len(P.ops)} sems={P.n_sems} sbuf_peak={sb.peak}")
    return nc


def _prep_core(core, I):
    f = np.float32
    b = core
    s0, s1 = 2 * core, 2 * core + 2
    xcat = np.concatenate([I["x_prompt"][b], I["x_sample"][s0], I["x_sample"][s0 + 1]], axis=0)
    xT = np.ascontiguousarray(xcat.reshape(T, 8, 128).transpose(2, 1, 0)).astype(f)
    ccat = np.concatenate([I["c_prompt"][b:b + 1], I["c_sample"][s0:s1]], axis=0)
    cT = np.ascontiguousarray(ccat.reshape(3, 8, 128).transpose(2, 1, 0)).astype(f)
    m = {"xT": xT, "cT": cT}
    m["ckT"] = np.ascontiguousarray(I["cache_fox_k"][:, s0:s1].transpose(0, 1, 3, 4, 2))
    a = I["cache_fox_v"][:, s0:s1].reshape(2, 2, 32, 128, 8, 64)
    m["cvh"] = np.ascontiguousarray(a.transpose(0, 1, 4, 3, 2, 5)).reshape(2, 2, 8, 128, 2048)
    m["clfT"] = np.ascontiguousarray(I["cache_fox_logf"][:, s0:s1].transpose(0, 1, 3, 2))
    a = I["state_lru_conv"][:, s0:s1].reshape(2, 2, 3, 2, 128)
    m["lconvT"] = np.ascontiguousarray(a.transpose(0, 4, 1, 3, 2))
    a = I["state_lru_h"][:, s0:s1].reshape(2, 2, 2, 128)
    m["lhT"] = np.ascontiguousarray(a.transpose(0, 3, 1, 2))
    a = I["state_ssd_conv"][:, s0:s1].reshape(2, 2, 3, 6, 128)
    m["sconvT"] = np.ascontiguousarray(a.transpose(0, 4, 1, 3, 2))
    m["sh0T"] = np.ascontiguousarray(I["state_ssd_h"][:, s0:s1].transpose(0, 1, 2, 4, 3))
    m["sh0f"] = np.ascontiguousarray(I["state_ssd_h"][:, s0:s1])
    return m


def _prep_shared(I):
    f = np.float32
    sm = np.zeros((2, 128, NSMALL), f)

    def fm(a, nch):
        sh = a.shape[:-1]
        a = a.reshape(sh + (nch, 128))
        return np.moveaxis(a, -1, 0)

    for l in range(2):
        sm[l, :, 0:24] = fm(I["norm_pre"][l], 8).reshape(128, 24)
        sm[l, :, 24:48] = fm(I["norm_post"][l], 8).reshape(128, 24)
        sm[l, :, 48:120] = I["b_mod"][l].reshape(72, 128).T
        sm[l, :, 120:128] = fm(I["lru_conv_w"][l], 2).transpose(0, 2, 1).reshape(128, 8)
        sm[l, :, 128:130] = fm(I["lru_conv_b"][l], 2)
        sm[l, :, 130:132] = fm(I["lru_ba"][l], 2)
        sm[l, :, 132:134] = fm(I["lru_bx"][l], 2)
        sm[l, :, 134:136] = fm(I["lru_lambda"][l], 2)
        sm[l, :, 136:160] = fm(I["ssd_conv_w"][l], 6).transpose(0, 2, 1).reshape(128, 24)
        sm[l, :, 160:166] = fm(I["ssd_conv_b"][l], 6)
        sm[l, :, 166:168] = fm(np.repeat(I["ssd_d"][l], 64), 2)
        sm[l, :, 168:170] = fm(I["ssd_norm_w"][l], 2)
    hp = np.zeros((2, 8, 4), f)
    hp[:, :, 0] = I["fox_f_bias"]
    hp[:, 0:4, 1] = I["ssd_dt_bias"]
    hp[:, 0:4, 2] = I["ssd_a_log"]
    sh = {"smallp": sm, "headp": hp}
    for k in ["w_mod", "ffn_w_gate", "ffn_w_up", "ffn_w_down", "w_in", "w_out", "lru_wa", "lru_wx"]:
        sh[k] = np.ascontiguousarray(I[k], dtype=f)
    return sh


_NC_CACHE = {}


def kernel(**inputs):
    I = {k: np.asarray(v) for k, v in inputs.items()}
    if "nc" not in _NC_CACHE:
        _NC_CACHE["nc"] = build_program()
    nc = _NC_CACHE["nc"]
    shared = _prep_shared(I)
    in_maps = []
    for c in range(NRUN):
        m = dict(shared)
        m.update(_prep_core(c, I))
        in_maps.append(m)
    res = run_bass_kernel_spmd(nc, in_maps, core_ids=list(range(NRUN)))
    R = res.results
    _NC_CACHE["last"] = R
    f = np.float32
    nb = 2 * NRUN
    y_p = np.zeros((8, NPR, D), f)
    y_s = np.zeros((16, NSQ, D), f)
    pk = np.zeros((2, 8, NPR, 8, 64), f); pv = np.zeros((2, 8, NPR, 8, 64), f); plf = np.zeros((2, 8, NPR, 8), f)
    plc = np.zeros((2, 8, 3, 256), f); plh = np.zeros((2, 8, 256), f); psc = np.zeros((2, 8, 3, 768), f)
    psh = np.zeros((2, 8, 4, 64, 128), f)
    sk = np.zeros((2, 16, NSQ, 8, 64), f); sv = np.zeros((2, 16, NSQ, 8, 64), f); slf = np.zeros((2, 16, NSQ, 8), f)
    slc = np.zeros((2, 16, 3, 256), f); slh = np.zeros((2, 16, 256), f); ssc = np.zeros((2, 16, 3, 768), f)
    ssh = np.zeros((2, 16, 4, 64, 128), f)
    for c in range(NRUN):
        r = R[c]
        y = r["yT"].transpose(2, 1, 0).reshape(T, D)
        y_p[c] = y[:NPR]
        y_s[2 * c] = y[NPR:NPR + NSQ]
        y_s[2 * c + 1] = y[NPR + NSQ:]
        k = r["fk_out"].transpose(0, 3, 2, 1).reshape(2, T, 8, 64)
        v = r["fv_out"].reshape(2, T, 8, 64)
        lf = r["logfT"].transpose(0, 2, 1)
        pk[:, c] = k[:, :NPR]; pv[:, c] = v[:, :NPR]; plf[:, c] = lf[:, :NPR]
        for si in range(2):
            sl = slice(NPR + si * NSQ, NPR + (si + 1) * NSQ)
            sk[:, 2 * c + si] = k[:, sl]; sv[:, 2 * c + si] = v[:, sl]; slf[:, 2 * c + si] = lf[:, sl]
        lc = r["lconv_o"].transpose(0, 2, 4, 3, 1).reshape(2, 3, 3, 256)
        lh = r["lh_o"].transpose(0, 2, 3, 1).reshape(2, 3, 256)
        sc = r["sconv_o"].transpose(0, 2, 4, 3, 1).reshape(2, 3, 3, 768)
        sh = r["sh_o"]
        plc[:, c] = lc[:, 0]; plh[:, c] = lh[:, 0]; psc[:, c] = sc[:, 0]; psh[:, c] = sh[:, 0]
        for si in range(2):
            slc[:, 2 * c + si] = lc[:, 1 + si]; slh[:, 2 * c + si] = lh[:, 1 + si]
            ssc[:, 2 * c + si] = sc[:, 1 + si]; ssh[:, 2 * c + si] = sh[:, 1 + si]
    return (y_p, y_s, pk, pv, plf, plc, plh, psc, psh, sk, sv, slf, slc, slh, ssc, ssh)
```

```python
import os
import math
import numpy as np
from contextlib import ExitStack
import concourse.bass as bass
import concourse.mybir as mybir
from concourse.bass_utils import run_bass_kernel_spmd

F32 = mybir.dt.float32
BF16 = mybir.dt.bfloat16
AF = mybir.ActivationFunctionType
ALU = mybir.AluOpType

NCORES = 8
D = 1024
NPR = 2048
NSQ = 32
T = NPR + 2 * NSQ
DFF = 2816
DIN = 3084
PAST = 4096
EPS = 1e-6
TILES = [(0, 512), (512, 512), (1024, 512), (1536, 512), (2048, 64)]
SEGS = [(0, 2048, 0), (2048, 32, 1), (2080, 32, 2)]
NSMALL = 170
ENGS = ("pe", "act", "dve", "pool", "sp")
SEM_LIMIT = 30000
SWDGE_DEPTH = int(os.environ.get("MK_SWDGE", "3"))
STAGE = int(os.environ.get("MK_STAGE", "99"))
STAGE_SUB = int(os.environ.get("MK_SUB", "99"))
DBG = int(os.environ.get("MK_DBG", "0"))
FOXS = int(os.environ.get("MK_FOX", "99"))
TOG = os.environ.get("MK_TOG", "")
NRUN = int(os.environ.get("MK_CORES", "8"))


class Res:
    __slots__ = ("name", "last_w", "readers")

    def __init__(self, name):
        self.name = name
        self.last_w = None
        self.readers = []


class Op:
    __slots__ = ("idx", "eng", "fn", "deps", "is_dma", "key", "sig", "sem", "val", "epoch", "bar")

    def __init__(self, idx, eng, fn, is_dma, key):
        self.idx = idx
        self.eng = eng
        self.fn = fn
        self.deps = set()
        self.is_dma = is_dma
        self.key = key
        self.sig = False
        self.sem = None
        self.val = 0


class Prog:
    def __init__(self, nc):
        self.nc = nc
        self.ops = []
        self.barrier_deps = {e: set() for e in ENGS}
        self.last_on_eng = {e: None for e in ENGS}
        self.dma_last = {}
        self.epoch = 0

    def op(self, eng, fn, reads=(), writes=(), dma=False, key=None):
        idx = len(self.ops)
        o = Op(idx, eng, fn, dma, key)
        o.epoch = self.epoch
        o.bar = set()
        if self.barrier_deps[eng]:
            o.deps |= self.barrier_deps[eng]
            o.bar = set(self.barrier_deps[eng])
            self.barrier_deps[eng] = set()
        for r in reads:
            if r.last_w is not None:
                o.deps.add(r.last_w)
        for w in writes:
            if w.last_w is not None:
                o.deps.add(w.last_w)
            for rd in w.readers:
                o.deps.add(rd)
        for r in reads:
            r.readers.append(idx)
        for w in writes:
            w.last_w = idx
            w.readers = []
        o.deps.discard(idx)
        if dma:
            assert key is not None
            self.dma_last[key] = idx
        self.ops.append(o)
        self.last_on_eng[eng] = idx
        return o

    def barrier(self):
        s = set()
        for e in ENGS:
            if self.last_on_eng[e] is not None:
                s.add(self.last_on_eng[e])
        for k, v in self.dma_last.items():
            s.add(v)
        for e in ENGS:
            self.barrier_deps[e] |= s
        self.dma_last = {}
        self.epoch += 1
        for e in ENGS:
            self.op(e, lambda eng: eng.nop())

    def emit(self, stack):
        nc = self.nc
        ops = self.ops
        for o in ops:
            for d in o.deps:
                ops[d].sig = True
            if o.is_dma and o.eng == "pool":
                o.sig = True
        eng_sem, eng_cnt = {}, {}
        nsem = [0]

        def new_sem(nm):
            nsem[0] += 1
            return stack.enter_context(nc.semaphore(nm + str(nsem[0])))

        sw_keys = set()
        for o in ops:
            if o.is_dma and o.eng == "pool":
                sw_keys.add((o.key, o.epoch))
        pools = {True: [], False: []}
        limbo = {True: [], False: []}
        active, cur_epoch = {}, 0
        for o in ops:
            if o.epoch != cur_epoch:
                for sw in (True, False):
                    pools[sw].extend(limbo[sw])
                    limbo[sw] = []
                for (k, sw), sc in active.items():
                    limbo[sw].append(sc)
                active = {}
                cur_epoch = o.epoch
            if o.is_dma:
                sw = (o.key, o.epoch) in sw_keys
                if (o.key, sw) not in active:
                    active[(o.key, sw)] = pools[sw].pop() if pools[sw] else [new_sem("ds" if sw else "dh"), 0]
                sc = active[(o.key, sw)]
                sc[1] += 16
                o.sem, o.val, o.sig = sc[0], sc[1], True
            elif o.sig:
                e = o.eng
                if e not in eng_sem or eng_cnt[e] >= SEM_LIMIT:
                    eng_sem[e] = new_sem("e" + e)
                    eng_cnt[e] = 0
                eng_cnt[e] += 1
                o.sem = eng_sem[e]
                o.val = eng_cnt[e]
        self.n_sems = nsem[0]
        per_eng = {e: [o for o in ops if o.eng == e] for e in ENGS}
        block = stack.enter_context(nc.Block())

        def make(e):
            lst = per_eng[e]

            def body(eng):
                waited = {}
                issued = []
                for o in lst:
                    need = {}
                    if e == "pool" and o.is_dma:
                        if len(issued) >= SWDGE_DEPTH:
                            p = issued[-SWDGE_DEPTH]
                            if waited.get(id(p.sem), 0) < p.val:
                                need[id(p.sem)] = (p.sem, p.val)
                        issued.append(o)
                    for d in o.deps:
                        p = ops[d]
                        if e == "pe" and p.eng == "pe" and not p.is_dma:
                            continue
                        if p.epoch < o.epoch and d not in o.bar:
                            continue
                        sid = id(p.sem)
                        if waited.get(sid, 0) >= p.val:
                            continue
                        if sid not in need or need[sid][1] < p.val:
                            need[sid] = (p.sem, p.val)
                    for sid, (sem, val) in need.items():
                        eng.wait_ge(sem, val)
                        waited[sid] = val
                    ins = o.fn(eng)
                    if o.sig:
                        ins.then_inc(o.sem, 16 if o.is_dma else 1)
            return body

        if per_eng["pe"]:
            block.tensor(make("pe"))
        if per_eng["act"]:
            block.scalar(make("act"))
        if per_eng["dve"]:
            block.vector(make("dve"))
        if per_eng["pool"]:
            block.gpsimd(make("pool"))
        if per_eng["sp"]:
            block.sync(make("sp"))


class SB:
    LO = 16512
    HI = 229376

    CNT = [0]

    def __init__(self, nc, lo=None, hi=None):
        self.nc = nc
        self.lo = SB.LO if lo is None else lo
        self.hi = SB.HI if hi is None else hi
        self.top = self.lo
        self.peak = 0

    def alloc(self, name, shape, dtype):
        esz = 2 if dtype == BF16 else 4
        nb = esz
        for s in shape[1:]:
            nb *= s
        nb = (nb + 63) // 64 * 64
        off = self.top
        self.top += nb
        self.peak = max(self.peak, self.top)
        assert self.top <= self.hi, f"SBUF overflow allocating {name}: {self.top} > {self.hi}"
        SB.CNT[0] += 1
        return self.nc.alloc_sbuf_tensor_at(f"{name}_{SB.CNT[0]}", list(shape), dtype, offset=off)

    def mark(self):
        return self.top

    def release(self, m):
        self.top = m


def build_program():
    nc = bass.Bass("TRN2", target_bir_lowering=False)

    def din(name, shape):
        return nc.dram_tensor(name, list(shape), F32, kind="ExternalInput").ap()

    def dout(name, shape):
        return nc.dram_tensor(name, list(shape), F32, kind="ExternalOutput").ap()

    xT_d = din("xT", [128, 8, T])
    cT_d = din("cT", [128, 8, 3])
    small_d = din("smallp", [2, 128, NSMALL])
    headp_d = din("headp", [2, 8, 4])
    wmod_d = din("w_mod", [2, D, 9 * D])
    wg_d = din("ffn_w_gate", [2, 2, D, DFF])
    wu_d = din("ffn_w_up", [2, 2, D, DFF])
    wd_d = din("ffn_w_down", [2, 2, DFF, D])
    win_d = din("w_in", [2, D, DIN])
    wout_d = din("w_out", [2, D, D])
    lwa_d = din("lru_wa", [2, 4, 64, 64])
    lwx_d = din("lru_wx", [2, 4, 64, 64])
    ckT_d = din("ckT", [2, 2, 8, 64, PAST])
    cv_d = din("cvh", [2, 2, 8, 128, 2048])
    clfT_d = din("clfT", [2, 2, 8, PAST])
    lconv_d = din("lconvT", [2, 128, 2, 2, 3])
    lh_d = din("lhT", [2, 128, 2, 2])
    sconv_d = din("sconvT", [2, 128, 2, 6, 3])
    sh0_d = din("sh0T", [2, 2, 4, 128, 64])
    sh0f_d = din("sh0f", [2, 2, 4, 64, 128])
    yT_o = dout("yT", [128, 8, T])
    kT_o = dout("fk_out", [2, 128, 4, T])
    v_o = dout("fv_out", [2, T, 512])
    lf_o = dout("logfT", [2, 8, T])
    lconv_o = dout("lconv_o", [2, 128, 3, 2, 3])
    lh_o = dout("lh_o", [2, 128, 3, 2])
    sconv_o = dout("sconv_o", [2, 128, 3, 6, 3])
    sh_o = dout("sh_o", [2, 3, 4, 64, 128])
    xspill = nc.dram_tensor("xspill", [128, 8, T], F32).ap()
    dbg_o = dout("dbg", [128, 8, T]) if DBG else None

    st = ExitStack()
    P = Prog(nc)
    sb = SB(nc)

    PS = [nc.alloc_psum_tensor(f"psb{i}", [128, 512], F32) for i in range(8)]
    RPS = [Res(f"ps{i}") for i in range(8)]

    X = sb.alloc("X", [128, 8, T], F32)
    RX = [Res(f"X{t}") for t in range(5)]
    ones_bf = sb.alloc("ones_bf", [128, 128], BF16)
    ident_bf = sb.alloc("ident_bf", [128, 128], BF16)
    ident_f = sb.alloc("ident_f", [128, 128], F32)
    maskneg = sb.alloc("maskneg", [128, 128], BF16)
    eps_t = sb.alloc("eps_t", [128, 1], F32)
    one_t = sb.alloc("one_t", [128, 1], F32)
    csil = sb.alloc("csil", [128, 8, 3], BF16)
    cin = sb.alloc("cin", [128, 8, 3], F32)
    CUR = [0]

    class Sel:
        def __init__(self, items):
            self.items = items

        def __getitem__(self, k):
            return self.items[CUR[0]][k]

    class ResSel:
        def __init__(self, items):
            object.__setattr__(self, "items", items)

        def __getattr__(self, n):
            return getattr(self.items[CUR[0]], n)

        def __setattr__(self, n, v):
            setattr(self.items[CUR[0]], n, v)

    small = Sel([sb.alloc("small", [128, NSMALL], F32) for _ in range(2)])
    headp = Sel([sb.alloc("headp", [8, 4], F32) for _ in range(2)])
    modsb = Sel([sb.alloc("modsb", [128, 72, 3], F32) for _ in range(2)])
    gs_t = Sel([sb.alloc("gs_t", [128, 3, 8, 3], F32) for _ in range(2)])
    gp_t = Sel([sb.alloc("gp_t", [128, 3, 8, 3], F32) for _ in range(2)])
    Rconst = Res("const")
    Rsmall = ResSel([Res("small0"), Res("small1")])
    Rmod_lj = [[Res(f"mod{l}_{j}") for j in range(3)] for l in range(2)]

    def RM(j):
        return Rmod_lj[CUR[0]][j]
    Rcs = Res("csil")
    dum = ones_bf
    ARENA = sb.mark()

    C_NPRE, C_NPOST, C_BMOD = 0, 24, 48
    C_LCW, C_LCB, C_LBA, C_LBX, C_LLAM = 120, 128, 130, 132, 134
    C_SCW, C_SCB, C_SD, C_SNW = 136, 160, 166, 168

    def dma(eng, out, in_, reads=(), writes=(), key=None, slow=False):
        if slow:
            return P.op(eng, lambda e: e.dma_start(out=out, in_=in_, allow_slow_non_contiguous=True),
                        reads=reads, writes=writes, dma=True, key=key)
        return P.op(eng, lambda e: e.dma_start(out=out, in_=in_), reads=reads, writes=writes, dma=True, key=key)

    def mm_group(out, pairs, reads, writes):
        n = len(pairs)

        def fn(e):
            ins = None
            for i, (l, r) in enumerate(pairs):
                ins = e.matmul(out, lhsT=l, rhs=r, start=(i == 0), stop=(i == n - 1))
            return ins
        return P.op("pe", fn, reads=reads, writes=writes)

    def act(out, in_, func, reads, writes, bias=None, scale=None):
        kw = {}
        if bias is not None:
            kw["bias"] = bias
        if scale is not None:
            kw["scale"] = scale
        return P.op("act", lambda e: e.activation(out=out, in_=in_, func=func, **kw), reads=reads, writes=writes)

    def dve(fn, reads, writes):
        return P.op("dve", fn, reads=reads, writes=writes)

    def tt(out, in0, in1, op, reads, writes, eng="dve"):
        return P.op(eng, lambda e: e.tensor_tensor(out=out, in0=in0, in1=in1, op=op), reads=reads, writes=writes)

    def stt(out, in0, scalar, in1, op0, op1, reads, writes):
        return P.op("dve", lambda e: e.scalar_tensor_tensor(out=out, in0=in0, scalar=scalar, in1=in1, op0=op0, op1=op1),
                    reads=reads, writes=writes)

    def tsc(out, in0, s1, s2, op0, op1, reads, writes, eng="dve"):
        if s2 is None:
            return P.op(eng, lambda e: e.tensor_scalar(out=out, in0=in0, scalar1=s1, scalar2=None, op0=op0),
                        reads=reads, writes=writes)
        return P.op(eng, lambda e: e.tensor_scalar(out=out, in0=in0, scalar1=s1, scalar2=s2, op0=op0, op1=op1),
                    reads=reads, writes=writes)

    def recip(out, in_, reads, writes):
        return P.op("dve", lambda e: e.reciprocal(out=out, in_=in_), reads=reads, writes=writes)

    def cp(out, in_, reads, writes, eng="dve"):
        return P.op(eng, lambda e: e.tensor_copy(out=out, in_=in_), reads=reads, writes=writes)

    def scan(out, d0, d1, init, reads, writes):
        return P.op("dve", lambda e: e.tensor_tensor_scan(out=out, data0=d0, data1=d1, initial=init,
                                                          op0=ALU.mult, op1=ALU.add), reads=reads, writes=writes)

    def memset(ap, val, writes, eng="dve"):
        return P.op(eng, lambda e: e.memset(ap, val), writes=writes)

    P.op("dve", lambda e: e.memset(ones_bf[:], 1.0), writes=[Rconst])
    P.op("dve", lambda e: e.memset(eps_t[:], EPS), writes=[Rconst])
    P.op("dve", lambda e: e.memset(one_t[:], 1.0), writes=[Rconst])
    P.op("dve", lambda e: e.memset(ident_f[:], 1.0), writes=[Rconst])
    P.op("pool", lambda e: e.affine_select(out=ident_f[:], in_=ident_f[:], pattern=[[-1, 128]],
                                           compare_op=ALU.is_equal, fill=0.0, base=0, channel_multiplier=1),
         reads=[Rconst], writes=[Rconst])
    P.op("dve", lambda e: e.tensor_copy(out=ident_bf[:], in_=ident_f[:]), reads=[Rconst], writes=[Rconst])
    P.op("dve", lambda e: e.memset(maskneg[:], 0.0), writes=[Rconst])
    P.op("pool", lambda e: e.affine_select(out=maskneg[:], in_=maskneg[:], pattern=[[1, 128]],
                                           compare_op=ALU.is_ge, fill=-30000.0, base=0, channel_multiplier=-1),
         reads=[Rconst], writes=[Rconst])

    NWARM = int(os.environ.get("MK_WARM", "0"))

    def keep_warm(bank, n):
        if n <= 0:
            return

        def fn(e):
            ins = None
            for _ in range(n):
                ins = e.matmul(PS[bank][:, 0:128], lhsT=ident_bf[:], rhs=dum[:], start=True, stop=True)
            return ins
        P.op("pe", fn, reads=[Rconst], writes=[RPS[bank]])

    for t, (t0, n) in enumerate(TILES):
        dma("sp", X[:, :, t0:t0 + n], xT_d[:, :, t0:t0 + n], writes=[RX[t]], key=RX[t])
    dma("sp", cin[:], cT_d, writes=[Rcs], key=Rcs)
    for l_ in range(2):
        dma("sp", small.items[l_][:], small_d[l_], writes=[Rsmall.items[l_]], key=Rsmall.items[l_])
        dma("sp", headp.items[l_][:], headp_d[l_], writes=[Rsmall.items[l_]], key=Rsmall.items[l_])
    act(csil[:], cin[:], AF.Silu, reads=[Rcs], writes=[Rcs])

    def mod_thunks(l, parts, wm, Rwm):
        th = []
        wv = wmod_d[l].rearrange("(kc p) n -> p kc n", p=128)
        psm = PS[7]
        sm, msb, gst, gpt = small.items[l], modsb.items[l], gs_t.items[l], gp_t.items[l]
        Rsm = Rsmall.items[l]
        for j in parts:
            for s_ in range(6 * j, 6 * j + 6):
                def slab(s_=s_):
                    b = s_ % 2
                    dma("pool", wm[b][:], wv[:, :, s_ * 512:(s_ + 1) * 512], writes=[Rwm[b]], key=Rwm[b])
                    for m in range(4):
                        mc = s_ * 4 + m
                        mm_group(psm[:, mc * 3:mc * 3 + 3],
                                 [(wm[b][:, kc, m * 128:(m + 1) * 128], csil[:, kc, :]) for kc in range(8)],
                                 reads=[Rwm[b], Rcs], writes=[RPS[7]])
                th.append(slab)

            def fin(j=j):
                Rm = Rmod_lj[l][j]
                psv = psm[:, 72 * j:72 * j + 72].rearrange("p (m s) -> p m s", s=3)
                w_j = 1.0 if j == 1 else 0.5
                for s_ in range(3):
                    tt(msb[:, 24 * j:24 * j + 24, s_], psv[:, :, s_], sm[:, C_BMOD + 24 * j:C_BMOD + 24 * j + 24], ALU.add,
                       reads=[RPS[7], Rsm], writes=[Rm])
                for s_ in range(3):
                    stt(gst[:, j, :, s_], msb[:, j * 24 + 8:j * 24 + 16, s_], 1.0,
                        sm[:, C_NPRE + j * 8:C_NPRE + j * 8 + 8], ALU.add, ALU.mult, reads=[Rm, Rsm], writes=[Rm])
                    stt(gpt[:, j, :, s_], msb[:, j * 24 + 16:j * 24 + 24, s_], w_j,
                        sm[:, C_NPOST + j * 8:C_NPOST + j * 8 + 8], ALU.mult, ALU.mult, reads=[Rm, Rsm], writes=[Rm])
            th.append(fin)
        return th

    def rms_stats(src_fn, ti, n, sq, Rsq, rs, Rrs, srcres):
        act(sq[:, :, :n], src_fn(), AF.Square, reads=srcres, writes=[Rsq])
        mm_group(PS[6][:, :n], [(ones_bf[:], sq[:, c, :n]) for c in range(8)], reads=[Rsq, Rconst], writes=[RPS[6]])
        act(rs[:, :n], PS[6][:, :n], AF.Sqrt, reads=[RPS[6], Rconst], writes=[Rrs], bias=eps_t[:], scale=1.0 / D)
        recip(rs[:, :n], rs[:, :n], reads=[Rrs], writes=[Rrs])

    def segs_in(t0, n):
        out = []
        for (s0, sl, s) in SEGS:
            a, b = max(s0, t0), min(s0 + sl, t0 + n)
            if a < b:
                out.append((a, b - a, s))
        return out

    def norm_pre_steps(j, tis, xn, Rxn, xoff, scr):
        steps = []
        for k_, ti in enumerate(tis):
            sq, Rsq, rs, Rrs, tmp, Rtmp = scr[k_ % len(scr)] if isinstance(scr, list) else scr
            t0, n = TILES[ti]
            steps.append(lambda ti=ti, t0=t0, n=n, sq=sq, Rsq=Rsq, rs=rs, Rrs=Rrs: rms_stats(
                lambda: X[:, :, t0:t0 + n], ti, n, sq, Rsq, rs, Rrs, [RX[ti]]))
            for c in range(8):
                def st(ti=ti, t0=t0, n=n, c=c, rs=rs, Rrs=Rrs, tmp=tmp, Rtmp=Rtmp):
                    b = c % 2
                    tt(tmp[b][:, :n], X[:, c, t0:t0 + n], rs[:, :n], ALU.mult, reads=[RX[ti], Rrs], writes=[Rtmp[b]])
                    for (a, ln, s) in segs_in(t0, n):
                        act(xn[:, c, a - xoff:a - xoff + ln], tmp[b][:, a - t0:a - t0 + ln], AF.Identity,
                            reads=[Rtmp[b], RM(j)], writes=[Rxn[ti]],
                            bias=modsb[:, j * 24 + c, s:s + 1], scale=gs_t[:, j, c, s:s + 1])
                steps.append(st)
        return steps

    def norm_pre(j, tis, xn, Rxn, xoff, scr):
        for st in norm_pre_steps(j, tis, xn, Rxn, xoff, scr):
            st()

    def post_norm_steps(j, tis, yacc, Ry, yoff, scr):
        sq, Rsq, rs, Rrs, tmp, Rtmp = scr
        steps = []
        for ti in tis:
            t0, n = TILES[ti]
            steps.append(lambda ti=ti, t0=t0, n=n: rms_stats(lambda: yacc[:, :, t0 - yoff:t0 - yoff + n], ti, n, sq, Rsq, rs, Rrs,
                                                             [Ry[ti]]))
            for c in range(8):
                def st(ti=ti, t0=t0, n=n, c=c):
                    b = c % 2
                    tt(tmp[b][:, :n], yacc[:, c, t0 - yoff:t0 - yoff + n], rs[:, :n], ALU.mult,
                       reads=[Ry[ti], Rrs], writes=[Rtmp[b]])
                    for (a, ln, s) in segs_in(t0, n):
                        stt(X[:, c, a:a + ln], tmp[b][:, a - t0:a - t0 + ln], gp_t[:, j, c, s:s + 1],
                            X[:, c, a:a + ln], ALU.mult, ALU.add,
                            reads=[Rtmp[b], RM(j), RX[ti]], writes=[RX[ti]])
                steps.append(st)
        return steps

    def post_norm(j, tis, yacc, Ry, yoff, scr):
        for st in post_norm_steps(j, tis, yacc, Ry, yoff, scr):
            st()

    def skewed(steps, per=9):
        groups = [steps[i:i + per] for i in range(0, len(steps), per)]
        out = []
        for g, grp in enumerate(groups):
            if g == 0:
                out.append(grp[0])
            if g + 1 < len(groups):
                out.append(groups[g + 1][0])
            out.extend(grp[1:])
        return out

    def interleave(main, side):
        nm, ns = len(main), len(side)
        k = 0
        for i, m in enumerate(main):
            m()
            tgt = (i + 1) * ns // max(nm, 1)
            while k < tgt:
                side[k]()
                k += 1
        while k < ns:
            side[k]()
            k += 1

    def alloc_norm_scratch():
        sq = sb.alloc("sq", [128, 8, 512], BF16)
        rs = sb.alloc("rs", [128, 512], F32)
        tmp = [sb.alloc("ntmp", [128, 512], F32) for _ in range(2)]
        return (sq, Res("sq"), rs, Res("rs"), tmp, [Res("ntmp0"), Res("ntmp1")])

    def ffn(l, wi, j):
        m0 = sb.mark()
        halves = [[0, 1], [2, 3, 4]]
        NT = 1088
        scr = alloc_norm_scratch()
        xn = sb.alloc("xn", [128, 8, NT], BF16)
        yacc = sb.alloc("yacc", [128, 8, NT], F32)
        hb = [sb.alloc("hb", [128, 4, NT], BF16) for _ in range(2)]
        wg = [sb.alloc("wg", [128, 8, 512], BF16) for _ in range(2)]
        wu = [sb.alloc("wu", [128, 8, 512], BF16) for _ in range(2)]
        wd = [sb.alloc("wd", [128, 4, 1024], BF16) for _ in range(2)]
        sg = [sb.alloc("sg", [128, 512], BF16) for _ in range(2)]
        Rxs_ = [Res(f"xn_s{p}") for p in range(3)]
        Rys_ = [Res(f"y_s{p}") for p in range(3)]
        Rhs_ = [[Res(f"h{b}_s{p}") for p in range(3)] for b in range(2)]
        Rwgu = [Res("wgu0"), Res("wgu1")]
        Rwd = [Res("wd0"), Res("wd1")]
        Rsg = [Res("sg0"), Res("sg1")]
        wgv = wg_d[l, wi].rearrange("(kc p) n -> p kc n", p=128)
        wuv = wu_d[l, wi].rearrange("(kc p) n -> p kc n", p=128)
        wdv = wd_d[l, wi].rearrange("(jc p) n -> p jc n", p=128)
        mch = [4, 4, 4, 4, 4, 2]
        t_lo = [TILES[h[0]][0] for h in halves]
        Rxn = [{ti: Rxs_[p] for p, ti in enumerate(h)} for h in halves]
        Ry = [{ti: Rys_[p] for p, ti in enumerate(h)} for h in halves]

        def load_gu(v):
            s_, b = v % 6, v % 2
            w = mch[s_] * 128
            dma("pool", wg[b][:, :, 0:w], wgv[:, :, s_ * 512:s_ * 512 + w], writes=[Rwgu[b]], key=Rwgu[b])
            dma("pool", wu[b][:, :, 0:w], wuv[:, :, s_ * 512:s_ * 512 + w], writes=[Rwgu[b]], key=Rwgu[b])

        def load_d(v):
            s_, b = v % 6, v % 2
            dma("pool", wd[b][:, 0:mch[s_], :], wdv[:, s_ * 4:s_ * 4 + mch[s_], :], writes=[Rwd[b]], key=Rwd[b])

        cnt = [0]
        dcnt = [0]

        def gu_units(v):
            hf, s_, b = v // 6, v % 6, v % 2
            units = []
            for p, ti in enumerate(halves[hf]):
                t0, n = TILES[ti]
                o = t0 - t_lo[hf]
                for m in range(mch[s_]):
                    def unit(p=p, n=n, o=o, m=m, b=b):
                        g = cnt[0] % 2
                        cnt[0] += 1
                        pg, pu = PS[g], PS[2 + g]
                        mm_group(pg[:, :n], [(wg[b][:, kc, m * 128:(m + 1) * 128], xn[:, kc, o:o + n]) for kc in range(8)],
                                 reads=[Rwgu[b], Rxs_[p]], writes=[RPS[g]])
                        mm_group(pu[:, :n], [(wu[b][:, kc, m * 128:(m + 1) * 128], xn[:, kc, o:o + n]) for kc in range(8)],
                                 reads=[Rwgu[b], Rxs_[p]], writes=[RPS[2 + g]])
                        act(sg[g][:, :n], pg[:, :n], AF.Silu, reads=[RPS[g]], writes=[Rsg[g]])
                        tt(hb[b][:, m, o:o + n], sg[g][:, :n], pu[:, :n], ALU.mult,
                           reads=[Rsg[g], RPS[2 + g]], writes=[Rhs_[b][p]])
                    units.append(unit)
            return units

        def down_units(v):
            hf, s_, b = v // 6, v % 6, v % 2
            units = []
            for p, ti in enumerate(halves[hf]):
                t0, n = TILES[ti]
                o = t0 - t_lo[hf]
                for oc in range(8):
                    def unit(p=p, n=n, o=o, oc=oc, b=b, s_=s_):
                        g = dcnt[0] % 2
                        dcnt[0] += 1
                        pd = PS[4 + g]
                        mm_group(pd[:, :n], [(wd[b][:, m, oc * 128:(oc + 1) * 128], hb[b][:, m, o:o + n]) for m in range(mch[s_])],
                                 reads=[Rwd[b], Rhs_[b][p]], writes=[RPS[4 + g]])
                        if s_ == 0:
                            act(yacc[:, oc, o:o + n], pd[:, :n], AF.Identity, reads=[RPS[4 + g]], writes=[Rys_[p]])
                        else:
                            tt(yacc[:, oc, o:o + n], yacc[:, oc, o:o + n], pd[:, :n], ALU.add,
                               reads=[RPS[4 + g], Rys_[p]], writes=[Rys_[p]])
                    units.append(unit)
            return units

        def run(lst):
            for u in lst:
                u()

        load_gu(0)
        load_d(0)
        load_gu(1)
        load_d(1)
        norm_pre(j, halves[0], xn, Rxn[0], t_lo[0], scr)
        NV = 12
        for v in range(NV):
            if v == 6:
                interleave(gu_units(v), post_norm_steps(j, halves[0], yacc, Ry[0], t_lo[0], scr))
            else:
                run(gu_units(v))
            if v + 2 < NV:
                load_gu(v + 2)
            if v == 5:
                pre_b = norm_pre_steps(j, halves[1], xn, Rxn[1], t_lo[1], scr)
                h1 = len(pre_b) // 2
                interleave(down_units(4), pre_b[:h1])
                load_d(6)
                interleave(down_units(5), pre_b[h1:])
                load_d(7)
            elif v >= 1 and v != 6:
                run(down_units(v - 1))
                if v + 1 < NV:
                    load_d(v + 1)
        run(down_units(NV - 1))
        post_norm(j, halves[1], yacc, Ry[1], t_lo[1], scr)
        P.barrier()
        sb.release(m0)

    def fox(l, ymix, Rym, xn, Rxn, winv):
        ms, mx = sb.mark(), sbx.mark()
        NB = 18
        blocks = [(tb * 128, 128) for tb in range(16)] + [(2048, 32), (2080, 32)]
        vaug = sb.alloc("vaug", [128, NB, 8, 66], BF16)
        Rva = Res("vaug")
        memset(vaug[:, :, :, 64:65], 1.0, writes=[Rva])
        ones_r = sb.alloc("ones_r", [128, 64], F32)
        memset(ones_r[:], 1.0, writes=[Rva])
        qs = sb.alloc("qs", [70, 8, 64], BF16)
        ks = sb.alloc("ks", [70, 8, 64], BF16)
        Rqs = Res("qs")
        fend = sb.alloc("fend", [8, 2], F32)
        nfb = sb.alloc("nfb", [8, 1], F32)
        Rfend = Res("fend")
        m_f = sb.mark()
        rch = sb.alloc("rch", [1, 1024], BF16)
        LG = sb.alloc("LG", [8, T], F32)
        FT = sb.alloc("FT", [128, T], F32)
        SP1 = sb.alloc("SP1", [128, T], BF16)
        SP2 = sb.alloc("SP2", [128, T], BF16)
        CLF = sb.alloc("CLF", [8, 2050], F32)
        RLG, RFT, RFR, RSP1, RSP2 = Res("LG"), Res("FT"), Res("FR"), Res("SP1"), Res("SP2")
        RCLF = Res("CLF")
        qa = [sbx.alloc("qa", [70, T], BF16) for _ in range(4)]
        ka = [sbx.alloc("ka", [70, T], BF16) for _ in range(4)]
        Rqa = [Res(f"qa{i}") for i in range(4)]
        Rka = [Res(f"ka{i}") for i in range(4)]
        wq = sbx.alloc("wq", [128, 8, 256], BF16)
        wk = sbx.alloc("wk", [128, 8, 256], BF16)
        wv = sbx.alloc("wv", [128, 8, 512], BF16)
        wf = sbx.alloc("wf", [128, 8, 8], BF16)
        Rwq, Rwk, Rwv, Rwf = Res("wq"), Res("wk"), Res("wv"), Res("wf")
        kst = [sbx.alloc("kst", [128, 512], F32) for _ in range(2)]
        vst = [sbx.alloc("vst", [128, 512], F32) for _ in range(2)]
        Rkst = [Res("kst0"), Res("kst1")]
        Rvst = [Res("vst0"), Res("vst1")]
        pbuf = [sbx.alloc("pbuf", [128, 512], BF16) for _ in range(4)]
        Rpb = [Res(f"pb{i}") for i in range(4)]
        rcb = sbx.alloc("rcb", [128, 512], F32)
        bcs = sbx.alloc("bcs", [64, 512], F32)
        Rrcb, Rbcs, Rrch = Res("rcb"), Res("bcs"), Res("rch")

        if FOXS <= -2:
            P.barrier()
            sb.release(ms)
            sbx.release(mx)
            return
        dma("pool", wf[:], winv[:, :, 2048:2056], writes=[Rwf], key=Rwf)
        tsc(nfb[:], headp[0:8, 0:1], -1.0, None, ALU.mult, None, reads=[Rsmall], writes=[Rfend])
        for ti, (t0, n) in enumerate(TILES):
            mm_group(PS[3][0:8, :n], [(wf[:, kc, :], xn[:, kc, t0:t0 + n]) for kc in range(8)],
                     reads=[Rwf, Rxn[ti]], writes=[RPS[3]])
            act(LG[0:8, t0:t0 + n], PS[3][0:8, :n], AF.Exp, reads=[RPS[3], Rfend], writes=[RLG], bias=nfb[:], scale=-1.0)
        act(LG[:], LG[:], AF.Ln, reads=[RLG, Rconst], writes=[RLG], bias=one_t[0:8])
        tsc(LG[:], LG[:], -1.0, None, ALU.mult, None, reads=[RLG], writes=[RLG])
        dma("sp", lf_o[l], LG[:], reads=[RLG], key=RLG)
        ones8 = lambda n: one_t[0:8, 0:1].to_broadcast([8, n])
        scan(FT[0:8, 0:NPR], ones8(NPR), LG[0:8, 0:NPR], 0.0, reads=[RLG, Rconst], writes=[RFT])
        for si in range(2):
            for hf in range(2):
                dma("sp", CLF[:, 0:2048], clfT_d[l, si, :, hf * 2048:(hf + 1) * 2048], writes=[RCLF], key=RCLF)
                P.op("dve", (lambda hf=hf: (lambda e: e.tensor_reduce(
                    out=CLF[:, 2048 + hf:2049 + hf], in_=CLF[:, 0:2048], axis=mybir.AxisListType.X, op=ALU.add)))(),
                    reads=[RCLF], writes=[RCLF])
            tt(fend[:, si:si + 1], CLF[:, 2048:2049], CLF[:, 2049:2050], ALU.add, reads=[RCLF], writes=[Rfend])
            scan(FT[0:8, NPR + si * 32:NPR + si * 32 + 32], ones8(32), LG[0:8, NPR + si * 32:NPR + si * 32 + 32],
                 fend[:, si:si + 1], reads=[RLG, Rconst, Rfend], writes=[RFT])
        split3(FT[0:8, :], FT[32:40, :], FT[64:72, :], SP1[0:8, :], SP1[32:40, :], SP1[64:72, :], RFT, RFR, RSP1)
        for q_ in (0, 32, 64):
            tsc(SP2[q_:q_ + 8, :], SP1[q_:q_ + 8, :], -1.0, None, ALU.mult, None, reads=[RSP1], writes=[RSP2])

        if FOXS <= -1:
            P.barrier()
            sb.release(ms)
            sbx.release(mx)
            return
        scnt = [0]
        ocnt = [0]
        deferred = []
        for hg in range(1 if "f" in TOG else 2):
            dma("pool", wq[:], winv[:, :, 512 + hg * 256:512 + hg * 256 + 256], writes=[Rwq], key=Rwq)
            dma("pool", wk[:], winv[:, :, 1024 + hg * 256:1024 + hg * 256 + 256], writes=[Rwk], key=Rwk)
            if hg == 0 and "e" not in TOG:
                dma("pool", wv[:], winv[:, :, 1536:2048], writes=[Rwv], key=Rwv)
            for hl in range(0 if "a" in TOG else 4):
                memset(qa[hl][64:70, :], 1.0, writes=[Rqa[hl]])
                memset(ka[hl][64:70, :], 1.0, writes=[Rka[hl]])
            for m in range(0 if "d" in TOG else 2):
                for ti, (t0, n) in enumerate(TILES):
                    iq, ik = ti % 2, 2 + ti % 2
                    psq, psk = PS[iq], PS[ik]
                    b = ti % 2
                    mm_group(psq[:, :n], [(wq[:, kc, m * 128:(m + 1) * 128], xn[:, kc, t0:t0 + n]) for kc in range(8)],
                             reads=[Rwq, Rxn[ti]], writes=[RPS[iq]])
                    act(qa[2 * m][0:64, t0:t0 + n], psq[0:64, :n], AF.Identity, reads=[RPS[iq]], writes=[Rqa[2 * m]], scale=0.125)
                    tsc(qa[2 * m + 1][0:64, t0:t0 + n], psq[64:128, :n], 0.125, None, ALU.mult, None,
                        reads=[RPS[iq]], writes=[Rqa[2 * m + 1]])
                    mm_group(psk[:, :n], [(wk[:, kc, m * 128:(m + 1) * 128], xn[:, kc, t0:t0 + n]) for kc in range(8)],
                             reads=[Rwk, Rxn[ti]], writes=[RPS[ik]])
                    act(ka[2 * m][0:64, t0:t0 + n], psk[0:64, :n], AF.Identity, reads=[RPS[ik]], writes=[Rka[2 * m]])
                    cp(ka[2 * m + 1][0:64, t0:t0 + n], psk[64:128, :n], reads=[RPS[ik]], writes=[Rka[2 * m + 1]])
                    if "c" not in TOG:
                        cp(kst[b][:, :n], psk[:, :n], reads=[RPS[ik]], writes=[Rkst[b]])
                        if "g" in TOG:
                            dma("sp", dbg_o[:, hg * 2 + m, t0:t0 + n], kst[b][:, :n], reads=[Rkst[b]], key=Rkst[b])
                        else:
                            dma("sp", kT_o[l, :, hg * 2 + m, t0:t0 + n], kst[b][:, :n], reads=[Rkst[b]], key=Rkst[b])
            if hg == 0 and FOXS >= 0 and "b" not in TOG:
                for tb, (k0, nb) in enumerate(blocks):
                    b = tb % 2
                    iv = 6 + tb % 2
                    psv = PS[iv]
                    mm_group(psv[0:nb, :], [(xn[:, kc, k0:k0 + nb], wv[:, kc, :]) for kc in range(8)],
                             reads=[Rwv] + [Rxn[i] for i in range(5)], writes=[RPS[iv]])
                    if "i" not in TOG:
                        cp(vaug[0:nb, tb, :, 0:64], psv[0:nb, :].rearrange("p (h d) -> p h d", d=64),
                           reads=[RPS[iv]], writes=[Rva])
                    if "h" not in TOG:
                        cp(vst[b][0:nb, :], psv[0:nb, :], reads=[RPS[iv]], writes=[Rvst[b]])
                        dma("sp", v_o[l, k0:k0 + nb, :], vst[b][0:nb, :], reads=[Rvst[b]], key=Rvst[b])
            for hl in range(4 if FOXS >= 1 else 0):
                h = hg * 4 + hl
                dma("sp", qa[hl][64:67, :], SP1[h:h + 65:32, :], reads=[RSP1], writes=[Rqa[hl]], key=Rqa[hl])
                dma("sp", ka[hl][67:70, :], SP2[h:h + 65:32, :], reads=[RSP2], writes=[Rka[hl]], key=Rka[hl])
            for hl in range(4 if FOXS >= 2 else 0):
                h = hg * 4 + hl
                for Q in range(4):
                    og = 4 + (ocnt[0] % 2)
                    ocnt[0] += 1
                    po = PS[og]
                    nkb = 4 * Q + 4
                    LOOK = 2
                    pend = []
                    for kb in range(nkb + LOOK):
                        if kb < nkb:
                            d = kb - 4 * Q
                            col0 = max(d, 0) * 128
                            g = scnt[0] % 4
                            scnt[0] += 1
                            pS = PS[g]

                            def fnS(e, pS=pS, col0=col0, d=d, hl=hl, kb=kb, Q=Q):
                                ins = e.matmul(pS[:, col0:512], lhsT=ka[hl][0:70, kb * 128:(kb + 1) * 128],
                                               rhs=qa[hl][0:70, Q * 512 + col0:Q * 512 + 512], start=True, stop=(d < 0))
                                if d >= 0:
                                    ins = e.matmul(pS[:, col0:col0 + 128], lhsT=ident_bf[:], rhs=maskneg[:], start=False, stop=True)
                                return ins
                            P.op("pe", fnS, reads=[Rka[hl], Rqa[hl], Rconst], writes=[RPS[g]])
                            act(pbuf[g][:, col0:512], pS[:, col0:512], AF.Exp, reads=[RPS[g]], writes=[Rpb[g]])
                            pend.append((kb, g, col0))
                            keep_warm(7, NWARM)
                            if kb == 1:
                                while deferred:
                                    deferred.pop(0)()
                        if kb >= LOOK:
                            kb_, g_, c0_ = pend.pop(0)

                            def fnV(e, kb_=kb_, g_=g_, c0_=c0_, po=po, h=h, nkb=nkb):
                                return e.matmul(po[0:65, c0_:512], lhsT=vaug[:, kb_, h, 0:65], rhs=pbuf[g_][:, c0_:512],
                                                start=(kb_ == 0), stop=(kb_ == nkb - 1))
                            P.op("pe", fnV, reads=[Rva, Rpb[g_]], writes=[RPS[og]])
                    recip(rcb[0:1, :], po[64:65, :], reads=[RPS[og]], writes=[Rrcb])
                    cp(rch[0:1, 0:512], rcb[0:1, :], reads=[Rrcb], writes=[Rrch])
                    tt(rcb[0:1, :], rcb[0:1, :], rch[0:1, 0:512], ALU.subtract, reads=[Rrcb, Rrch], writes=[Rrcb])
                    cp(rch[0:1, 512:1024], rcb[0:1, :], reads=[Rrcb], writes=[Rrch])

                    def epilogue(po=po, og=og, h=h, Q=Q):
                        mm_group(PS[6][0:64, :], [(ones_bf[0:1, 0:64], rch[0:1, 0:512]), (ones_bf[0:1, 0:64], rch[0:1, 512:1024])],
                                 reads=[Rrch, Rconst], writes=[RPS[6]])
                        act(bcs[:, :], PS[6][0:64, :], AF.Identity, reads=[RPS[6]], writes=[Rbcs])
                        tt(ymix[(h % 2) * 64:(h % 2) * 64 + 64, 2 + h // 2, Q * 512:Q * 512 + 512], po[0:64, :], bcs[:, :],
                           ALU.mult, reads=[RPS[og], Rbcs], writes=[Rym[2 + h // 2]])
                    deferred.append(epilogue)
            while deferred:
                deferred.pop(0)()
            for hl in range(4 if FOXS >= 1 else 0):
                h = hg * 4 + hl
                cp(qs[0:70, h, :], qa[hl][0:70, NPR:T], reads=[Rqa[hl]], writes=[Rqs])
                cp(ks[0:70, h, :], ka[hl][0:70, NPR:T], reads=[Rka[hl]], writes=[Rqs])
        P.barrier()
        sb.release(m_f)
        if FOXS < 3:
            sb.release(ms)
            sbx.release(mx)
            return
        FC = sb.alloc("FC", [128, PAST], F32)
        SPC = sb.alloc("SPC", [128, PAST], BF16)
        RFC, RFCR, RSPC = Res("FC"), Res("FCR"), Res("SPC")
        kcb = [sb.alloc("kcb", [70, PAST], BF16) for _ in range(2)]
        vcb = [sb.alloc("vcb", [128, 2048], BF16) for _ in range(2)]
        rsum = sb.alloc("rsum", [1, 64], F32)
        Rrsum = Res("rsum")
        Rkc = [Res("kc0"), Res("kc1")]
        Rkcr = [Res("kcr0"), Res("kcr1")]
        Rvc = [Res("vc0"), Res("vc1")]
        for b in range(2):
            memset(kcb[b][64:70, :], 1.0, writes=[Rkc[b]])
        for si in range(2):
            dma("sp", FC[64:72, :], clfT_d[l, si], writes=[RFCR], key=RFCR)
            scan(FC[0:8, :], one_t[64:72, 0:1].to_broadcast([8, PAST]), FC[64:72, :], 0.0, reads=[RFCR, Rconst], writes=[RFC])
            split3(FC[0:8, :], FC[32:40, :], FC[64:72, :], SPC[0:8, :], SPC[32:40, :], SPC[64:72, :], RFC, RFCR, RSPC)
            for q_ in (0, 32, 64):
                tsc(SPC[q_:q_ + 8, :], SPC[q_:q_ + 8, :], -1.0, None, ALU.mult, None, reads=[RSPC], writes=[RSPC])
            for h in range(8):
                b = h % 2
                dma("pool", kcb[b][0:64, :], ckT_d[l, si, h], writes=[Rkc[b]], key=Rkc[b])
                dma("sp", kcb[b][67:70, :], SPC[h:h + 65:32, :], reads=[RSPC], writes=[Rkc[b]], key=Rkcr[b])
                dma("pool", vcb[b][:, :], cv_d[l, si, h], writes=[Rvc[b]], key=Rvc[b])
                qcol = qs[0:70, h, si * 32:si * 32 + 32]
                for hf in range(2):
                    pS = PS[hf]

                    def fnC(e, pS=pS, hf=hf, b=b, qcol=qcol):
                        ins = None
                        for bb in range(16):
                            kb = hf * 16 + bb
                            ins = e.matmul(pS[:, bb * 32:bb * 32 + 32], lhsT=kcb[b][0:70, kb * 128:(kb + 1) * 128], rhs=qcol,
                                           start=True, stop=True)
                        return ins
                    P.op("pe", fnC, reads=[Rkc[b], Rqs], writes=[RPS[hf]])
                    act(pbuf[hf][:, :], pS[:, :], AF.Exp, reads=[RPS[hf]], writes=[Rpb[hf]])
                kcol = ks[0:70, h, si * 32:si * 32 + 32]

                def fnN(e, kcol=kcol, qcol=qcol):
                    e.matmul(PS[2][0:32, 0:32], lhsT=kcol, rhs=qcol, start=True, stop=False)
                    return e.matmul(PS[2][0:32, 0:32], lhsT=ident_bf[0:32, 0:32], rhs=maskneg[0:32, 0:32], start=False, stop=True)
                P.op("pe", fnN, reads=[Rqs, Rconst], writes=[RPS[2]])
                act(pbuf[2][0:32, 0:32], PS[2][0:32, 0:32], AF.Exp, reads=[RPS[2]], writes=[Rpb[2]])
                og = 4 + h % 2
                po = PS[og]
                vc4 = vcb[b]

                def fnPV(e, po=po, vc4=vc4, h=h, si=si):
                    for kb in range(32):
                        e.matmul(po[0:64, 0:32], lhsT=vc4[:, kb * 64:(kb + 1) * 64], rhs=pbuf[kb // 16][:, (kb % 16) * 32:(kb % 16) * 32 + 32],
                                 start=(kb == 0), stop=False)
                    return e.matmul(po[0:64, 0:32], lhsT=vaug[0:32, 16 + si, h, 0:64], rhs=pbuf[2][0:32, 0:32], start=False, stop=True)
                P.op("pe", fnPV, reads=[Rvc[b], Rva, Rpb[0], Rpb[1], Rpb[2]], writes=[RPS[og]])

                def fnSum(e):
                    e.matmul(PS[3][0:1, :], lhsT=ones_bf[:, 0:1], rhs=pbuf[0][:, :], start=True, stop=False)
                    e.matmul(PS[3][0:1, :], lhsT=ones_bf[:, 0:1], rhs=pbuf[1][:, :], start=False, stop=True)
                    return e.matmul(PS[7][0:1, 0:32], lhsT=ones_bf[0:32, 0:1], rhs=pbuf[2][0:32, 0:32], start=True, stop=True)
                P.op("pe", fnSum, reads=[Rconst, Rpb[0], Rpb[1], Rpb[2]], writes=[RPS[3], RPS[7]])
                P.op("dve", lambda e: e.tensor_reduce(out=rsum[0:1, 0:32], in_=PS[3][0:1, :].rearrange("p (b q) -> p q b", q=32),
                                                      axis=mybir.AxisListType.X, op=ALU.add), reads=[RPS[3]], writes=[Rrsum])
                tt(rsum[0:1, 32:64], rsum[0:1, 0:32], PS[7][0:1, 0:32], ALU.add, reads=[Rrsum, RPS[7]], writes=[Rrsum])
                recip(rcb[0:1, 0:32], rsum[0:1, 32:64], reads=[Rrsum], writes=[Rrcb])
                mm_group(PS[6][0:64, 0:32], [(ones_r[0:1, 0:64], rcb[0:1, 0:32])], reads=[Rrcb, Rva], writes=[RPS[6]])
                act(bcs[:, 0:32], PS[6][0:64, 0:32], AF.Identity, reads=[RPS[6]], writes=[Rbcs])
                tt(ymix[(h % 2) * 64:(h % 2) * 64 + 64, 2 + h // 2, NPR + si * 32:NPR + si * 32 + 32], po[0:64, 0:32], bcs[:, 0:32],
                   ALU.mult, reads=[RPS[og], Rbcs], writes=[Rym[2 + h // 2]])
        P.barrier()
        sb.release(ms)
        sbx.release(mx)

    def ssd(l, ymix, Rym, xn, Rxn, winv):
        ms, mx = sb.mark(), sbx.mark()
        NB = 18
        blocks = [(tb * 128, 128) for tb in range(16)] + [(2048, 32), (2080, 32)]
        seq_blocks = {0: list(range(16)), 1: [16], 2: [17]}
        xs = sb.alloc("xs", [128, 2, T], F32)
        BT = sb.alloc("BT", [128, 2, T], BF16)
        CT = sb.alloc("CT", [128, 2, T], BF16)
        DT = sb.alloc("DT", [128, T], F32)
        ET = sb.alloc("ET", [128, T], F32)
        S1 = sb.alloc("S1", [128, T], BF16)
        dctok = sb.alloc("dctok", [128, NB, 12], F32)
        at = sb.alloc("at", [4, 2], F32)
        ecd = sb.alloc("ecd", [4, 2, 4], F32)
        ecb = sb.alloc("ecb", [64, 2, 4], F32)
        Rxs = [Res("xs0"), Res("xs1")]
        RBT, RCT, RDT, RET, RS1, Rdc, Rat = [Res(n) for n in "BT CT DT ET S1 dctok at".split()]
        mconv = sbx.mark()
        wx1 = sbx.alloc("wx1", [128, 8, 512], BF16)
        wx2 = sbx.alloc("wx2", [128, 8, 260], BF16)
        Rwx1, Rwx2 = Res("wx1"), Res("wx2")
        dma("pool", wx1[:], winv[:, :, 2312:2824], writes=[Rwx1], key=Rwx1)
        dma("pool", wx2[:], winv[:, :, 2824:3084], writes=[Rwx2], key=Rwx2)
        xp6 = [sbx.alloc("xp6", [128, XPW], F32) for _ in range(6)]
        mu6 = sb.mark()
        u6 = [sb.alloc("u6", [128, T], F32)] * 2
        Rxp6 = [Res(f"xp6_{i}") for i in range(6)]
        Ru6 = [Res("u6a")] * 2
        for ci in range(6):
            xp = xp6[ci]
            memset(xp[:, 0:3], 0.0, writes=[Rxp6[ci]])
            dma("sp", xp[:, 2051:2054], sconv_d[l, :, 0, ci, :], writes=[Rxp6[ci]], key=Rxp6[ci])
            dma("sp", xp[:, 2086:2089], sconv_d[l, :, 1, ci, :], writes=[Rxp6[ci]], key=Rxp6[ci])
            wsl, Rw, col = (wx1, Rwx1, ci * 128) if ci < 4 else (wx2, Rwx2, (ci - 4) * 128)
            for ti, (t0, n) in enumerate(TILES):
                g = (ci * 5 + ti) % 4
                mm_group(PS[g][:, :n], [(wsl[:, kc, col:col + 128], xn[:, kc, t0:t0 + n]) for kc in range(8)],
                         reads=[Rw, Rxn[ti]], writes=[RPS[g]])
                if ti < 4:
                    if ti % 2 == 0:
                        act(xp[:, 3 + t0:3 + t0 + n], PS[g][:, :n], AF.Identity, reads=[RPS[g]], writes=[Rxp6[ci]])
                    else:
                        cp(xp[:, 3 + t0:3 + t0 + n], PS[g][:, :n], reads=[RPS[g]], writes=[Rxp6[ci]])
                else:
                    act(xp[:, 2054:2086], PS[g][:, 0:32], AF.Identity, reads=[RPS[g]], writes=[Rxp6[ci]])
                    cp(xp[:, 2089:2121], PS[g][:, 32:64], reads=[RPS[g]], writes=[Rxp6[ci]])
        for ci in range(6):
            b = ci % 2
            xp, uu = xp6[ci], u6[b]
            cw = lambda k: small[:, C_SCW + ci * 4 + k:C_SCW + ci * 4 + k + 1]
            cb = small[:, C_SCB + ci:C_SCB + ci + 1]
            for (s0, sl, s) in SEGS:
                p0 = POFF[s]
                tsc(uu[:, s0:s0 + sl], xp[:, p0:p0 + sl], cw(0), cb, ALU.mult, ALU.add, reads=[Rxp6[ci], Rsmall], writes=[Ru6[b]])
                for k in range(1, 4):
                    stt(uu[:, s0:s0 + sl], xp[:, p0 + k:p0 + k + sl], cw(k), uu[:, s0:s0 + sl], ALU.mult, ALU.add,
                        reads=[Rxp6[ci], Rsmall, Ru6[b]], writes=[Ru6[b]])
                dma("sp", sconv_o[l, :, s, ci, :], xp[:, p0 + sl:p0 + sl + 3], reads=[Rxp6[ci]], key=Rxp6[ci])
            if ci < 2:
                act(xs[:, ci, :], uu[:], AF.Silu, reads=[Ru6[b]], writes=[Rxs[ci]])
            elif ci < 4:
                act(BT[:, ci - 2, :], uu[:], AF.Silu, reads=[Ru6[b]], writes=[RBT])
            else:
                act(CT[:, ci - 4, :], uu[:], AF.Silu, reads=[Ru6[b]], writes=[RCT])
        for ti, (t0, n) in enumerate(TILES):
            mm_group(PS[2][0:4, :n], [(wx2[:, kc, 256:260], xn[:, kc, t0:t0 + n]) for kc in range(8)],
                     reads=[Rwx2, Rxn[ti]], writes=[RPS[2]])
            act(DT[0:4, t0:t0 + n], PS[2][0:4, :n], AF.Exp, reads=[RPS[2], Rsmall], writes=[RDT], bias=headp[0:4, 1:2])
        act(DT[0:4, :], DT[0:4, :], AF.Ln, reads=[RDT, Rconst], writes=[RDT], bias=one_t[0:4])
        act(at[:, 0:1], headp[0:4, 2:3], AF.Exp, reads=[Rsmall], writes=[Rat])
        tsc(at[:, 1:2], at[:, 0:1], -1.0, None, ALU.mult, None, reads=[Rat], writes=[Rat])
        tsc(DT[32:36, :], DT[0:4, :], at[:, 1:2], None, ALU.mult, None, reads=[RDT, Rat], writes=[RDT])
        for (s0, sl, s) in SEGS:
            scan(DT[64:68, s0:s0 + sl], one_t[32:36, 0:1].to_broadcast([4, sl]), DT[32:36, s0:s0 + sl], 0.0,
                 reads=[RDT, Rconst], writes=[RDT])
            act(ET[64:68, s0:s0 + sl], DT[64:68, s0:s0 + sl], AF.Exp, reads=[RDT], writes=[RET],
                bias=DT[64:68, s0 + sl - 1:s0 + sl], scale=-1.0)
        split3(DT[64:68, :], ET[0:4, :], ET[32:36, :], S1[64:68, :], S1[0:4, :], S1[32:36, :], RDT, RET, RS1)
        for si in range(2):
            last = NPR + si * 32 + 31
            act(at[:, 0:1], DT[64:68, last:last + 1], AF.Exp, reads=[RDT, Rat], writes=[Rat])
            tsc(ecd[:, si, :], ident_f[0:4, 0:4], at[:, 0:1], None, ALU.mult, None, reads=[Rat, Rconst], writes=[Rat])
            mm_group(PS[3][0:64, si * 4:si * 4 + 4], [(ones_f4[0:4, 0:64], ecd[:, si, :])], reads=[Rat, Rconst], writes=[RPS[3]])
        cp(ecb[:].rearrange("p a b -> p (a b)"), PS[3][0:64, 0:8], reads=[RPS[3]], writes=[Rat])
        P.barrier()
        sbx.release(mconv)
        sb.release(mu6)
        xtok = sbx.alloc("xtok", [128, NB, 256], BF16)
        xwtok = sbx.alloc("xwtok", [128, NB, 256], BF16)
        Btok = sbx.alloc("Btok", [128, NB, 256], BF16)
        Rxt, Rxw, RBt = Res("xtok"), Res("xwtok"), Res("Btok")
        aq = [sbx.alloc("aq", [6, T], BF16) for _ in range(4)]
        ak = [sbx.alloc("ak", [6, T], BF16) for _ in range(4)]
        Raq = [Res(f"aq{h}") for h in range(4)]
        Rak = [Res(f"ak{h}") for h in range(4)]
        Lb = [sbx.alloc("Lb", [128, 512], BF16) for _ in range(3)]
        Gb = [sb.alloc("Gb", [128, 512], BF16) for _ in range(8)]
        RL = [Res(f"L{i}") for i in range(3)]
        RG = [Res(f"G{i}") for i in range(8)]
        for h in range(4):
            memset(aq[h][:], 1.0, writes=[Raq[h]])
            memset(ak[h][:], 1.0, writes=[Rak[h]])
            dma("sp", aq[h][3:6, :], S1[h:h + 65:32, :], reads=[RS1], writes=[Raq[h]], key=Raq[h])
            dma("sp", ak[h][0:3, :], S1[h:h + 65:32, :], reads=[RS1], writes=[Rak[h]], key=Rak[h])
            tsc(ak[h][0:3, :], ak[h][0:3, :], -1.0, None, ALU.mult, None, reads=[Rak[h]], writes=[Rak[h]])
        PSb = [PS[i][:].bitcast(BF16) for i in range(8)]
        for tb, (k0, nb) in enumerate(blocks):
            g = tb % 2
            P.op("pe", (lambda k0=k0, nb=nb, g=g: (lambda e: e.transpose(out=PS[g][0:nb, 0:68], in_=DT[0:68, k0:k0 + nb],
                                                                       identity=ident_f[0:68, 0:68])))(),
                 reads=[RDT, Rconst], writes=[RPS[g]])
            cp(dctok[0:nb, tb, 0:4], PS[g][0:nb, 0:4], reads=[RPS[g]], writes=[Rdc])
            P.op("pe", (lambda k0=k0, nb=nb, g=g: (lambda e: e.transpose(out=PS[g][0:nb, 128:196], in_=ET[0:68, k0:k0 + nb],
                                                                       identity=ident_f[0:68, 0:68])))(),
                 reads=[RET, Rconst], writes=[RPS[g]])
            cp(dctok[0:nb, tb, 4:8], PS[g][0:nb, 192:196], reads=[RPS[g]], writes=[Rdc])
            tt(dctok[0:nb, tb, 8:12], dctok[0:nb, tb, 0:4], dctok[0:nb, tb, 4:8], ALU.mult, reads=[Rdc], writes=[Rdc])
            for c in range(2):
                g2 = 2 + c
                P.op("pe", (lambda k0=k0, nb=nb, c=c, g2=g2: (lambda e: e.transpose(
                    out=PS[g2][0:nb, 0:128], in_=xs[:, c, k0:k0 + nb], identity=ident_f[:])))(),
                    reads=[Rxs[c], Rconst], writes=[RPS[g2]])
                act(xtok[0:nb, tb, c * 128:(c + 1) * 128], PS[g2][0:nb, 0:128], AF.Identity, reads=[RPS[g2]], writes=[Rxt])
                for hh_ in range(2):
                    h = 2 * c + hh_
                    tsc(xwtok[0:nb, tb, h * 64:(h + 1) * 64], PS[g2][0:nb, hh_ * 64:(hh_ + 1) * 64], dctok[0:nb, tb, 8 + h:9 + h],
                        None, ALU.mult, None, reads=[RPS[g2], Rdc], writes=[Rxw])
            for gI in range(2):
                g3 = 4 + gI
                P.op("pe", (lambda k0=k0, nb=nb, gI=gI, g3=g3: (lambda e: e.transpose(
                    out=PSb[g3][0:nb, 0:128], in_=BT[:, gI, k0:k0 + nb], identity=ident_bf[:])))(),
                    reads=[RBT, Rconst], writes=[RPS[g3]])
                cp(Btok[0:nb, tb, gI * 128:(gI + 1) * 128], PSb[g3][0:nb, 0:128], reads=[RPS[g3]], writes=[RBt])
        cnt = [0]
        for Q in range(4):
            nsb = 4 * Q + 4
            pend = []
            for sbk in range(nsb + 1):
                cur = []
                if sbk < nsb:
                    d = sbk - 4 * Q
                    col0 = max(d, 0) * 128
                    for gI in range(2):
                        pcb = PS[gI]
                        mm_group(pcb[:, col0:512], [(BT[:, gI, sbk * 128:(sbk + 1) * 128], CT[:, gI, Q * 512 + col0:Q * 512 + 512])],
                                 reads=[RBT, RCT], writes=[RPS[gI]])
                        for hh_ in range(2):
                            h = 2 * gI + hh_
                            pe_ = PS[2 + hh_]
                            i3 = cnt[0] % 3
                            i8 = cnt[0] % 8
                            cnt[0] += 1

                            def fnE(e, pe_=pe_, col0=col0, d=d, h=h, sbk=sbk, Q=Q):
                                ins = e.matmul(pe_[:, col0:512], lhsT=ak[h][0:6, sbk * 128:(sbk + 1) * 128],
                                               rhs=aq[h][0:6, Q * 512 + col0:Q * 512 + 512], start=True, stop=(d < 0))
                                if d >= 0:
                                    ins = e.matmul(pe_[:, col0:col0 + 128], lhsT=ident_bf[:], rhs=maskneg[:], start=False, stop=True)
                                return ins
                            P.op("pe", fnE, reads=[Rak[h], Raq[h], Rconst], writes=[RPS[2 + hh_]])
                            act(Lb[i3][:, col0:512], pe_[:, col0:512], AF.Exp, reads=[RPS[2 + hh_]], writes=[RL[i3]])
                            stt(Gb[i8][:, col0:512], pcb[:, col0:512], dctok[:, sbk, h:h + 1], Lb[i3][:, col0:512], ALU.mult, ALU.mult,
                                reads=[RPS[gI], Rdc, RL[i3]], writes=[RG[i8]])
                            cur.append((sbk, h, i8, col0))
                        keep_warm(7 if gI == 0 else 6, NWARM)
                for (sb_, h_, i3_, c0_) in pend:
                    og = 4 + h_ // 2
                    r0 = (h_ % 2) * 64

                    def fnV(e, sb_=sb_, h_=h_, i3_=i3_, c0_=c0_, og=og, r0=r0, nsb=nsb):
                        return e.matmul(PS[og][r0:r0 + 64, c0_:512], lhsT=xtok[:, sb_, h_ * 64:(h_ + 1) * 64], rhs=Gb[i3_][:, c0_:512],
                                        start=(sb_ == 0), stop=(sb_ == nsb - 1))
                    P.op("pe", fnV, reads=[Rxt, RG[i3_]], writes=[RPS[og]])
                pend = cur
            for c in range(2):
                stt(xs[:, c, Q * 512:(Q + 1) * 512], xs[:, c, Q * 512:(Q + 1) * 512], small[:, C_SD + c:C_SD + c + 1], PS[4 + c][:, :],
                    ALU.mult, ALU.add, reads=[Rxs[c], Rsmall, RPS[4 + c]], writes=[Rxs[c]])
        h0T = sb.alloc("h0T", [128, 2, 4, 64], BF16)
        h0f = sb.alloc("h0f", [64, 2, 4, 128], F32)
        hst = [sb.alloc("hst", [64, 128], F32) for _ in range(2)]
        Rh0T, Rh0f = Res("h0T"), Res("h0f")
        Rhst = [Res("hst0"), Res("hst1")]
        dma("pool", h0T[:], sh0_d[l].rearrange("s h n p -> n s h p"), writes=[Rh0T], key=Rh0T)
        dma("sp", h0f[:], sh0f_d[l].rearrange("s h p n -> p s h n"), writes=[Rh0f], key=Rh0f)
        for si in range(2):
            tk = slice(NPR + si * 32, NPR + si * 32 + 32)
            tb = 16 + si
            for h in range(4):
                gI = h // 2
                r0 = (h % 2) * 64
                og = 4 + gI
                mm_group(PS[2][:, 0:32], [(akv[0:6, :], aq[h][0:6, tk])], reads=[Rconst, Raq[h]], writes=[RPS[2]])
                act(Lb[0][:, 0:32], PS[2][:, 0:32], AF.Exp, reads=[RPS[2]], writes=[RL[0]])
                tt(Gb[0][:, 0:32], CT[:, gI, tk], Lb[0][:, 0:32], ALU.mult, reads=[RCT, RL[0]], writes=[RG[0]])
                mm_group(PS[0][0:32, 0:32], [(BT[:, gI, tk], CT[:, gI, tk])], reads=[RBT, RCT], writes=[RPS[0]])

                def fnE2(e, h=h, tk=tk):
                    e.matmul(PS[3][0:32, 0:32], lhsT=ak[h][0:6, tk], rhs=aq[h][0:6, tk], start=True, stop=False)
                    return e.matmul(PS[3][0:32, 0:32], lhsT=ident_bf[0:32, 0:32], rhs=maskneg[0:32, 0:32], start=False, stop=True)
                P.op("pe", fnE2, reads=[Rak[h], Raq[h], Rconst], writes=[RPS[3]])
                act(Lb[1][0:32, 0:32], PS[3][0:32, 0:32], AF.Exp, reads=[RPS[3]], writes=[RL[1]])
                stt(Gb[1][0:32, 0:32], PS[0][0:32, 0:32], dctok[0:32, tb, h:h + 1], Lb[1][0:32, 0:32], ALU.mult, ALU.mult,
                    reads=[RPS[0], Rdc, RL[1]], writes=[RG[1]])

                def fnV2(e, h=h, si=si, tb=tb, og=og, r0=r0):
                    e.matmul(PS[og][r0:r0 + 64, 0:32], lhsT=h0T[:, si, h, :], rhs=Gb[0][:, 0:32], start=True, stop=False)
                    return e.matmul(PS[og][r0:r0 + 64, 0:32], lhsT=xtok[0:32, tb, h * 64:(h + 1) * 64], rhs=Gb[1][0:32, 0:32],
                                    start=False, stop=True)
                P.op("pe", fnV2, reads=[Rh0T, Rxt, RG[0], RG[1]], writes=[RPS[og]])
                stt(xs[r0:r0 + 64, gI, tk], xs[r0:r0 + 64, gI, tk], small[r0:r0 + 64, C_SD + gI:C_SD + gI + 1], PS[og][r0:r0 + 64, 0:32],
                    ALU.mult, ALU.add, reads=[Rxs[gI], Rsmall, RPS[og]], writes=[Rxs[gI]])
        fcnt = 0
        for s in range(3):
            for h in range(4):
                g = 6 + fcnt % 2
                b = fcnt % 2
                fcnt += 1
                mm_group(PS[g][0:64, 0:128],
                         [(xwtok[0:blocks[tb][1], tb, h * 64:(h + 1) * 64], Btok[0:blocks[tb][1], tb, (h // 2) * 128:(h // 2) * 128 + 128])
                          for tb in seq_blocks[s]], reads=[Rxw, RBt], writes=[RPS[g]])
                if s == 0:
                    cp(hst[b][:], PS[g][0:64, 0:128], reads=[RPS[g]], writes=[Rhst[b]])
                else:
                    stt(hst[b][:], h0f[:, s - 1, h, :], ecb[:, s - 1, h:h + 1], PS[g][0:64, 0:128], ALU.mult, ALU.add,
                        reads=[Rh0f, Rat, RPS[g]], writes=[Rhst[b]])
                dma("sp", sh_o[l, s, h], hst[b][:], reads=[Rhst[b]], key=Rhst[b])
        P.barrier()
        sbx.release(mconv)
        wz = sbx.alloc("wz", [128, 8, 256], BF16)
        Rwz = Res("wz")
        dma("pool", wz[:], winv[:, :, 2056:2312], writes=[Rwz], key=Rwz)
        zt = [sbx.alloc("zt", [128, 512], F32) for _ in range(2)]
        Rzt = [Res("zt0"), Res("zt1")]
        sqz = sbx.alloc("sqz", [128, 2, 512], BF16)
        rsz = sbx.alloc("rsz", [128, 512], F32)
        Rsqz, Rrsz = Res("sqz"), Res("rsz")
        for ti, (t0, n) in enumerate(TILES):
            for c in range(2):
                mm_group(PS[c][:, :n], [(wz[:, kc, c * 128:(c + 1) * 128], xn[:, kc, t0:t0 + n]) for kc in range(8)],
                         reads=[Rwz, Rxn[ti]], writes=[RPS[c]])
                act(zt[c][:, :n], PS[c][:, :n], AF.Silu, reads=[RPS[c]], writes=[Rzt[c]])
                tt(xs[:, c, t0:t0 + n], xs[:, c, t0:t0 + n], zt[c][:, :n], ALU.mult, reads=[Rxs[c], Rzt[c]], writes=[Rxs[c]])
                act(sqz[:, c, :n], xs[:, c, t0:t0 + n], AF.Square, reads=[Rxs[c]], writes=[Rsqz])
            mm_group(PS[2][:, :n], [(ones_bf[:], sqz[:, c, :n]) for c in range(2)], reads=[Rsqz, Rconst], writes=[RPS[2]])
            act(rsz[:, :n], PS[2][:, :n], AF.Sqrt, reads=[RPS[2], Rconst], writes=[Rrsz], bias=eps_t[:], scale=1.0 / 256)
            recip(rsz[:, :n], rsz[:, :n], reads=[Rrsz], writes=[Rrsz])
            for c in range(2):
                stt(ymix[:, 6 + c, t0:t0 + n], xs[:, c, t0:t0 + n], small[:, C_SNW + c:C_SNW + c + 1], rsz[:, :n], ALU.mult, ALU.mult,
                    reads=[Rxs[c], Rsmall, Rrsz], writes=[Rym[6 + c]])
        P.barrier()
        sb.release(ms)
        sbx.release(mx)

    sbx = SB(nc, SB.LO, SB.LO + 8 * T * 4)
    Rspill = Res("xspill")
    POFF = {0: 0, 1: 2051, 2: 2086}
    XPW = 2121
    ones_f4 = sb.alloc("ones_f4", [4, 64], F32)
    memset(ones_f4[:], 1.0, writes=[Rconst])
    akv = sb.alloc("akv", [6, 128], BF16)
    memset(akv[:], 1.0, writes=[Rconst])
    P.op("pool", lambda e: e.affine_select(out=akv[:], in_=akv[:], pattern=[[0, 128]], compare_op=ALU.is_ge,
                                           fill=0.0, base=-3, channel_multiplier=1), reads=[Rconst], writes=[Rconst])
    ARENA2 = sb.mark()

    def split3(F_ap, R1_ap, R2_ap, hi_ap, mid_ap, lo_ap, RF, RR, RS):
        cp(hi_ap, F_ap, reads=[RF], writes=[RS])
        tt(R1_ap, F_ap, hi_ap, ALU.subtract, reads=[RF, RS], writes=[RR])
        cp(mid_ap, R1_ap, reads=[RR], writes=[RS])
        tt(R2_ap, R1_ap, mid_ap, ALU.subtract, reads=[RR, RS], writes=[RR])
        cp(lo_ap, R2_ap, reads=[RR], writes=[RS])

    def mix(l):
        j = 1
        m_arena = sb.mark()
        ymix = sb.alloc("ymix", [128, 8, T], BF16)
        Rym = [Res(f"ym{c}") for c in range(8)]
        xn = sb.alloc("xn", [128, 8, T], BF16)
        Rxn = {ti: Res(f"xn{ti}") for ti in range(5)}
        m1 = sb.mark()
        scrs = [alloc_norm_scratch(), alloc_norm_scratch()]
        for st_ in skewed(norm_pre_steps(1, list(range(5)), xn, Rxn, 0, scrs)):
            st_()
        for ti, (t0, n) in enumerate(TILES):
            dma("sp", xspill[:, :, t0:t0 + n], X[:, :, t0:t0 + n], reads=[RX[ti]], writes=[Rspill], key=RX[ti])
        P.barrier()
        sb.release(m1)
        winv = win_d[l].rearrange("(kc p) n -> p kc n", p=128)
        allxn = [Rxn[ti] for ti in range(5)]

        def lru():
            ms, mx = sb.mark(), sbx.mark()
            wl = sb.alloc("wl", [128, 8, 512], BF16)
            Rwl = Res("wl")
            dma("pool", wl[:], winv[:, :, 0:512], writes=[Rwl], key=Rwl)
            h0sb = sb.alloc("h0sb", [128, 2, 2], F32)
            Rh0 = Res("h0")
            dma("sp", h0sb[:], lh_d[l], writes=[Rh0], key=Rh0)
            bd = [[sb.alloc("bd", [128, 128], BF16) for c in range(2)] for g in range(2)]
            Rbd = [[Res("bd") for c in range(2)] for g in range(2)]
            for g, src in ((0, lwa_d), (1, lwx_d)):
                for c in range(2):
                    memset(bd[g][c][:], 0.0, writes=[Rbd[g][c]])
                    for hh_ in range(2):
                        dma("pool", bd[g][c][hh_ * 64:(hh_ + 1) * 64, hh_ * 64:(hh_ + 1) * 64], src[l, 2 * c + hh_],
                            writes=[Rbd[g][c]], key=Rbd[g][c])
            c1t = sb.alloc("c1t", [128, 2, 4], F32)
            Rc1 = Res("c1t")
            wmx = [sb.alloc("wmx", [128, 8, 512], BF16) for _ in range(2)]
            Rwmx = [Res("wmx0"), Res("wmx1")]
            extra = mod_thunks(l, [2], wmx, Rwmx)
            if l + 1 < 2:
                extra += mod_thunks(l + 1, [0, 1], wmx, Rwmx)

            def more(k=1):
                for _ in range(k):
                    if extra:
                        extra.pop(0)()
            xp = sbx.alloc("xp", [128, XPW], F32)
            u = sbx.alloc("u", [128, T], F32)
            ub = sbx.alloc("ub", [128, T], BF16)
            r = sbx.alloc("r", [128, T], F32)
            ig = sbx.alloc("ig", [128, T], F32)
            a = sbx.alloc("a", [128, T], F32)
            t1 = sbx.alloc("t1", [128, T], F32)
            hh = sb.alloc("hh", [128, T], F32)
            gg = sb.alloc("gg", [128, T], F32)
            Rxp, Ru, Rub, Rr, Rig, Ra, Rt1, Rhh, Rgg = [Res(n) for n in "xp u ub r ig a t1 hh gg".split()]
            for c in range(2):
                lam = small[:, C_LLAM + c:C_LLAM + c + 1]
                act(c1t[:, c, 0:1], lam, AF.Exp, reads=[Rsmall], writes=[Rc1], scale=-1.0)
                act(c1t[:, c, 1:2], c1t[:, c, 0:1], AF.Ln, reads=[Rc1, Rconst], writes=[Rc1], bias=one_t[:])
                tsc(c1t[:, c, 2:3], c1t[:, c, 1:2], -8.0, None, ALU.mult, None, reads=[Rc1], writes=[Rc1])
                tsc(c1t[:, c, 3:4], c1t[:, c, 1:2], -16.0, None, ALU.mult, None, reads=[Rc1], writes=[Rc1])
                memset(xp[:, 0:3], 0.0, writes=[Rxp])
                dma("sp", xp[:, 2051:2054], lconv_d[l, :, 0, c, :], writes=[Rxp], key=Rxp)
                dma("sp", xp[:, 2086:2089], lconv_d[l, :, 1, c, :], writes=[Rxp], key=Rxp)
                for ti, (t0, n) in enumerate(TILES):
                    pa, pb = PS[ti % 2], PS[2 + ti % 2]
                    mm_group(pa[:, :n], [(wl[:, kc, c * 128:(c + 1) * 128], xn[:, kc, t0:t0 + n]) for kc in range(8)],
                             reads=[Rwl, Rxn[ti]], writes=[RPS[ti % 2]])
                    mm_group(pb[:, :n], [(wl[:, kc, 256 + c * 128:256 + (c + 1) * 128], xn[:, kc, t0:t0 + n]) for kc in range(8)],
                             reads=[Rwl, Rxn[ti]], writes=[RPS[2 + ti % 2]])
                    if ti < 4:
                        act(xp[:, 3 + t0:3 + t0 + n], pa[:, :n], AF.Identity, reads=[RPS[ti % 2]], writes=[Rxp])
                    else:
                        act(xp[:, 2054:2086], pa[:, 0:32], AF.Identity, reads=[RPS[ti % 2]], writes=[Rxp])
                        act(xp[:, 2089:2121], pa[:, 32:64], AF.Identity, reads=[RPS[ti % 2]], writes=[Rxp])
                    cp(gg[:, t0:t0 + n], pb[:, :n], reads=[RPS[2 + ti % 2]], writes=[Rgg])
                    more(1)
                cw = lambda k: small[:, C_LCW + c * 4 + k:C_LCW + c * 4 + k + 1]
                cb = small[:, C_LCB + c:C_LCB + c + 1]
                for (s0, sl, s) in SEGS:
                    p0 = POFF[s]
                    tsc(u[:, s0:s0 + sl], xp[:, p0:p0 + sl], cw(0), cb, ALU.mult, ALU.add, reads=[Rxp, Rsmall], writes=[Ru])
                    for k in range(1, 4):
                        stt(u[:, s0:s0 + sl], xp[:, p0 + k:p0 + k + sl], cw(k), u[:, s0:s0 + sl], ALU.mult, ALU.add,
                            reads=[Rxp, Rsmall, Ru], writes=[Ru])
                    dma("sp", lconv_o[l, :, s, c, :], xp[:, p0 + sl:p0 + sl + 3], reads=[Rxp], key=Rxp)
                act(ub[:], u[:], AF.Identity, reads=[Ru], writes=[Rub])
                for ti, (t0, n) in enumerate(TILES):
                    pa, pb = PS[ti % 2], PS[2 + ti % 2]
                    mm_group(pa[:, :n], [(bd[0][c][:], ub[:, t0:t0 + n])], reads=[Rbd[0][c], Rub], writes=[RPS[ti % 2]])
                    mm_group(pb[:, :n], [(bd[1][c][:], ub[:, t0:t0 + n])], reads=[Rbd[1][c], Rub], writes=[RPS[2 + ti % 2]])
                    act(r[:, t0:t0 + n], pa[:, :n], AF.Sigmoid, reads=[RPS[ti % 2], Rsmall], writes=[Rr],
                        bias=small[:, C_LBA + c:C_LBA + c + 1])
                    act(ig[:, t0:t0 + n], pb[:, :n], AF.Sigmoid, reads=[RPS[2 + ti % 2], Rsmall], writes=[Rig],
                        bias=small[:, C_LBX + c:C_LBX + c + 1])
                    more(1)
                act(a[:], r[:], AF.Exp, reads=[Rr, Rc1], writes=[Ra], scale=c1t[:, c, 2:3])
                act(t1[:], r[:], AF.Exp, reads=[Rr, Rc1], writes=[Rt1], scale=c1t[:, c, 3:4])
                tsc(t1[:], t1[:], -1.0, 1.0, ALU.mult, ALU.add, reads=[Rt1], writes=[Rt1])
                tsc(t1[:], t1[:], 1e-30, None, ALU.max, None, reads=[Rt1], writes=[Rt1])
                act(t1[:], t1[:], AF.Sqrt, reads=[Rt1], writes=[Rt1])
                tt(ig[:], ig[:], u[:], ALU.mult, reads=[Rig, Ru], writes=[Rig])
                tt(ig[:], ig[:], t1[:], ALU.mult, reads=[Rig, Rt1], writes=[Rig])
                for (s0, sl, s) in SEGS:
                    init = 0.0 if s == 0 else h0sb[:, s - 1, c:c + 1]
                    scan(hh[:, s0:s0 + sl], a[:, s0:s0 + sl], ig[:, s0:s0 + sl], init, reads=[Ra, Rig, Rh0], writes=[Rhh])
                    dma("sp", lh_o[l, :, s, c:c + 1], hh[:, s0 + sl - 1:s0 + sl], reads=[Rhh], key=Rhh, slow=True)
                act(t1[:], gg[:], AF.Square, reads=[Rgg], writes=[Rt1])
                tsc(t1[:], t1[:], 0.044715, 1.0, ALU.mult, ALU.add, reads=[Rt1], writes=[Rt1])
                tt(t1[:], t1[:], gg[:], ALU.mult, reads=[Rt1, Rgg], writes=[Rt1])
                act(t1[:], t1[:], AF.Sigmoid, reads=[Rt1], writes=[Rt1], scale=1.5957691216057308)
                tt(gg[:], gg[:], t1[:], ALU.mult, reads=[Rgg, Rt1], writes=[Rgg])
                tt(ymix[:, c, :], hh[:], gg[:], ALU.mult, reads=[Rhh, Rgg], writes=[Rym[c]])
            more(100)
            P.barrier()
            sb.release(ms)
            sbx.release(mx)

        lru()
        if STAGE_SUB >= 2:
            ssd(l, ymix, Rym, xn, Rxn, winv)
        if STAGE_SUB >= 3:
            fox(l, ymix, Rym, xn, Rxn, winv)

        if DBG:
            md = sb.mark()
            stg = [sb.alloc("dstg", [128, 512], F32) for _ in range(2)]
            Rstg = [Res("dstg0"), Res("dstg1")]
            kk = 0
            for c in range(8):
                for ti, (t0, n) in enumerate(TILES):
                    b = kk % 2
                    kk += 1
                    cp(stg[b][:, :n], ymix[:, c, t0:t0 + n], reads=[Rym[c]], writes=[Rstg[b]])
                    dma("sp", dbg_o[:, c, t0:t0 + n], stg[b][:, :n], reads=[Rstg[b]], key=Rstg[b])
            P.barrier()
            sb.release(md)
        for ti, (t0, n) in enumerate(TILES):
            dma("sp", X[:, :, t0:t0 + n], xspill[:, :, t0:t0 + n], reads=[Rspill], writes=[RX[ti]], key=RX[ti])
        m2 = sb.mark()
        wo = sb.alloc("wo", [128, 8, 1024], BF16)
        Rwo = Res("wo")
        dma("pool", wo[:], wout_d[l].rearrange("(kc p) n -> p kc n", p=128), writes=[Rwo], key=Rwo)
        scr = alloc_norm_scratch()
        for half in ([0, 1], [2, 3, 4]):
            t_lo = TILES[half[0]][0]
            t_hi = TILES[half[-1]][0] + TILES[half[-1]][1]
            m3 = sb.mark()
            yacc = sb.alloc("yacc", [128, 8, t_hi - t_lo], F32)
            Ry = {ti: Res(f"y{ti}") for ti in half}
            cnt = 0
            for ti in half:
                t0, n = TILES[ti]
                for oc in range(8):
                    g = cnt % 2
                    cnt += 1
                    mm_group(PS[g][:, :n], [(wo[:, kc, oc * 128:(oc + 1) * 128], ymix[:, kc, t0:t0 + n]) for kc in range(8)],
                             reads=[Rwo] + Rym, writes=[RPS[g]])
                    if oc % 2 == 0:
                        act(yacc[:, oc, t0 - t_lo:t0 - t_lo + n], PS[g][:, :n], AF.Identity, reads=[RPS[g]], writes=[Ry[ti]])
                    else:
                        cp(yacc[:, oc, t0 - t_lo:t0 - t_lo + n], PS[g][:, :n], reads=[RPS[g]], writes=[Ry[ti]])
            post_norm(1, half, yacc, Ry, t_lo, scr)
            P.barrier()
            sb.release(m3)
        P.barrier()
        sb.release(m_arena)

    nsub = 0
    done = False
    m0_ = sb.mark()
    wm0 = [sb.alloc("wm", [128, 8, 512], BF16) for _ in range(2)]
    Rwm0 = [Res("wm0"), Res("wm1")]
    for th in mod_thunks(0, [0, 1], wm0, Rwm0):
        th()
    P.barrier()
    sb.release(m0_)
    for l in range(2):
        CUR[0] = l
        for kind in ("ffn0", "mix", "ffn1"):
            if nsub >= STAGE:
                done = True
                break
            if kind == "ffn0":
                ffn(l, 0, 0)
            elif kind == "mix":
                mix(l)
            else:
                ffn(l, 1, 2)
            nsub += 1
        if done:
            break

    for t, (t0, n) in enumerate(TILES):
        dma("sp", yT_o[:, :, t0:t0 + n], X[:, :, t0:t0 + n], reads=[RX[t]], key=RX[t])
    P.barrier()
    P.op("sp", lambda e: e.nop())
    P.emit(st)
    st.close()
    print(f"[mk] ops={len(P.ops)} sems={P.n_sems} sbuf_peak={sb.peak}")
    return nc


def _prep_core(core, I):
    f = np.float32
    b = core
    s0, s1 = 2 * core, 2 * core + 2
    xcat = np.concatenate([I["x_prompt"][b], I["x_sample"][s0], I["x_sample"][s0 + 1]], axis=0)
    xT = np.ascontiguousarray(xcat.reshape(T, 8, 128).transpose(2, 1, 0)).astype(f)
    ccat = np.concatenate([I["c_prompt"][b:b + 1], I["c_sample"][s0:s1]], axis=0)
    cT = np.ascontiguousarray(ccat.reshape(3, 8, 128).transpose(2, 1, 0)).astype(f)
    m = {"xT": xT, "cT": cT}
    m["ckT"] = np.ascontiguousarray(I["cache_fox_k"][:, s0:s1].transpose(0, 1, 3, 4, 2))
    a = I["cache_fox_v"][:, s0:s1].reshape(2, 2, 32, 128, 8, 64)
    m["cvh"] = np.ascontiguousarray(a.transpose(0, 1, 4, 3, 2, 5)).reshape(2, 2, 8, 128, 2048)
    m["clfT"] = np.ascontiguousarray(I["cache_fox_logf"][:, s0:s1].transpose(0, 1, 3, 2))
    a = I["state_lru_conv"][:, s0:s1].reshape(2, 2, 3, 2, 128)
    m["lconvT"] = np.ascontiguousarray(a.transpose(0, 4, 1, 3, 2))
    a = I["state_lru_h"][:, s0:s1].reshape(2, 2, 2, 128)
    m["lhT"] = np.ascontiguousarray(a.transpose(0, 3, 1, 2))
    a = I["state_ssd_conv"][:, s0:s1].reshape(2, 2, 3, 6, 128)
    m["sconvT"] = np.ascontiguousarray(a.transpose(0, 4, 1, 3, 2))
    m["sh0T"] = np.ascontiguousarray(I["state_ssd_h"][:, s0:s1].transpose(0, 1, 2, 4, 3))
    m["sh0f"] = np.ascontiguousarray(I["state_ssd_h"][:, s0:s1])
    return m


def _prep_shared(I):
    f = np.float32
    sm = np.zeros((2, 128, NSMALL), f)

    def fm(a, nch):
        sh = a.shape[:-1]
        a = a.reshape(sh + (nch, 128))
        return np.moveaxis(a, -1, 0)

    for l in range(2):
        sm[l, :, 0:24] = fm(I["norm_pre"][l], 8).reshape(128, 24)
        sm[l, :, 24:48] = fm(I["norm_post"][l], 8).reshape(128, 24)
        sm[l, :, 48:120] = I["b_mod"][l].reshape(72, 128).T
        sm[l, :, 120:128] = fm(I["lru_conv_w"][l], 2).transpose(0, 2, 1).reshape(128, 8)
        sm[l, :, 128:130] = fm(I["lru_conv_b"][l], 2)
        sm[l, :, 130:132] = fm(I["lru_ba"][l], 2)
        sm[l, :, 132:134] = fm(I["lru_bx"][l], 2)
        sm[l, :, 134:136] = fm(I["lru_lambda"][l], 2)
        sm[l, :, 136:160] = fm(I["ssd_conv_w"][l], 6).transpose(0, 2, 1).reshape(128, 24)
        sm[l, :, 160:166] = fm(I["ssd_conv_b"][l], 6)
        sm[l, :, 166:168] = fm(np.repeat(I["ssd_d"][l], 64), 2)
        sm[l, :, 168:170] = fm(I["ssd_norm_w"][l], 2)
    hp = np.zeros((2, 8, 4), f)
    hp[:, :, 0] = I["fox_f_bias"]
    hp[:, 0:4, 1] = I["ssd_dt_bias"]
    hp[:, 0:4, 2] = I["ssd_a_log"]
    sh = {"smallp": sm, "headp": hp}
    for k in ["w_mod", "ffn_w_gate", "ffn_w_up", "ffn_w_down", "w_in", "w_out", "lru_wa", "lru_wx"]:
        sh[k] = np.ascontiguousarray(I[k], dtype=f)
    return sh


_NC_CACHE = {}


def kernel(**inputs):
    I = {k: np.asarray(v) for k, v in inputs.items()}
    if "nc" not in _NC_CACHE:
        _NC_CACHE["nc"] = build_program()
    nc = _NC_CACHE["nc"]
    shared = _prep_shared(I)
    in_maps = []
    for c in range(NRUN):
        m = dict(shared)
        m.update(_prep_core(c, I))
        in_maps.append(m)
    res = run_bass_kernel_spmd(nc, in_maps, core_ids=list(range(NRUN)))
    R = res.results
    _NC_CACHE["last"] = R
    f = np.float32
    nb = 2 * NRUN
    y_p = np.zeros((8, NPR, D), f)
    y_s = np.zeros((16, NSQ, D), f)
    pk = np.zeros((2, 8, NPR, 8, 64), f); pv = np.zeros((2, 8, NPR, 8, 64), f); plf = np.zeros((2, 8, NPR, 8), f)
    plc = np.zeros((2, 8, 3, 256), f); plh = np.zeros((2, 8, 256), f); psc = np.zeros((2, 8, 3, 768), f)
    psh = np.zeros((2, 8, 4, 64, 128), f)
    sk = np.zeros((2, 16, NSQ, 8, 64), f); sv = np.zeros((2, 16, NSQ, 8, 64), f); slf = np.zeros((2, 16, NSQ, 8), f)
    slc = np.zeros((2, 16, 3, 256), f); slh = np.zeros((2, 16, 256), f); ssc = np.zeros((2, 16, 3, 768), f)
    ssh = np.zeros((2, 16, 4, 64, 128), f)
    for c in range(NRUN):
        r = R[c]
        y = r["yT"].transpose(2, 1, 0).reshape(T, D)
        y_p[c] = y[:NPR]
        y_s[2 * c] = y[NPR:NPR + NSQ]
        y_s[2 * c + 1] = y[NPR + NSQ:]
        k = r["fk_out"].transpose(0, 3, 2, 1).reshape(2, T, 8, 64)
        v = r["fv_out"].reshape(2, T, 8, 64)
        lf = r["logfT"].transpose(0, 2, 1)
        pk[:, c] = k[:, :NPR]; pv[:, c] = v[:, :NPR]; plf[:, c] = lf[:, :NPR]
        for si in range(2):
            sl = slice(NPR + si * NSQ, NPR + (si + 1) * NSQ)
            sk[:, 2 * c + si] = k[:, sl]; sv[:, 2 * c + si] = v[:, sl]; slf[:, 2 * c + si] = lf[:, sl]
        lc = r["lconv_o"].transpose(0, 2, 4, 3, 1).reshape(2, 3, 3, 256)
        lh = r["lh_o"].transpose(0, 2, 3, 1).reshape(2, 3, 256)
        sc = r["sconv_o"].transpose(0, 2, 4, 3, 1).reshape(2, 3, 3, 768)
        sh = r["sh_o"]
        plc[:, c] = lc[:, 0]; plh[:, c] = lh[:, 0]; psc[:, c] = sc[:, 0]; psh[:, c] = sh[:, 0]
        for si in range(2):
            slc[:, 2 * c + si] = lc[:, 1 + si]; slh[:, 2 * c + si] = lh[:, 1 + si]
            ssc[:, 2 * c + si] = sc[:, 1 + si]; ssh[:, 2 * c + si] = sh[:, 1 + si]
    return (y_p, y_s, pk, pv, plf, plc, plh, psc, psh, sk, sv, slf, slc, slh, ssc, ssh)
```

```python
import os
import math
import numpy as np
from contextlib import ExitStack
import concourse.bass as bass
import concourse.mybir as mybir
from concourse.bass_utils import run_bass_kernel_spmd

F32 = mybir.dt.float32
BF16 = mybir.dt.bfloat16
AF = mybir.ActivationFunctionType
ALU = mybir.AluOpType

NCORES = 8
D = 1024
NPR = 2048
NSQ = 32
T = NPR + 2 * NSQ
DFF = 2816
DIN = 3084
PAST = 4096
EPS = 1e-6
TILES = [(0, 512), (512, 512), (1024, 512), (1536, 512), (2048, 64)]
SEGS = [(0, 2048, 0), (2048, 32, 1), (2080, 32, 2)]
NSMALL = 170
ENGS = ("pe", "act", "dve", "pool", "sp")
SEM_LIMIT = 30000
SWDGE_DEPTH = int(os.environ.get("MK_SWDGE", "3"))
STAGE = int(os.environ.get("MK_STAGE", "99"))
STAGE_SUB = int(os.environ.get("MK_SUB", "99"))
DBG = int(os.environ.get("MK_DBG", "0"))
FOXS = int(os.environ.get("MK_FOX", "99"))
TOG = os.environ.get("MK_TOG", "")
NRUN = int(os.environ.get("MK_CORES", "8"))


class Res:
    __slots__ = ("name", "last_w", "readers")

    def __init__(self, name):
        self.name = name
        self.last_w = None
        self.readers = []


class Op:
    __slots__ = ("idx", "eng", "fn", "deps", "is_dma", "key", "sig", "sem", "val", "epoch", "bar")

    def __init__(self, idx, eng, fn, is_dma, key):
        self.idx = idx
        self.eng = eng
        self.fn = fn
        self.deps = set()
        self.is_dma = is_dma
        self.key = key
        self.sig = False
        self.sem = None
        self.val = 0


class Prog:
    def __init__(self, nc):
        self.nc = nc
        self.ops = []
        self.barrier_deps = {e: set() for e in ENGS}
        self.last_on_eng = {e: None for e in ENGS}
        self.dma_last = {}
        self.epoch = 0

    def op(self, eng, fn, reads=(), writes=(), dma=False, key=None):
        idx = len(self.ops)
        o = Op(idx, eng, fn, dma, key)
        o.epoch = self.epoch
        o.bar = set()
        if self.barrier_deps[eng]:
            o.deps |= self.barrier_deps[eng]
            o.bar = set(self.barrier_deps[eng])
            self.barrier_deps[eng] = set()
        for r in reads:
            if r.last_w is not None:
                o.deps.add(r.last_w)
        for w in writes:
            if w.last_w is not None:
                o.deps.add(w.last_w)
            for rd in w.readers:
                o.deps.add(rd)
        for r in reads:
            r.readers.append(idx)
        for w in writes:
            w.last_w = idx
            w.readers = []
        o.deps.discard(idx)
        if dma:
            assert key is not None
            self.dma_last[key] = idx
        self.ops.append(o)
        self.last_on_eng[eng] = idx
        return o

    def barrier(self):
        s = set()
        for e in ENGS:
            if self.last_on_eng[e] is not None:
                s.add(self.last_on_eng[e])
        for k, v in self.dma_last.items():
            s.add(v)
        for e in ENGS:
            self.barrier_deps[e] |= s
        self.dma_last = {}
        self.epoch += 1
        for e in ENGS:
            self.op(e, lambda eng: eng.nop())

    def emit(self, stack):
        nc = self.nc
        ops = self.ops
        for o in ops:
            for d in o.deps:
                ops[d].sig = True
            if o.is_dma and o.eng == "pool":
                o.sig = True
        eng_sem, eng_cnt = {}, {}
        nsem = [0]

        def new_sem(nm):
            nsem[0] += 1
            return stack.enter_context(nc.semaphore(nm + str(nsem[0])))

        sw_keys = set()
        for o in ops:
            if o.is_dma and o.eng == "pool":
                sw_keys.add((o.key, o.epoch))
        pools = {True: [], False: []}
        limbo = {True: [], False: []}
        active, cur_epoch = {}, 0
        for o in ops:
            if o.epoch != cur_epoch:
                for sw in (True, False):
                    pools[sw].extend(limbo[sw])
                    limbo[sw] = []
                for (k, sw), sc in active.items():
                    limbo[sw].append(sc)
                active = {}
                cur_epoch = o.epoch
            if o.is_dma:
                sw = (o.key, o.epoch) in sw_keys
                if (o.key, sw) not in active:
                    active[(o.key, sw)] = pools[sw].pop() if pools[sw] else [new_sem("ds" if sw else "dh"), 0]
                sc = active[(o.key, sw)]
                sc[1] += 16
                o.sem, o.val, o.sig = sc[0], sc[1], True
            elif o.sig:
                e = o.eng
                if e not in eng_sem or eng_cnt[e] >= SEM_LIMIT:
                    eng_sem[e] = new_sem("e" + e)
                    eng_cnt[e] = 0
                eng_cnt[e] += 1
                o.sem = eng_sem[e]
                o.val = eng_cnt[e]
        self.n_sems = nsem[0]
        per_eng = {e: [o for o in ops if o.eng == e] for e in ENGS}
        block = stack.enter_context(nc.Block())

        def make(e):
            lst = per_eng[e]

            def body(eng):
                waited = {}
                issued = []
                for o in lst:
                    need = {}
                    if e == "pool" and o.is_dma:
                        if len(issued) >= SWDGE_DEPTH:
                            p = issued[-SWDGE_DEPTH]
                            if waited.get(id(p.sem), 0) < p.val:
                                need[id(p.sem)] = (p.sem, p.val)
                        issued.append(o)
                    for d in o.deps:
                        p = ops[d]
                        if e == "pe" and p.eng == "pe" and not p.is_dma:
                            continue
                        if p.epoch < o.epoch and d not in o.bar:
                            continue
                        sid = id(p.sem)
                        if waited.get(sid, 0) >= p.val:
                            continue
                        if sid not in need or need[sid][1] < p.val:
                            need[sid] = (p.sem, p.val)
                    for sid, (sem, val) in need.items():
                        eng.wait_ge(sem, val)
                        waited[sid] = val
                    ins = o.fn(eng)
                    if o.sig:
                        ins.then_inc(o.sem, 16 if o.is_dma else 1)
            return body

        if per_eng["pe"]:
            block.tensor(make("pe"))
        if per_eng["act"]:
            block.scalar(make("act"))
        if per_eng["dve"]:
            block.vector(make("dve"))
        if per_eng["pool"]:
            block.gpsimd(make("pool"))
        if per_eng["sp"]:
            block.sync(make("sp"))


class SB:
    LO = 16512
    HI = 229376

    CNT = [0]

    def __init__(self, nc, lo=None, hi=None):
        self.nc = nc
        self.lo = SB.LO if lo is None else lo
        self.hi = SB.HI if hi is None else hi
        self.top = self.lo
        self.peak = 0

    def alloc(self, name, shape, dtype):
        esz = 2 if dtype == BF16 else 4
        nb = esz
        for s in shape[1:]:
            nb *= s
        nb = (nb + 63) // 64 * 64
        off = self.top
        self.top += nb
        self.peak = max(self.peak, self.top)
        assert self.top <= self.hi, f"SBUF overflow allocating {name}: {self.top} > {self.hi}"
        SB.CNT[0] += 1
        return self.nc.alloc_sbuf_tensor_at(f"{name}_{SB.CNT[0]}", list(shape), dtype, offset=off)

    def mark(self):
        return self.top

    def release(self, m):
        self.top = m


def build_program():
    nc = bass.Bass("TRN2", target_bir_lowering=False)

    def din(name, shape):
        return nc.dram_tensor(name, list(shape), F32, kind="ExternalInput").ap()

    def dout(name, shape):
        return nc.dram_tensor(name, list(shape), F32, kind="ExternalOutput").ap()

    xT_d = din("xT", [128, 8, T])
    cT_d = din("cT", [128, 8, 3])
    small_d = din("smallp", [2, 128, NSMALL])
    headp_d = din("headp", [2, 8, 4])
    wmod_d = din("w_mod", [2, D, 9 * D])
    wg_d = din("ffn_w_gate", [2, 2, D, DFF])
    wu_d = din("ffn_w_up", [2, 2, D, DFF])
    wd_d = din("ffn_w_down", [2, 2, DFF, D])
    win_d = din("w_in", [2, D, DIN])
    wout_d = din("w_out", [2, D, D])
    lwa_d = din("lru_wa", [2, 4, 64, 64])
    lwx_d = din("lru_wx", [2, 4, 64, 64])
    ckT_d = din("ckT", [2, 2, 8, 64, PAST])
    cv_d = din("cvh", [2, 2, 8, 128, 2048])
    clfT_d = din("clfT", [2, 2, 8, PAST])
    lconv_d = din("lconvT", [2, 128, 2, 2, 3])
    lh_d = din("lhT", [2, 128, 2, 2])
    sconv_d = din("sconvT", [2, 128, 2, 6, 3])
    sh0_d = din("sh0T", [2, 2, 4, 128, 64])
    sh0f_d = din("sh0f", [2, 2, 4, 64, 128])
    yT_o = dout("yT", [128, 8, T])
    kT_o = dout("fk_out", [2, 128, 4, T])
    v_o = dout("fv_out", [2, T, 512])
    lf_o = dout("logfT", [2, 8, T])
    lconv_o = dout("lconv_o", [2, 128, 3, 2, 3])
    lh_o = dout("lh_o", [2, 128, 3, 2])
    sconv_o = dout("sconv_o", [2, 128, 3, 6, 3])
    sh_o = dout("sh_o", [2, 3, 4, 64, 128])
    xspill = nc.dram_tensor("xspill", [128, 8, T], F32).ap()
    dbg_o = dout("dbg", [128, 8, T]) if DBG else None

    st = ExitStack()
    P = Prog(nc)
    sb = SB(nc)

    PS = [nc.alloc_psum_tensor(f"psb{i}", [128, 512], F32) for i in range(8)]
    RPS = [Res(f"ps{i}") for i in range(8)]

    X = sb.alloc("X", [128, 8, T], F32)
    RX = [Res(f"X{t}") for t in range(5)]
    ones_bf = sb.alloc("ones_bf", [128, 128], BF16)
    ident_bf = sb.alloc("ident_bf", [128, 128], BF16)
    ident_f = sb.alloc("ident_f", [128, 128], F32)
    maskneg = sb.alloc("maskneg", [128, 128], BF16)
    eps_t = sb.alloc("eps_t", [128, 1], F32)
    one_t = sb.alloc("one_t", [128, 1], F32)
    csil = sb.alloc("csil", [128, 8, 3], BF16)
    cin = sb.alloc("cin", [128, 8, 3], F32)
    CUR = [0]

    class Sel:
        def __init__(self, items):
            self.items = items

        def __getitem__(self, k):
            return self.items[CUR[0]][k]

    class ResSel:
        def __init__(self, items):
            object.__setattr__(self, "items", items)

        def __getattr__(self, n):
            return getattr(self.items[CUR[0]], n)

        def __setattr__(self, n, v):
            setattr(self.items[CUR[0]], n, v)

    small = Sel([sb.alloc("small", [128, NSMALL], F32) for _ in range(2)])
    headp = Sel([sb.alloc("headp", [8, 4], F32) for _ in range(2)])
    modsb = Sel([sb.alloc("modsb", [128, 72, 3], F32) for _ in range(2)])
    gs_t = Sel([sb.alloc("gs_t", [128, 3, 8, 3], F32) for _ in range(2)])
    gp_t = Sel([sb.alloc("gp_t", [128, 3, 8, 3], F32) for _ in range(2)])
    Rconst = Res("const")
    Rsmall = ResSel([Res("small0"), Res("small1")])
    Rmod_lj = [[Res(f"mod{l}_{j}") for j in range(3)] for l in range(2)]

    def RM(j):
        return Rmod_lj[CUR[0]][j]
    Rcs = Res("csil")
    dum = ones_bf
    ARENA = sb.mark()

    C_NPRE, C_NPOST, C_BMOD = 0, 24, 48
    C_LCW, C_LCB, C_LBA, C_LBX, C_LLAM = 120, 128, 130, 132, 134
    C_SCW, C_SCB, C_SD, C_SNW = 136, 160, 166, 168

    def dma(eng, out, in_, reads=(), writes=(), key=None, slow=False):
        if slow:
            return P.op(eng, lambda e: e.dma_start(out=out, in_=in_, allow_slow_non_contiguous=True),
                        reads=reads, writes=writes, dma=True, key=key)
        return P.op(eng, lambda e: e.dma_start(out=out, in_=in_), reads=reads, writes=writes, dma=True, key=key)

    def mm_group(out, pairs, reads, writes):
        n = len(pairs)

        def fn(e):
            ins = None
            for i, (l, r) in enumerate(pairs):
                ins = e.matmul(out, lhsT=l, rhs=r, start=(i == 0), stop=(i == n - 1))
            return ins
        return P.op("pe", fn, reads=reads, writes=writes)

    def act(out, in_, func, reads, writes, bias=None, scale=None):
        kw = {}
        if bias is not None:
            kw["bias"] = bias
        if scale is not None:
            kw["scale"] = scale
        return P.op("act", lambda e: e.activation(out=out, in_=in_, func=func, **kw), reads=reads, writes=writes)

    def dve(fn, reads, writes):
        return P.op("dve", fn, reads=reads, writes=writes)

    def tt(out, in0, in1, op, reads, writes, eng="dve"):
        return P.op(eng, lambda e: e.tensor_tensor(out=out, in0=in0, in1=in1, op=op), reads=reads, writes=writes)

    def stt(out, in0, scalar, in1, op0, op1, reads, writes):
        return P.op("dve", lambda e: e.scalar_tensor_tensor(out=out, in0=in0, scalar=scalar, in1=in1, op0=op0, op1=op1),
                    reads=reads, writes=writes)

    def tsc(out, in0, s1, s2, op0, op1, reads, writes, eng="dve"):
        if s2 is None:
            return P.op(eng, lambda e: e.tensor_scalar(out=out, in0=in0, scalar1=s1, scalar2=None, op0=op0),
                        reads=reads, writes=writes)
        return P.op(eng, lambda e: e.tensor_scalar(out=out, in0=in0, scalar1=s1, scalar2=s2, op0=op0, op1=op1),
                    reads=reads, writes=writes)

    def recip(out, in_, reads, writes):
        return P.op("dve", lambda e: e.reciprocal(out=out, in_=in_), reads=reads, writes=writes)

    def cp(out, in_, reads, writes, eng="dve"):
        return P.op(eng, lambda e: e.tensor_copy(out=out, in_=in_), reads=reads, writes=writes)

    def scan(out, d0, d1, init, reads, writes):
        return P.op("dve", lambda e: e.tensor_tensor_scan(out=out, data0=d0, data1=d1, initial=init,
                                                          op0=ALU.mult, op1=ALU.add), reads=reads, writes=writes)

    def memset(ap, val, writes, eng="dve"):
        return P.op(eng, lambda e: e.memset(ap, val), writes=writes)

    P.op("dve", lambda e: e.memset(ones_bf[:], 1.0), writes=[Rconst])
    P.op("dve", lambda e: e.memset(eps_t[:], EPS), writes=[Rconst])
    P.op("dve", lambda e: e.memset(one_t[:], 1.0), writes=[Rconst])
    P.op("dve", lambda e: e.memset(ident_f[:], 1.0), writes=[Rconst])
    P.op("pool", lambda e: e.affine_select(out=ident_f[:], in_=ident_f[:], pattern=[[-1, 128]],
                                           compare_op=ALU.is_equal, fill=0.0, base=0, channel_multiplier=1),
         reads=[Rconst], writes=[Rconst])
    P.op("dve", lambda e: e.tensor_copy(out=ident_bf[:], in_=ident_f[:]), reads=[Rconst], writes=[Rconst])
    P.op("dve", lambda e: e.memset(maskneg[:], 0.0), writes=[Rconst])
    P.op("pool", lambda e: e.affine_select(out=maskneg[:], in_=maskneg[:], pattern=[[1, 128]],
                                           compare_op=ALU.is_ge, fill=-30000.0, base=0, channel_multiplier=-1),
         reads=[Rconst], writes=[Rconst])

    NWARM = int(os.environ.get("MK_WARM", "0"))

    def keep_warm(bank, n):
        if n <= 0:
            return

        def fn(e):
            ins = None
            for _ in range(n):
                ins = e.matmul(PS[bank][:, 0:128], lhsT=ident_bf[:], rhs=dum[:], start=True, stop=True)
            return ins
        P.op("pe", fn, reads=[Rconst], writes=[RPS[bank]])

    dma("sp", cin[:], cT_d, writes=[Rcs], key=Rcs)
    for l_ in range(2):
        dma("sp", small.items[l_][:], small_d[l_], writes=[Rsmall.items[l_]], key=Rsmall.items[l_])
        dma("sp", headp.items[l_][:], headp_d[l_], writes=[Rsmall.items[l_]], key=Rsmall.items[l_])
    for t, (t0, n) in enumerate(TILES):
        dma("sp", X[:, :, t0:t0 + n], xT_d[:, :, t0:t0 + n], writes=[RX[t]], key=RX[t])
    act(csil[:], cin[:], AF.Silu, reads=[Rcs], writes=[Rcs])

    def mod_thunks(l, parts, wm, Rwm):
        th = []
        wv = wmod_d[l].rearrange("(kc p) n -> p kc n", p=128)
        psm = PS[7]
        sm, msb, gst, gpt = small.items[l], modsb.items[l], gs_t.items[l], gp_t.items[l]
        Rsm = Rsmall.items[l]
        for j in parts:
            for s_ in range(6 * j, 6 * j + 6):
                def slab(s_=s_):
                    b = s_ % 2
                    dma("pool", wm[b][:], wv[:, :, s_ * 512:(s_ + 1) * 512], writes=[Rwm[b]], key=Rwm[b])
                    for m in range(4):
                        mc = s_ * 4 + m
                        mm_group(psm[:, mc * 3:mc * 3 + 3],
                                 [(wm[b][:, kc, m * 128:(m + 1) * 128], csil[:, kc, :]) for kc in range(8)],
                                 reads=[Rwm[b], Rcs], writes=[RPS[7]])
                th.append(slab)

            def fin(j=j):
                Rm = Rmod_lj[l][j]
                psv = psm[:, 72 * j:72 * j + 72].rearrange("p (m s) -> p m s", s=3)
                w_j = 1.0 if j == 1 else 0.5
                for s_ in range(3):
                    tt(msb[:, 24 * j:24 * j + 24, s_], psv[:, :, s_], sm[:, C_BMOD + 24 * j:C_BMOD + 24 * j + 24], ALU.add,
                       reads=[RPS[7], Rsm], writes=[Rm])
                for s_ in range(3):
                    stt(gst[:, j, :, s_], msb[:, j * 24 + 8:j * 24 + 16, s_], 1.0,
                        sm[:, C_NPRE + j * 8:C_NPRE + j * 8 + 8], ALU.add, ALU.mult, reads=[Rm, Rsm], writes=[Rm])
                    stt(gpt[:, j, :, s_], msb[:, j * 24 + 16:j * 24 + 24, s_], w_j,
                        sm[:, C_NPOST + j * 8:C_NPOST + j * 8 + 8], ALU.mult, ALU.mult, reads=[Rm, Rsm], writes=[Rm])
            th.append(fin)
        return th

    def rms_stats(src_fn, ti, n, sq, Rsq, rs, Rrs, srcres):
        act(sq[:, :, :n], src_fn(), AF.Square, reads=srcres, writes=[Rsq])
        mm_group(PS[6][:, :n], [(ones_bf[:], sq[:, c, :n]) for c in range(8)], reads=[Rsq, Rconst], writes=[RPS[6]])
        act(rs[:, :n], PS[6][:, :n], AF.Sqrt, reads=[RPS[6], Rconst], writes=[Rrs], bias=eps_t[:], scale=1.0 / D)
        recip(rs[:, :n], rs[:, :n], reads=[Rrs], writes=[Rrs])

    def segs_in(t0, n):
        out = []
        for (s0, sl, s) in SEGS:
            a, b = max(s0, t0), min(s0 + sl, t0 + n)
            if a < b:
                out.append((a, b - a, s))
        return out

    def norm_pre_steps(j, tis, xn, Rxn, xoff, scr):
        steps = []
        for k_, ti in enumerate(tis):
            sq, Rsq, rs, Rrs, tmp, Rtmp = scr[k_ % len(scr)] if isinstance(scr, list) else scr
            t0, n = TILES[ti]
            steps.append(lambda ti=ti, t0=t0, n=n, sq=sq, Rsq=Rsq, rs=rs, Rrs=Rrs: rms_stats(
                lambda: X[:, :, t0:t0 + n], ti, n, sq, Rsq, rs, Rrs, [RX[ti]]))
            for c in range(8):
                def st(ti=ti, t0=t0, n=n, c=c, rs=rs, Rrs=Rrs, tmp=tmp, Rtmp=Rtmp):
                    b = c % 2
                    tt(tmp[b][:, :n], X[:, c, t0:t0 + n], rs[:, :n], ALU.mult, reads=[RX[ti], Rrs], writes=[Rtmp[b]])
                    for (a, ln, s) in segs_in(t0, n):
                        act(xn[:, c, a - xoff:a - xoff + ln], tmp[b][:, a - t0:a - t0 + ln], AF.Identity,
                            reads=[Rtmp[b], RM(j)], writes=[Rxn[ti]],
                            bias=modsb[:, j * 24 + c, s:s + 1], scale=gs_t[:, j, c, s:s + 1])
                steps.append(st)
        return steps

    def norm_pre(j, tis, xn, Rxn, xoff, scr):
        for st in norm_pre_steps(j, tis, xn, Rxn, xoff, scr):
            st()

    def post_norm_steps(j, tis, yacc, Ry, yoff, scr):
        sq, Rsq, rs, Rrs, tmp, Rtmp = scr
        steps = []
        for ti in tis:
            t0, n = TILES[ti]
            steps.append(lambda ti=ti, t0=t0, n=n: rms_stats(lambda: yacc[:, :, t0 - yoff:t0 - yoff + n], ti, n, sq, Rsq, rs, Rrs,
                                                             [Ry[ti]]))
            for c in range(8):
                def st(ti=ti, t0=t0, n=n, c=c):
                    b = c % 2
                    tt(tmp[b][:, :n], yacc[:, c, t0 - yoff:t0 - yoff + n], rs[:, :n], ALU.mult,
                       reads=[Ry[ti], Rrs], writes=[Rtmp[b]])
                    for (a, ln, s) in segs_in(t0, n):
                        stt(X[:, c, a:a + ln], tmp[b][:, a - t0:a - t0 + ln], gp_t[:, j, c, s:s + 1],
                            X[:, c, a:a + ln], ALU.mult, ALU.add,
                            reads=[Rtmp[b], RM(j), RX[ti]], writes=[RX[ti]])
                steps.append(st)
        return steps

    def post_norm(j, tis, yacc, Ry, yoff, scr):
        for st in post_norm_steps(j, tis, yacc, Ry, yoff, scr):
            st()

    def skewed(steps, per=9):
        groups = [steps[i:i + per] for i in range(0, len(steps), per)]
        out = []
        for g, grp in enumerate(groups):
            if g == 0:
                out.append(grp[0])
            if g + 1 < len(groups):
                out.append(groups[g + 1][0])
            out.extend(grp[1:])
        return out

    def interleave(main, side):
        nm, ns = len(main), len(side)
        k = 0
        for i, m in enumerate(main):
            m()
            tgt = (i + 1) * ns // max(nm, 1)
            while k < tgt:
                side[k]()
                k += 1
        while k < ns:
            side[k]()
            k += 1

    def alloc_norm_scratch():
        sq = sb.alloc("sq", [128, 8, 512], BF16)
        rs = sb.alloc("rs", [128, 512], F32)
        tmp = [sb.alloc("ntmp", [128, 512], F32) for _ in range(2)]
        return (sq, Res("sq"), rs, Res("rs"), tmp, [Res("ntmp0"), Res("ntmp1")])

    def ffn(l, wi, j):
        m0 = sb.mark()
        halves = [[0, 1], [2, 3, 4]]
        NT = 1088
        scr = alloc_norm_scratch()
        xn = sb.alloc("xn", [128, 8, NT], BF16)
        yacc = sb.alloc("yacc", [128, 8, NT], F32)
        hb = [sb.alloc("hb", [128, 4, NT], BF16) for _ in range(2)]
        wg = [sb.alloc("wg", [128, 8, 512], BF16) for _ in range(2)]
        wu = [sb.alloc("wu", [128, 8, 512], BF16) for _ in range(2)]
        wd = [sb.alloc("wd", [128, 4, 1024], BF16) for _ in range(2)]
        sg = [sb.alloc("sg", [128, 512], BF16) for _ in range(2)]
        Rxs_ = [Res(f"xn_s{p}") for p in range(3)]
        Rys_ = [Res(f"y_s{p}") for p in range(3)]
        Rhs_ = [[Res(f"h{b}_s{p}") for p in range(3)] for b in range(2)]
        Rwgu = [Res("wgu0"), Res("wgu1")]
        Rwd = [Res("wd0"), Res("wd1")]
        Rsg = [Res("sg0"), Res("sg1")]
        wgv = wg_d[l, wi].rearrange("(kc p) n -> p kc n", p=128)
        wuv = wu_d[l, wi].rearrange("(kc p) n -> p kc n", p=128)
        wdv = wd_d[l, wi].rearrange("(jc p) n -> p jc n", p=128)
        mch = [4, 4, 4, 4, 4, 2]
        t_lo = [TILES[h[0]][0] for h in halves]
        Rxn = [{ti: Rxs_[p] for p, ti in enumerate(h)} for h in halves]
        Ry = [{ti: Rys_[p] for p, ti in enumerate(h)} for h in halves]

        def load_gu(v):
            s_, b = v % 6, v % 2
            w = mch[s_] * 128
            dma("pool", wg[b][:, :, 0:w], wgv[:, :, s_ * 512:s_ * 512 + w], writes=[Rwgu[b]], key=Rwgu[b])
            dma("pool", wu[b][:, :, 0:w], wuv[:, :, s_ * 512:s_ * 512 + w], writes=[Rwgu[b]], key=Rwgu[b])

        def load_d(v):
            s_, b = v % 6, v % 2
            dma("pool", wd[b][:, 0:mch[s_], :], wdv[:, s_ * 4:s_ * 4 + mch[s_], :], writes=[Rwd[b]], key=Rwd[b])

        cnt = [0]
        dcnt = [0]

        def gu_units(v):
            hf, s_, b = v // 6, v % 6, v % 2
            units = []
            for p, ti in enumerate(halves[hf]):
                t0, n = TILES[ti]
                o = t0 - t_lo[hf]
                for m in range(mch[s_]):
                    def unit(p=p, n=n, o=o, m=m, b=b):
                        g = cnt[0] % 2
                        cnt[0] += 1
                        pg, pu = PS[g], PS[2 + g]
                        mm_group(pg[:, :n], [(wg[b][:, kc, m * 128:(m + 1) * 128], xn[:, kc, o:o + n]) for kc in range(8)],
                                 reads=[Rwgu[b], Rxs_[p]], writes=[RPS[g]])
                        mm_group(pu[:, :n], [(wu[b][:, kc, m * 128:(m + 1) * 128], xn[:, kc, o:o + n]) for kc in range(8)],
                                 reads=[Rwgu[b], Rxs_[p]], writes=[RPS[2 + g]])
                        act(sg[g][:, :n], pg[:, :n], AF.Silu, reads=[RPS[g]], writes=[Rsg[g]])
                        tt(hb[b][:, m, o:o + n], sg[g][:, :n], pu[:, :n], ALU.mult,
                           reads=[Rsg[g], RPS[2 + g]], writes=[Rhs_[b][p]])
                    units.append(unit)
            return units

        def down_units(v):
            hf, s_, b = v // 6, v % 6, v % 2
            units = []
            for p, ti in enumerate(halves[hf]):
                t0, n = TILES[ti]
                o = t0 - t_lo[hf]
                for oc in range(8):
                    def unit(p=p, n=n, o=o, oc=oc, b=b, s_=s_):
                        g = dcnt[0] % 2
                        dcnt[0] += 1
                        pd = PS[4 + g]
                        mm_group(pd[:, :n], [(wd[b][:, m, oc * 128:(oc + 1) * 128], hb[b][:, m, o:o + n]) for m in range(mch[s_])],
                                 reads=[Rwd[b], Rhs_[b][p]], writes=[RPS[4 + g]])
                        if s_ == 0:
                            act(yacc[:, oc, o:o + n], pd[:, :n], AF.Identity, reads=[RPS[4 + g]], writes=[Rys_[p]])
                        else:
                            tt(yacc[:, oc, o:o + n], yacc[:, oc, o:o + n], pd[:, :n], ALU.add,
                               reads=[RPS[4 + g], Rys_[p]], writes=[Rys_[p]])
                    units.append(unit)
            return units

        def run(lst):
            for u in lst:
                u()

        load_gu(0)
        load_d(0)
        load_gu(1)
        load_d(1)
        norm_pre(j, halves[0], xn, Rxn[0], t_lo[0], scr)
        NV = 12
        for v in range(NV):
            if v == 6:
                interleave(gu_units(v), post_norm_steps(j, halves[0], yacc, Ry[0], t_lo[0], scr))
            else:
                run(gu_units(v))
            if v + 2 < NV:
                load_gu(v + 2)
            if v == 5:
                pre_b = norm_pre_steps(j, halves[1], xn, Rxn[1], t_lo[1], scr)
                h1 = len(pre_b) // 2
                interleave(down_units(4), pre_b[:h1])
                load_d(6)
                interleave(down_units(5), pre_b[h1:])
                load_d(7)
            elif v >= 1 and v != 6:
                run(down_units(v - 1))
                if v + 1 < NV:
                    load_d(v + 1)
        run(down_units(NV - 1))
        post_norm(j, halves[1], yacc, Ry[1], t_lo[1], scr)
        P.barrier()
        sb.release(m0)

    def fox(l, ymix, Rym, xn, Rxn, winv):
        ms, mx = sb.mark(), sbx.mark()
        NB = 18
        blocks = [(tb * 128, 128) for tb in range(16)] + [(2048, 32), (2080, 32)]
        vaug = sb.alloc("vaug", [128, NB, 8, 66], BF16)
        Rva = Res("vaug")
        memset(vaug[:, :, :, 64:65], 1.0, writes=[Rva])
        ones_r = sb.alloc("ones_r", [128, 64], F32)
        memset(ones_r[:], 1.0, writes=[Rva])
        qs = sb.alloc("qs", [70, 8, 64], BF16)
        ks = sb.alloc("ks", [70, 8, 64], BF16)
        Rqs = Res("qs")
        fend = sb.alloc("fend", [8, 2], F32)
        nfb = sb.alloc("nfb", [8, 1], F32)
        Rfend = Res("fend")
        m_f = sb.mark()
        LG = sb.alloc("LG", [8, T], F32)
        FT = sb.alloc("FT", [128, T], F32)
        SP1 = sb.alloc("SP1", [128, T], BF16)
        SP2 = sb.alloc("SP2", [128, T], BF16)
        CLF = sb.alloc("CLF", [8, 2050], F32)
        RLG, RFT, RFR, RSP1, RSP2 = Res("LG"), Res("FT"), Res("FR"), Res("SP1"), Res("SP2")
        RCLF = Res("CLF")
        qa = [sbx.alloc("qa", [70, T], BF16) for _ in range(4)]
        ka = [sbx.alloc("ka", [70, T], BF16) for _ in range(4)]
        Rqa = [Res(f"qa{i}") for i in range(4)]
        Rka = [Res(f"ka{i}") for i in range(4)]
        wq = sbx.alloc("wq", [128, 8, 256], BF16)
        wk = sbx.alloc("wk", [128, 8, 256], BF16)
        wv = sbx.alloc("wv", [128, 8, 512], BF16)
        wf = sbx.alloc("wf", [128, 8, 8], BF16)
        Rwq, Rwk, Rwv, Rwf = Res("wq"), Res("wk"), Res("wv"), Res("wf")
        kst = [sbx.alloc("kst", [128, 512], F32) for _ in range(2)]
        vst = [sbx.alloc("vst", [128, 512], F32) for _ in range(2)]
        Rkst = [Res("kst0"), Res("kst1")]
        Rvst = [Res("vst0"), Res("vst1")]
        pbuf = [sbx.alloc("pbuf", [128, 512], BF16) for _ in range(4)]
        Rpb = [Res(f"pb{i}") for i in range(4)]
        rcb = sbx.alloc("rcb", [128, 512], F32)
        bcs = sbx.alloc("bcs", [64, 512], F32)
        Rrcb, Rbcs = Res("rcb"), Res("bcs")

        if FOXS <= -2:
            P.barrier()
            sb.release(ms)
            sbx.release(mx)
            return
        dma("pool", wf[:], winv[:, :, 2048:2056], writes=[Rwf], key=Rwf)
        tsc(nfb[:], headp[0:8, 0:1], -1.0, None, ALU.mult, None, reads=[Rsmall], writes=[Rfend])
        for ti, (t0, n) in enumerate(TILES):
            mm_group(PS[3][0:8, :n], [(wf[:, kc, :], xn[:, kc, t0:t0 + n]) for kc in range(8)],
                     reads=[Rwf, Rxn[ti]], writes=[RPS[3]])
            act(LG[0:8, t0:t0 + n], PS[3][0:8, :n], AF.Exp, reads=[RPS[3], Rfend], writes=[RLG], bias=nfb[:], scale=-1.0)
        act(LG[:], LG[:], AF.Ln, reads=[RLG, Rconst], writes=[RLG], bias=one_t[0:8])
        tsc(LG[:], LG[:], -1.0, None, ALU.mult, None, reads=[RLG], writes=[RLG])
        dma("sp", lf_o[l], LG[:], reads=[RLG], key=RLG)
        ones8 = lambda n: one_t[0:8, 0:1].to_broadcast([8, n])
        scan(FT[0:8, 0:NPR], ones8(NPR), LG[0:8, 0:NPR], 0.0, reads=[RLG, Rconst], writes=[RFT])
        for si in range(2):
            for hf in range(2):
                dma("sp", CLF[:, 0:2048], clfT_d[l, si, :, hf * 2048:(hf + 1) * 2048], writes=[RCLF], key=RCLF)
                P.op("dve", (lambda hf=hf: (lambda e: e.tensor_reduce(
                    out=CLF[:, 2048 + hf:2049 + hf], in_=CLF[:, 0:2048], axis=mybir.AxisListType.X, op=ALU.add)))(),
                    reads=[RCLF], writes=[RCLF])
            tt(fend[:, si:si + 1], CLF[:, 2048:2049], CLF[:, 2049:2050], ALU.add, reads=[RCLF], writes=[Rfend])
            scan(FT[0:8, NPR + si * 32:NPR + si * 32 + 32], ones8(32), LG[0:8, NPR + si * 32:NPR + si * 32 + 32],
                 fend[:, si:si + 1], reads=[RLG, Rconst, Rfend], writes=[RFT])
        split3(FT[0:8, :], FT[32:40, :], FT[64:72, :], SP1[0:8, :], SP1[32:40, :], SP1[64:72, :], RFT, RFR, RSP1)
        for q_ in (0, 32, 64):
            tsc(SP2[q_:q_ + 8, :], SP1[q_:q_ + 8, :], -1.0, None, ALU.mult, None, reads=[RSP1], writes=[RSP2])

        if FOXS <= -1:
            P.barrier()
            sb.release(ms)
            sbx.release(mx)
            return
        scnt = [0]
        ocnt = [0]
        deferred = []
        for hg in range(1 if "f" in TOG else 2):
            dma("pool", wq[:], winv[:, :, 512 + hg * 256:512 + hg * 256 + 256], writes=[Rwq], key=Rwq)
            dma("pool", wk[:], winv[:, :, 1024 + hg * 256:1024 + hg * 256 + 256], writes=[Rwk], key=Rwk)
            if hg == 0 and "e" not in TOG:
                dma("pool", wv[:], winv[:, :, 1536:2048], writes=[Rwv], key=Rwv)
            for hl in range(0 if "a" in TOG else 4):
                memset(qa[hl][64:70, :], 1.0, writes=[Rqa[hl]])
                memset(ka[hl][64:70, :], 1.0, writes=[Rka[hl]])
            for m in range(0 if "d" in TOG else 2):
                for ti, (t0, n) in enumerate(TILES):
                    iq, ik = ti % 2, 2 + ti % 2
                    psq, psk = PS[iq], PS[ik]
                    b = ti % 2
                    mm_group(psq[:, :n], [(wq[:, kc, m * 128:(m + 1) * 128], xn[:, kc, t0:t0 + n]) for kc in range(8)],
                             reads=[Rwq, Rxn[ti]], writes=[RPS[iq]])
                    act(qa[2 * m][0:64, t0:t0 + n], psq[0:64, :n], AF.Identity, reads=[RPS[iq]], writes=[Rqa[2 * m]], scale=0.125)
                    tsc(qa[2 * m + 1][0:64, t0:t0 + n], psq[64:128, :n], 0.125, None, ALU.mult, None,
                        reads=[RPS[iq]], writes=[Rqa[2 * m + 1]])
                    mm_group(psk[:, :n], [(wk[:, kc, m * 128:(m + 1) * 128], xn[:, kc, t0:t0 + n]) for kc in range(8)],
                             reads=[Rwk, Rxn[ti]], writes=[RPS[ik]])
                    act(ka[2 * m][0:64, t0:t0 + n], psk[0:64, :n], AF.Identity, reads=[RPS[ik]], writes=[Rka[2 * m]])
                    cp(ka[2 * m + 1][0:64, t0:t0 + n], psk[64:128, :n], reads=[RPS[ik]], writes=[Rka[2 * m + 1]])
                    if "c" not in TOG:
                        cp(kst[b][:, :n], psk[:, :n], reads=[RPS[ik]], writes=[Rkst[b]])
                        if "g" in TOG:
                            dma("sp", dbg_o[:, hg * 2 + m, t0:t0 + n], kst[b][:, :n], reads=[Rkst[b]], key=Rkst[b])
                        else:
                            dma("sp", kT_o[l, :, hg * 2 + m, t0:t0 + n], kst[b][:, :n], reads=[Rkst[b]], key=Rkst[b])
            if hg == 0 and FOXS >= 0 and "b" not in TOG:
                for tb, (k0, nb) in enumerate(blocks):
                    b = tb % 2
                    iv = 6 + tb % 2
                    psv = PS[iv]
                    mm_group(psv[0:nb, :], [(xn[:, kc, k0:k0 + nb], wv[:, kc, :]) for kc in range(8)],
                             reads=[Rwv] + [Rxn[i] for i in range(5)], writes=[RPS[iv]])
                    if "i" not in TOG:
                        cp(vaug[0:nb, tb, :, 0:64], psv[0:nb, :].rearrange("p (h d) -> p h d", d=64),
                           reads=[RPS[iv]], writes=[Rva])
                    if "h" not in TOG:
                        cp(vst[b][0:nb, :], psv[0:nb, :], reads=[RPS[iv]], writes=[Rvst[b]])
                        dma("sp", v_o[l, k0:k0 + nb, :], vst[b][0:nb, :], reads=[Rvst[b]], key=Rvst[b])
            for hl in range(4 if FOXS >= 1 else 0):
                h = hg * 4 + hl
                dma("sp", qa[hl][64:67, :], SP1[h:h + 65:32, :], reads=[RSP1], writes=[Rqa[hl]], key=Rqa[hl])
                dma("sp", ka[hl][67:70, :], SP2[h:h + 65:32, :], reads=[RSP2], writes=[Rka[hl]], key=Rka[hl])
            for hl in range(4 if FOXS >= 2 else 0):
                h = hg * 4 + hl
                for Q in range(4):
                    og = 4 + (ocnt[0] % 2)
                    ocnt[0] += 1
                    po = PS[og]
                    nkb = 4 * Q + 4
                    LOOK = 2
                    pend = []
                    for kb in range(nkb + LOOK):
                        if kb < nkb:
                            d = kb - 4 * Q
                            col0 = max(d, 0) * 128
                            g = scnt[0] % 4
                            scnt[0] += 1
                            pS = PS[g]

                            def fnS(e, pS=pS, col0=col0, d=d, hl=hl, kb=kb, Q=Q):
                                ins = e.matmul(pS[:, col0:512], lhsT=ka[hl][0:70, kb * 128:(kb + 1) * 128],
                                               rhs=qa[hl][0:70, Q * 512 + col0:Q * 512 + 512], start=True, stop=(d < 0))
                                if d >= 0:
                                    ins = e.matmul(pS[:, col0:col0 + 128], lhsT=ident_bf[:], rhs=maskneg[:], start=False, stop=True)
                                return ins
                            P.op("pe", fnS, reads=[Rka[hl], Rqa[hl], Rconst], writes=[RPS[g]])
                            act(pbuf[g][:, col0:512], pS[:, col0:512], AF.Exp, reads=[RPS[g]], writes=[Rpb[g]])
                            pend.append((kb, g, col0))
                            keep_warm(7, NWARM)
                            if kb == 1:
                                while deferred:
                                    deferred.pop(0)()
                        if kb >= LOOK:
                            kb_, g_, c0_ = pend.pop(0)

                            def fnV(e, kb_=kb_, g_=g_, c0_=c0_, po=po, h=h, nkb=nkb):
                                return e.matmul(po[0:65, c0_:512], lhsT=vaug[:, kb_, h, 0:65], rhs=pbuf[g_][:, c0_:512],
                                                start=(kb_ == 0), stop=(kb_ == nkb - 1))
                            P.op("pe", fnV, reads=[Rva, Rpb[g_]], writes=[RPS[og]])
                    recip(rcb[0:1, :], po[64:65, :], reads=[RPS[og]], writes=[Rrcb])

                    def epilogue(po=po, og=og, h=h, Q=Q):
                        mm_group(PS[6][0:64, :], [(ones_r[0:1, 0:64], rcb[0:1, :])], reads=[Rrcb, Rva], writes=[RPS[6]])
                        act(bcs[:, :], PS[6][0:64, :], AF.Identity, reads=[RPS[6]], writes=[Rbcs])
                        tt(ymix[(h % 2) * 64:(h % 2) * 64 + 64, 2 + h // 2, Q * 512:Q * 512 + 512], po[0:64, :], bcs[:, :],
                           ALU.mult, reads=[RPS[og], Rbcs], writes=[Rym[2 + h // 2]])
                    deferred.append(epilogue)
            while deferred:
                deferred.pop(0)()
            for hl in range(4 if FOXS >= 1 else 0):
                h = hg * 4 + hl
                cp(qs[0:70, h, :], qa[hl][0:70, NPR:T], reads=[Rqa[hl]], writes=[Rqs])
                cp(ks[0:70, h, :], ka[hl][0:70, NPR:T], reads=[Rka[hl]], writes=[Rqs])
        P.barrier()
        sb.release(m_f)
        if FOXS < 3:
            sb.release(ms)
            sbx.release(mx)
            return
        FC = sb.alloc("FC", [128, PAST], F32)
        SPC = sb.alloc("SPC", [128, PAST], BF16)
        RFC, RFCR, RSPC = Res("FC"), Res("FCR"), Res("SPC")
        kcb = [sb.alloc("kcb", [70, PAST], BF16) for _ in range(2)]
        vcb = [sb.alloc("vcb", [128, 2048], BF16) for _ in range(2)]
        rsum = sb.alloc("rsum", [1, 64], F32)
        Rrsum = Res("rsum")
        Rkc = [Res("kc0"), Res("kc1")]
        Rkcr = [Res("kcr0"), Res("kcr1")]
        Rvc = [Res("vc0"), Res("vc1")]
        for b in range(2):
            memset(kcb[b][64:70, :], 1.0, writes=[Rkc[b]])
        for si in range(2):
            dma("sp", FC[64:72, :], clfT_d[l, si], writes=[RFCR], key=RFCR)
            scan(FC[0:8, :], one_t[64:72, 0:1].to_broadcast([8, PAST]), FC[64:72, :], 0.0, reads=[RFCR, Rconst], writes=[RFC])
            split3(FC[0:8, :], FC[32:40, :], FC[64:72, :], SPC[0:8, :], SPC[32:40, :], SPC[64:72, :], RFC, RFCR, RSPC)
            for q_ in (0, 32, 64):
                tsc(SPC[q_:q_ + 8, :], SPC[q_:q_ + 8, :], -1.0, None, ALU.mult, None, reads=[RSPC], writes=[RSPC])
            for h in range(8):
                b = h % 2
                dma("pool", kcb[b][0:64, :], ckT_d[l, si, h], writes=[Rkc[b]], key=Rkc[b])
                dma("sp", kcb[b][67:70, :], SPC[h:h + 65:32, :], reads=[RSPC], writes=[Rkc[b]], key=Rkcr[b])
                dma("pool", vcb[b][:, :], cv_d[l, si, h], writes=[Rvc[b]], key=Rvc[b])
                qcol = qs[0:70, h, si * 32:si * 32 + 32]
                for hf in range(2):
                    pS = PS[hf]

                    def fnC(e, pS=pS, hf=hf, b=b, qcol=qcol):
                        ins = None
                        for bb in range(16):
                            kb = hf * 16 + bb
                            ins = e.matmul(pS[:, bb * 32:bb * 32 + 32], lhsT=kcb[b][0:70, kb * 128:(kb + 1) * 128], rhs=qcol,
                                           start=True, stop=True)
                        return ins
                    P.op("pe", fnC, reads=[Rkc[b], Rqs], writes=[RPS[hf]])
                    act(pbuf[hf][:, :], pS[:, :], AF.Exp, reads=[RPS[hf]], writes=[Rpb[hf]])
                kcol = ks[0:70, h, si * 32:si * 32 + 32]

                def fnN(e, kcol=kcol, qcol=qcol):
                    e.matmul(PS[2][0:32, 0:32], lhsT=kcol, rhs=qcol, start=True, stop=False)
                    return e.matmul(PS[2][0:32, 0:32], lhsT=ident_bf[0:32, 0:32], rhs=maskneg[0:32, 0:32], start=False, stop=True)
                P.op("pe", fnN, reads=[Rqs, Rconst], writes=[RPS[2]])
                act(pbuf[2][0:32, 0:32], PS[2][0:32, 0:32], AF.Exp, reads=[RPS[2]], writes=[Rpb[2]])
                og = 4 + h % 2
                po = PS[og]
                vc4 = vcb[b]

                def fnPV(e, po=po, vc4=vc4, h=h, si=si):
                    for kb in range(32):
                        e.matmul(po[0:64, 0:32], lhsT=vc4[:, kb * 64:(kb + 1) * 64], rhs=pbuf[kb // 16][:, (kb % 16) * 32:(kb % 16) * 32 + 32],
                                 start=(kb == 0), stop=False)
                    return e.matmul(po[0:64, 0:32], lhsT=vaug[0:32, 16 + si, h, 0:64], rhs=pbuf[2][0:32, 0:32], start=False, stop=True)
                P.op("pe", fnPV, reads=[Rvc[b], Rva, Rpb[0], Rpb[1], Rpb[2]], writes=[RPS[og]])

                def fnSum(e):
                    e.matmul(PS[3][0:1, :], lhsT=ones_bf[:, 0:1], rhs=pbuf[0][:, :], start=True, stop=False)
                    e.matmul(PS[3][0:1, :], lhsT=ones_bf[:, 0:1], rhs=pbuf[1][:, :], start=False, stop=True)
                    return e.matmul(PS[7][0:1, 0:32], lhsT=ones_bf[0:32, 0:1], rhs=pbuf[2][0:32, 0:32], start=True, stop=True)
                P.op("pe", fnSum, reads=[Rconst, Rpb[0], Rpb[1], Rpb[2]], writes=[RPS[3], RPS[7]])
                P.op("dve", lambda e: e.tensor_reduce(out=rsum[0:1, 0:32], in_=PS[3][0:1, :].rearrange("p (b q) -> p q b", q=32),
                                                      axis=mybir.AxisListType.X, op=ALU.add), reads=[RPS[3]], writes=[Rrsum])
                tt(rsum[0:1, 32:64], rsum[0:1, 0:32], PS[7][0:1, 0:32], ALU.add, reads=[Rrsum, RPS[7]], writes=[Rrsum])
                recip(rcb[0:1, 0:32], rsum[0:1, 32:64], reads=[Rrsum], writes=[Rrcb])
                mm_group(PS[6][0:64, 0:32], [(ones_r[0:1, 0:64], rcb[0:1, 0:32])], reads=[Rrcb, Rva], writes=[RPS[6]])
                act(bcs[:, 0:32], PS[6][0:64, 0:32], AF.Identity, reads=[RPS[6]], writes=[Rbcs])
                tt(ymix[(h % 2) * 64:(h % 2) * 64 + 64, 2 + h // 2, NPR + si * 32:NPR + si * 32 + 32], po[0:64, 0:32], bcs[:, 0:32],
                   ALU.mult, reads=[RPS[og], Rbcs], writes=[Rym[2 + h // 2]])
        P.barrier()
        sb.release(ms)
        sbx.release(mx)

    def ssd(l, ymix, Rym, xn, Rxn, winv):
        ms, mx = sb.mark(), sbx.mark()
        NB = 18
        blocks = [(tb * 128, 128) for tb in range(16)] + [(2048, 32), (2080, 32)]
        seq_blocks = {0: list(range(16)), 1: [16], 2: [17]}
        xs = sb.alloc("xs", [128, 2, T], F32)
        BT = sb.alloc("BT", [128, 2, T], BF16)
        CT = sb.alloc("CT", [128, 2, T], BF16)
        DT = sb.alloc("DT", [128, T], F32)
        ET = sb.alloc("ET", [128, T], F32)
        S1 = sb.alloc("S1", [128, T], BF16)
        dctok = sb.alloc("dctok", [128, NB, 12], F32)
        at = sb.alloc("at", [4, 2], F32)
        ecd = sb.alloc("ecd", [4, 2, 4], F32)
        ecb = sb.alloc("ecb", [64, 2, 4], F32)
        Rxs = [Res("xs0"), Res("xs1")]
        RBT, RCT, RDT, RET, RS1, Rdc, Rat = [Res(n) for n in "BT CT DT ET S1 dctok at".split()]
        mconv = sbx.mark()
        wx1 = sbx.alloc("wx1", [128, 8, 512], BF16)
        wx2 = sbx.alloc("wx2", [128, 8, 260], BF16)
        Rwx1, Rwx2 = Res("wx1"), Res("wx2")
        dma("pool", wx1[:], winv[:, :, 2312:2824], writes=[Rwx1], key=Rwx1)
        dma("pool", wx2[:], winv[:, :, 2824:3084], writes=[Rwx2], key=Rwx2)
        xp6 = [sbx.alloc("xp6", [128, XPW], F32) for _ in range(6)]
        mu6 = sb.mark()
        u6 = [sb.alloc("u6", [128, T], F32)] * 2
        Rxp6 = [Res(f"xp6_{i}") for i in range(6)]
        Ru6 = [Res("u6a")] * 2
        for ci in range(6):
            xp = xp6[ci]
            memset(xp[:, 0:3], 0.0, writes=[Rxp6[ci]])
            dma("sp", xp[:, 2051:2054], sconv_d[l, :, 0, ci, :], writes=[Rxp6[ci]], key=Rxp6[ci])
            dma("sp", xp[:, 2086:2089], sconv_d[l, :, 1, ci, :], writes=[Rxp6[ci]], key=Rxp6[ci])
            wsl, Rw, col = (wx1, Rwx1, ci * 128) if ci < 4 else (wx2, Rwx2, (ci - 4) * 128)
            for ti, (t0, n) in enumerate(TILES):
                g = (ci * 5 + ti) % 4
                mm_group(PS[g][:, :n], [(wsl[:, kc, col:col + 128], xn[:, kc, t0:t0 + n]) for kc in range(8)],
                         reads=[Rw, Rxn[ti]], writes=[RPS[g]])
                if ti < 4:
                    if ti % 2 == 0:
                        act(xp[:, 3 + t0:3 + t0 + n], PS[g][:, :n], AF.Identity, reads=[RPS[g]], writes=[Rxp6[ci]])
                    else:
                        cp(xp[:, 3 + t0:3 + t0 + n], PS[g][:, :n], reads=[RPS[g]], writes=[Rxp6[ci]])
                else:
                    act(xp[:, 2054:2086], PS[g][:, 0:32], AF.Identity, reads=[RPS[g]], writes=[Rxp6[ci]])
                    cp(xp[:, 2089:2121], PS[g][:, 32:64], reads=[RPS[g]], writes=[Rxp6[ci]])
        for ci in range(6):
            b = ci % 2
            xp, uu = xp6[ci], u6[b]
            cw = lambda k: small[:, C_SCW + ci * 4 + k:C_SCW + ci * 4 + k + 1]
            cb = small[:, C_SCB + ci:C_SCB + ci + 1]
            for (s0, sl, s) in SEGS:
                p0 = POFF[s]
                tsc(uu[:, s0:s0 + sl], xp[:, p0:p0 + sl], cw(0), cb, ALU.mult, ALU.add, reads=[Rxp6[ci], Rsmall], writes=[Ru6[b]])
                for k in range(1, 4):
                    stt(uu[:, s0:s0 + sl], xp[:, p0 + k:p0 + k + sl], cw(k), uu[:, s0:s0 + sl], ALU.mult, ALU.add,
                        reads=[Rxp6[ci], Rsmall, Ru6[b]], writes=[Ru6[b]])
                dma("sp", sconv_o[l, :, s, ci, :], xp[:, p0 + sl:p0 + sl + 3], reads=[Rxp6[ci]], key=Rxp6[ci])
            if ci < 2:
                act(xs[:, ci, :], uu[:], AF.Silu, reads=[Ru6[b]], writes=[Rxs[ci]])
            elif ci < 4:
                act(BT[:, ci - 2, :], uu[:], AF.Silu, reads=[Ru6[b]], writes=[RBT])
            else:
                act(CT[:, ci - 4, :], uu[:], AF.Silu, reads=[Ru6[b]], writes=[RCT])
        for ti, (t0, n) in enumerate(TILES):
            mm_group(PS[2][0:4, :n], [(wx2[:, kc, 256:260], xn[:, kc, t0:t0 + n]) for kc in range(8)],
                     reads=[Rwx2, Rxn[ti]], writes=[RPS[2]])
            act(DT[0:4, t0:t0 + n], PS[2][0:4, :n], AF.Exp, reads=[RPS[2], Rsmall], writes=[RDT], bias=headp[0:4, 1:2])
        act(DT[0:4, :], DT[0:4, :], AF.Ln, reads=[RDT, Rconst], writes=[RDT], bias=one_t[0:4])
        act(at[:, 0:1], headp[0:4, 2:3], AF.Exp, reads=[Rsmall], writes=[Rat])
        tsc(at[:, 1:2], at[:, 0:1], -1.0, None, ALU.mult, None, reads=[Rat], writes=[Rat])
        tsc(DT[32:36, :], DT[0:4, :], at[:, 1:2], None, ALU.mult, None, reads=[RDT, Rat], writes=[RDT])
        for (s0, sl, s) in SEGS:
            scan(DT[64:68, s0:s0 + sl], one_t[32:36, 0:1].to_broadcast([4, sl]), DT[32:36, s0:s0 + sl], 0.0,
                 reads=[RDT, Rconst], writes=[RDT])
            act(ET[64:68, s0:s0 + sl], DT[64:68, s0:s0 + sl], AF.Exp, reads=[RDT], writes=[RET],
                bias=DT[64:68, s0 + sl - 1:s0 + sl], scale=-1.0)
        split3(DT[64:68, :], ET[0:4, :], ET[32:36, :], S1[64:68, :], S1[0:4, :], S1[32:36, :], RDT, RET, RS1)
        for si in range(2):
            last = NPR + si * 32 + 31
            act(at[:, 0:1], DT[64:68, last:last + 1], AF.Exp, reads=[RDT, Rat], writes=[Rat])
            tsc(ecd[:, si, :], ident_f[0:4, 0:4], at[:, 0:1], None, ALU.mult, None, reads=[Rat, Rconst], writes=[Rat])
            mm_group(PS[3][0:64, si * 4:si * 4 + 4], [(ones_f4[0:4, 0:64], ecd[:, si, :])], reads=[Rat, Rconst], writes=[RPS[3]])
        cp(ecb[:].rearrange("p a b -> p (a b)"), PS[3][0:64, 0:8], reads=[RPS[3]], writes=[Rat])
        P.barrier()
        sbx.release(mconv)
        sb.release(mu6)
        xtok = sbx.alloc("xtok", [128, NB, 256], BF16)
        xwtok = sbx.alloc("xwtok", [128, NB, 256], BF16)
        Btok = sbx.alloc("Btok", [128, NB, 256], BF16)
        Rxt, Rxw, RBt = Res("xtok"), Res("xwtok"), Res("Btok")
        aq = [sbx.alloc("aq", [6, T], BF16) for _ in range(4)]
        ak = [sbx.alloc("ak", [6, T], BF16) for _ in range(4)]
        Raq = [Res(f"aq{h}") for h in range(4)]
        Rak = [Res(f"ak{h}") for h in range(4)]
        Lb = [sbx.alloc("Lb", [128, 512], BF16) for _ in range(3)]
        Gb = [sb.alloc("Gb", [128, 512], BF16) for _ in range(8)]
        RL = [Res(f"L{i}") for i in range(3)]
        RG = [Res(f"G{i}") for i in range(8)]
        for h in range(4):
            memset(aq[h][:], 1.0, writes=[Raq[h]])
            memset(ak[h][:], 1.0, writes=[Rak[h]])
            dma("sp", aq[h][3:6, :], S1[h:h + 65:32, :], reads=[RS1], writes=[Raq[h]], key=Raq[h])
            dma("sp", ak[h][0:3, :], S1[h:h + 65:32, :], reads=[RS1], writes=[Rak[h]], key=Rak[h])
            tsc(ak[h][0:3, :], ak[h][0:3, :], -1.0, None, ALU.mult, None, reads=[Rak[h]], writes=[Rak[h]])
        PSb = [PS[i][:].bitcast(BF16) for i in range(8)]
        for tb, (k0, nb) in enumerate(blocks):
            g = tb % 2
            P.op("pe", (lambda k0=k0, nb=nb, g=g: (lambda e: e.transpose(out=PS[g][0:nb, 0:68], in_=DT[0:68, k0:k0 + nb],
                                                                       identity=ident_f[0:68, 0:68])))(),
                 reads=[RDT, Rconst], writes=[RPS[g]])
            cp(dctok[0:nb, tb, 0:4], PS[g][0:nb, 0:4], reads=[RPS[g]], writes=[Rdc])
            P.op("pe", (lambda k0=k0, nb=nb, g=g: (lambda e: e.transpose(out=PS[g][0:nb, 128:196], in_=ET[0:68, k0:k0 + nb],
                                                                       identity=ident_f[0:68, 0:68])))(),
                 reads=[RET, Rconst], writes=[RPS[g]])
            cp(dctok[0:nb, tb, 4:8], PS[g][0:nb, 192:196], reads=[RPS[g]], writes=[Rdc])
            tt(dctok[0:nb, tb, 8:12], dctok[0:nb, tb, 0:4], dctok[0:nb, tb, 4:8], ALU.mult, reads=[Rdc], writes=[Rdc])
            for c in range(2):
                g2 = 2 + c
                P.op("pe", (lambda k0=k0, nb=nb, c=c, g2=g2: (lambda e: e.transpose(
                    out=PS[g2][0:nb, 0:128], in_=xs[:, c, k0:k0 + nb], identity=ident_f[:])))(),
                    reads=[Rxs[c], Rconst], writes=[RPS[g2]])
                act(xtok[0:nb, tb, c * 128:(c + 1) * 128], PS[g2][0:nb, 0:128], AF.Identity, reads=[RPS[g2]], writes=[Rxt])
                for hh_ in range(2):
                    h = 2 * c + hh_
                    tsc(xwtok[0:nb, tb, h * 64:(h + 1) * 64], PS[g2][0:nb, hh_ * 64:(hh_ + 1) * 64], dctok[0:nb, tb, 8 + h:9 + h],
                        None, ALU.mult, None, reads=[RPS[g2], Rdc], writes=[Rxw])
            for gI in range(2):
                g3 = 4 + gI
                P.op("pe", (lambda k0=k0, nb=nb, gI=gI, g3=g3: (lambda e: e.transpose(
                    out=PSb[g3][0:nb, 0:128], in_=BT[:, gI, k0:k0 + nb], identity=ident_bf[:])))(),
                    reads=[RBT, Rconst], writes=[RPS[g3]])
                cp(Btok[0:nb, tb, gI * 128:(gI + 1) * 128], PSb[g3][0:nb, 0:128], reads=[RPS[g3]], writes=[RBt])
        cnt = [0]
        for Q in range(4):
            nsb = 4 * Q + 4
            pend = []
            for sbk in range(nsb + 1):
                cur = []
                if sbk < nsb:
                    d = sbk - 4 * Q
                    col0 = max(d, 0) * 128
                    for gI in range(2):
                        pcb = PS[gI]
                        mm_group(pcb[:, col0:512], [(BT[:, gI, sbk * 128:(sbk + 1) * 128], CT[:, gI, Q * 512 + col0:Q * 512 + 512])],
                                 reads=[RBT, RCT], writes=[RPS[gI]])
                        for hh_ in range(2):
                            h = 2 * gI + hh_
                            pe_ = PS[2 + hh_]
                            i3 = cnt[0] % 3
                            i8 = cnt[0] % 8
                            cnt[0] += 1

                            def fnE(e, pe_=pe_, col0=col0, d=d, h=h, sbk=sbk, Q=Q):
                                ins = e.matmul(pe_[:, col0:512], lhsT=ak[h][0:6, sbk * 128:(sbk + 1) * 128],
                                               rhs=aq[h][0:6, Q * 512 + col0:Q * 512 + 512], start=True, stop=(d < 0))
                                if d >= 0:
                                    ins = e.matmul(pe_[:, col0:col0 + 128], lhsT=ident_bf[:], rhs=maskneg[:], start=False, stop=True)
                                return ins
                            P.op("pe", fnE, reads=[Rak[h], Raq[h], Rconst], writes=[RPS[2 + hh_]])
                            act(Lb[i3][:, col0:512], pe_[:, col0:512], AF.Exp, reads=[RPS[2 + hh_]], writes=[RL[i3]])
                            stt(Gb[i8][:, col0:512], pcb[:, col0:512], dctok[:, sbk, h:h + 1], Lb[i3][:, col0:512], ALU.mult, ALU.mult,
                                reads=[RPS[gI], Rdc, RL[i3]], writes=[RG[i8]])
                            cur.append((sbk, h, i8, col0))
                        keep_warm(7 if gI == 0 else 6, NWARM)
                for (sb_, h_, i3_, c0_) in pend:
                    og = 4 + h_ // 2
                    r0 = (h_ % 2) * 64

                    def fnV(e, sb_=sb_, h_=h_, i3_=i3_, c0_=c0_, og=og, r0=r0, nsb=nsb):
                        return e.matmul(PS[og][r0:r0 + 64, c0_:512], lhsT=xtok[:, sb_, h_ * 64:(h_ + 1) * 64], rhs=Gb[i3_][:, c0_:512],
                                        start=(sb_ == 0), stop=(sb_ == nsb - 1))
                    P.op("pe", fnV, reads=[Rxt, RG[i3_]], writes=[RPS[og]])
                pend = cur
            for c in range(2):
                stt(xs[:, c, Q * 512:(Q + 1) * 512], xs[:, c, Q * 512:(Q + 1) * 512], small[:, C_SD + c:C_SD + c + 1], PS[4 + c][:, :],
                    ALU.mult, ALU.add, reads=[Rxs[c], Rsmall, RPS[4 + c]], writes=[Rxs[c]])
        h0T = sb.alloc("h0T", [128, 2, 4, 64], BF16)
        h0f = sb.alloc("h0f", [64, 2, 4, 128], F32)
        hst = [sb.alloc("hst", [64, 128], F32) for _ in range(2)]
        Rh0T, Rh0f = Res("h0T"), Res("h0f")
        Rhst = [Res("hst0"), Res("hst1")]
        dma("pool", h0T[:], sh0_d[l].rearrange("s h n p -> n s h p"), writes=[Rh0T], key=Rh0T)
        dma("sp", h0f[:], sh0f_d[l].rearrange("s h p n -> p s h n"), writes=[Rh0f], key=Rh0f)
        for si in range(2):
            tk = slice(NPR + si * 32, NPR + si * 32 + 32)
            tb = 16 + si
            for h in range(4):
                gI = h // 2
                r0 = (h % 2) * 64
                og = 4 + gI
                mm_group(PS[2][:, 0:32], [(akv[0:6, :], aq[h][0:6, tk])], reads=[Rconst, Raq[h]], writes=[RPS[2]])
                act(Lb[0][:, 0:32], PS[2][:, 0:32], AF.Exp, reads=[RPS[2]], writes=[RL[0]])
                tt(Gb[0][:, 0:32], CT[:, gI, tk], Lb[0][:, 0:32], ALU.mult, reads=[RCT, RL[0]], writes=[RG[0]])
                mm_group(PS[0][0:32, 0:32], [(BT[:, gI, tk], CT[:, gI, tk])], reads=[RBT, RCT], writes=[RPS[0]])

                def fnE2(e, h=h, tk=tk):
                    e.matmul(PS[3][0:32, 0:32], lhsT=ak[h][0:6, tk], rhs=aq[h][0:6, tk], start=True, stop=False)
                    return e.matmul(PS[3][0:32, 0:32], lhsT=ident_bf[0:32, 0:32], rhs=maskneg[0:32, 0:32], start=False, stop=True)
                P.op("pe", fnE2, reads=[Rak[h], Raq[h], Rconst], writes=[RPS[3]])
                act(Lb[1][0:32, 0:32], PS[3][0:32, 0:32], AF.Exp, reads=[RPS[3]], writes=[RL[1]])
                stt(Gb[1][0:32, 0:32], PS[0][0:32, 0:32], dctok[0:32, tb, h:h + 1], Lb[1][0:32, 0:32], ALU.mult, ALU.mult,
                    reads=[RPS[0], Rdc, RL[1]], writes=[RG[1]])

                def fnV2(e, h=h, si=si, tb=tb, og=og, r0=r0):
                    e.matmul(PS[og][r0:r0 + 64, 0:32], lhsT=h0T[:, si, h, :], rhs=Gb[0][:, 0:32], start=True, stop=False)
                    return e.matmul(PS[og][r0:r0 + 64, 0:32], lhsT=xtok[0:32, tb, h * 64:(h + 1) * 64], rhs=Gb[1][0:32, 0:32],
                                    start=False, stop=True)
                P.op("pe", fnV2, reads=[Rh0T, Rxt, RG[0], RG[1]], writes=[RPS[og]])
                stt(xs[r0:r0 + 64, gI, tk], xs[r0:r0 + 64, gI, tk], small[r0:r0 + 64, C_SD + gI:C_SD + gI + 1], PS[og][r0:r0 + 64, 0:32],
                    ALU.mult, ALU.add, reads=[Rxs[gI], Rsmall, RPS[og]], writes=[Rxs[gI]])
        fcnt = 0
        for s in range(3):
            for h in range(4):
                g = 6 + fcnt % 2
                b = fcnt % 2
                fcnt += 1
                mm_group(PS[g][0:64, 0:128],
                         [(xwtok[0:blocks[tb][1], tb, h * 64:(h + 1) * 64], Btok[0:blocks[tb][1], tb, (h // 2) * 128:(h // 2) * 128 + 128])
                          for tb in seq_blocks[s]], reads=[Rxw, RBt], writes=[RPS[g]])
                if s == 0:
                    cp(hst[b][:], PS[g][0:64, 0:128], reads=[RPS[g]], writes=[Rhst[b]])
                else:
                    stt(hst[b][:], h0f[:, s - 1, h, :], ecb[:, s - 1, h:h + 1], PS[g][0:64, 0:128], ALU.mult, ALU.add,
                        reads=[Rh0f, Rat, RPS[g]], writes=[Rhst[b]])
                dma("sp", sh_o[l, s, h], hst[b][:], reads=[Rhst[b]], key=Rhst[b])
        P.barrier()
        sbx.release(mconv)
        wz = sbx.alloc("wz", [128, 8, 256], BF16)
        Rwz = Res("wz")
        dma("pool", wz[:], winv[:, :, 2056:2312], writes=[Rwz], key=Rwz)
        zt = [sbx.alloc("zt", [128, 512], F32) for _ in range(2)]
        Rzt = [Res("zt0"), Res("zt1")]
        sqz = sbx.alloc("sqz", [128, 2, 512], BF16)
        rsz = sbx.alloc("rsz", [128, 512], F32)
        Rsqz, Rrsz = Res("sqz"), Res("rsz")
        for ti, (t0, n) in enumerate(TILES):
            for c in range(2):
                mm_group(PS[c][:, :n], [(wz[:, kc, c * 128:(c + 1) * 128], xn[:, kc, t0:t0 + n]) for kc in range(8)],
                         reads=[Rwz, Rxn[ti]], writes=[RPS[c]])
                act(zt[c][:, :n], PS[c][:, :n], AF.Silu, reads=[RPS[c]], writes=[Rzt[c]])
                tt(xs[:, c, t0:t0 + n], xs[:, c, t0:t0 + n], zt[c][:, :n], ALU.mult, reads=[Rxs[c], Rzt[c]], writes=[Rxs[c]])
                act(sqz[:, c, :n], xs[:, c, t0:t0 + n], AF.Square, reads=[Rxs[c]], writes=[Rsqz])
            mm_group(PS[2][:, :n], [(ones_bf[:], sqz[:, c, :n]) for c in range(2)], reads=[Rsqz, Rconst], writes=[RPS[2]])
            act(rsz[:, :n], PS[2][:, :n], AF.Sqrt, reads=[RPS[2], Rconst], writes=[Rrsz], bias=eps_t[:], scale=1.0 / 256)
            recip(rsz[:, :n], rsz[:, :n], reads=[Rrsz], writes=[Rrsz])
            for c in range(2):
                stt(ymix[:, 6 + c, t0:t0 + n], xs[:, c, t0:t0 + n], small[:, C_SNW + c:C_SNW + c + 1], rsz[:, :n], ALU.mult, ALU.mult,
                    reads=[Rxs[c], Rsmall, Rrsz], writes=[Rym[6 + c]])
        P.barrier()
        sb.release(ms)
        sbx.release(mx)

    sbx = SB(nc, SB.LO, SB.LO + 8 * T * 4)
    Rspill = Res("xspill")
    POFF = {0: 0, 1: 2051, 2: 2086}
    XPW = 2121
    ones_f4 = sb.alloc("ones_f4", [4, 64], F32)
    memset(ones_f4[:], 1.0, writes=[Rconst])
    akv = sb.alloc("akv", [6, 128], BF16)
    memset(akv[:], 1.0, writes=[Rconst])
    P.op("pool", lambda e: e.affine_select(out=akv[:], in_=akv[:], pattern=[[0, 128]], compare_op=ALU.is_ge,
                                           fill=0.0, base=-3, channel_multiplier=1), reads=[Rconst], writes=[Rconst])
    ARENA2 = sb.mark()

    def split3(F_ap, R1_ap, R2_ap, hi_ap, mid_ap, lo_ap, RF, RR, RS):
        cp(hi_ap, F_ap, reads=[RF], writes=[RS])
        tt(R1_ap, F_ap, hi_ap, ALU.subtract, reads=[RF, RS], writes=[RR])
        cp(mid_ap, R1_ap, reads=[RR], writes=[RS])
        tt(R2_ap, R1_ap, mid_ap, ALU.subtract, reads=[RR, RS], writes=[RR])
        cp(lo_ap, R2_ap, reads=[RR], writes=[RS])

    def mix(l):
        j = 1
        m_arena = sb.mark()
        ymix = sb.alloc("ymix", [128, 8, T], BF16)
        Rym = [Res(f"ym{c}") for c in range(8)]
        xn = sb.alloc("xn", [128, 8, T], BF16)
        Rxn = {ti: Res(f"xn{ti}") for ti in range(5)}
        m1 = sb.mark()
        scrs = [alloc_norm_scratch(), alloc_norm_scratch()]
        for st_ in skewed(norm_pre_steps(1, list(range(5)), xn, Rxn, 0, scrs)):
            st_()
        for ti, (t0, n) in enumerate(TILES):
            dma("sp", xspill[:, :, t0:t0 + n], X[:, :, t0:t0 + n], reads=[RX[ti]], writes=[Rspill], key=RX[ti])
        P.barrier()
        sb.release(m1)
        winv = win_d[l].rearrange("(kc p) n -> p kc n", p=128)
        allxn = [Rxn[ti] for ti in range(5)]

        def lru():
            ms, mx = sb.mark(), sbx.mark()
            wl = sb.alloc("wl", [128, 8, 512], BF16)
            Rwl = Res("wl")
            dma("pool", wl[:], winv[:, :, 0:512], writes=[Rwl], key=Rwl)
            h0sb = sb.alloc("h0sb", [128, 2, 2], F32)
            Rh0 = Res("h0")
            dma("sp", h0sb[:], lh_d[l], writes=[Rh0], key=Rh0)
            bd = [[sb.alloc("bd", [128, 128], BF16) for c in range(2)] for g in range(2)]
            Rbd = [[Res("bd") for c in range(2)] for g in range(2)]
            for g, src in ((0, lwa_d), (1, lwx_d)):
                for c in range(2):
                    memset(bd[g][c][:], 0.0, writes=[Rbd[g][c]])
                    for hh_ in range(2):
                        dma("pool", bd[g][c][hh_ * 64:(hh_ + 1) * 64, hh_ * 64:(hh_ + 1) * 64], src[l, 2 * c + hh_],
                            writes=[Rbd[g][c]], key=Rbd[g][c])
            c1t = sb.alloc("c1t", [128, 2, 4], F32)
            Rc1 = Res("c1t")
            wmx = [sb.alloc("wmx", [128, 8, 512], BF16) for _ in range(2)]
            Rwmx = [Res("wmx0"), Res("wmx1")]
            extra = mod_thunks(l, [2], wmx, Rwmx)
            if l + 1 < 2:
                extra += mod_thunks(l + 1, [0, 1], wmx, Rwmx)

            def more(k=1):
                for _ in range(k):
                    if extra:
                        extra.pop(0)()
            xp = sbx.alloc("xp", [128, XPW], F32)
            u = sbx.alloc("u", [128, T], F32)
            ub = sbx.alloc("ub", [128, T], BF16)
            r = sbx.alloc("r", [128, T], F32)
            ig = sbx.alloc("ig", [128, T], F32)
            a = sbx.alloc("a", [128, T], F32)
            t1 = sbx.alloc("t1", [128, T], F32)
            hh = sb.alloc("hh", [128, T], F32)
            gg = sb.alloc("gg", [128, T], F32)
            Rxp, Ru, Rub, Rr, Rig, Ra, Rt1, Rhh, Rgg = [Res(n) for n in "xp u ub r ig a t1 hh gg".split()]
            for c in range(2):
                lam = small[:, C_LLAM + c:C_LLAM + c + 1]
                act(c1t[:, c, 0:1], lam, AF.Exp, reads=[Rsmall], writes=[Rc1], scale=-1.0)
                act(c1t[:, c, 1:2], c1t[:, c, 0:1], AF.Ln, reads=[Rc1, Rconst], writes=[Rc1], bias=one_t[:])
                tsc(c1t[:, c, 2:3], c1t[:, c, 1:2], -8.0, None, ALU.mult, None, reads=[Rc1], writes=[Rc1])
                tsc(c1t[:, c, 3:4], c1t[:, c, 1:2], -16.0, None, ALU.mult, None, reads=[Rc1], writes=[Rc1])
                memset(xp[:, 0:3], 0.0, writes=[Rxp])
                dma("sp", xp[:, 2051:2054], lconv_d[l, :, 0, c, :], writes=[Rxp], key=Rxp)
                dma("sp", xp[:, 2086:2089], lconv_d[l, :, 1, c, :], writes=[Rxp], key=Rxp)
                for ti, (t0, n) in enumerate(TILES):
                    pa, pb = PS[ti % 2], PS[2 + ti % 2]
                    mm_group(pa[:, :n], [(wl[:, kc, c * 128:(c + 1) * 128], xn[:, kc, t0:t0 + n]) for kc in range(8)],
                             reads=[Rwl, Rxn[ti]], writes=[RPS[ti % 2]])
                    mm_group(pb[:, :n], [(wl[:, kc, 256 + c * 128:256 + (c + 1) * 128], xn[:, kc, t0:t0 + n]) for kc in range(8)],
                             reads=[Rwl, Rxn[ti]], writes=[RPS[2 + ti % 2]])
                    if ti < 4:
                        act(xp[:, 3 + t0:3 + t0 + n], pa[:, :n], AF.Identity, reads=[RPS[ti % 2]], writes=[Rxp])
                    else:
                        act(xp[:, 2054:2086], pa[:, 0:32], AF.Identity, reads=[RPS[ti % 2]], writes=[Rxp])
                        act(xp[:, 2089:2121], pa[:, 32:64], AF.Identity, reads=[RPS[ti % 2]], writes=[Rxp])
                    cp(gg[:, t0:t0 + n], pb[:, :n], reads=[RPS[2 + ti % 2]], writes=[Rgg])
                    more(1)
                cw = lambda k: small[:, C_LCW + c * 4 + k:C_LCW + c * 4 + k + 1]
                cb = small[:, C_LCB + c:C_LCB + c + 1]
                for (s0, sl, s) in SEGS:
                    p0 = POFF[s]
                    tsc(u[:, s0:s0 + sl], xp[:, p0:p0 + sl], cw(0), cb, ALU.mult, ALU.add, reads=[Rxp, Rsmall], writes=[Ru])
                    for k in range(1, 4):
                        stt(u[:, s0:s0 + sl], xp[:, p0 + k:p0 + k + sl], cw(k), u[:, s0:s0 + sl], ALU.mult, ALU.add,
                            reads=[Rxp, Rsmall, Ru], writes=[Ru])
                    dma("sp", lconv_o[l, :, s, c, :], xp[:, p0 + sl:p0 + sl + 3], reads=[Rxp], key=Rxp)
                act(ub[:], u[:], AF.Identity, reads=[Ru], writes=[Rub])
                for ti, (t0, n) in enumerate(TILES):
                    pa, pb = PS[ti % 2], PS[2 + ti % 2]
                    mm_group(pa[:, :n], [(bd[0][c][:], ub[:, t0:t0 + n])], reads=[Rbd[0][c], Rub], writes=[RPS[ti % 2]])
                    mm_group(pb[:, :n], [(bd[1][c][:], ub[:, t0:t0 + n])], reads=[Rbd[1][c], Rub], writes=[RPS[2 + ti % 2]])
                    act(r[:, t0:t0 + n], pa[:, :n], AF.Sigmoid, reads=[RPS[ti % 2], Rsmall], writes=[Rr],
                        bias=small[:, C_LBA + c:C_LBA + c + 1])
                    act(ig[:, t0:t0 + n], pb[:, :n], AF.Sigmoid, reads=[RPS[2 + ti % 2], Rsmall], writes=[Rig],
                        bias=small[:, C_LBX + c:C_LBX + c + 1])
                    more(1)
                act(a[:], r[:], AF.Exp, reads=[Rr, Rc1], writes=[Ra], scale=c1t[:, c, 2:3])
                act(t1[:], r[:], AF.Exp, reads=[Rr, Rc1], writes=[Rt1], scale=c1t[:, c, 3:4])
                tsc(t1[:], t1[:], -1.0, 1.0, ALU.mult, ALU.add, reads=[Rt1], writes=[Rt1])
                tsc(t1[:], t1[:], 1e-30, None, ALU.max, None, reads=[Rt1], writes=[Rt1])
                act(t1[:], t1[:], AF.Sqrt, reads=[Rt1], writes=[Rt1])
                tt(ig[:], ig[:], u[:], ALU.mult, reads=[Rig, Ru], writes=[Rig])
                tt(ig[:], ig[:], t1[:], ALU.mult, reads=[Rig, Rt1], writes=[Rig])
                for (s0, sl, s) in SEGS:
                    init = 0.0 if s == 0 else h0sb[:, s - 1, c:c + 1]
                    scan(hh[:, s0:s0 + sl], a[:, s0:s0 + sl], ig[:, s0:s0 + sl], init, reads=[Ra, Rig, Rh0], writes=[Rhh])
                    dma("sp", lh_o[l, :, s, c:c + 1], hh[:, s0 + sl - 1:s0 + sl], reads=[Rhh], key=Rhh, slow=True)
                act(t1[:], gg[:], AF.Square, reads=[Rgg], writes=[Rt1])
                tsc(t1[:], t1[:], 0.044715, 1.0, ALU.mult, ALU.add, reads=[Rt1], writes=[Rt1])
                tt(t1[:], t1[:], gg[:], ALU.mult, reads=[Rt1, Rgg], writes=[Rt1])
                act(t1[:], t1[:], AF.Sigmoid, reads=[Rt1], writes=[Rt1], scale=1.5957691216057308)
                tt(gg[:], gg[:], t1[:], ALU.mult, reads=[Rgg, Rt1], writes=[Rgg])
                tt(ymix[:, c, :], hh[:], gg[:], ALU.mult, reads=[Rhh, Rgg], writes=[Rym[c]])
            more(100)
            P.barrier()
            sb.release(ms)
            sbx.release(mx)

        lru()
        if STAGE_SUB >= 2:
            ssd(l, ymix, Rym, xn, Rxn, winv)
        if STAGE_SUB >= 3:
            fox(l, ymix, Rym, xn, Rxn, winv)

        if DBG:
            md = sb.mark()
            stg = [sb.alloc("dstg", [128, 512], F32) for _ in range(2)]
            Rstg = [Res("dstg0"), Res("dstg1")]
            kk = 0
            for c in range(8):
                for ti, (t0, n) in enumerate(TILES):
                    b = kk % 2
                    kk += 1
                    cp(stg[b][:, :n], ymix[:, c, t0:t0 + n], reads=[Rym[c]], writes=[Rstg[b]])
                    dma("sp", dbg_o[:, c, t0:t0 + n], stg[b][:, :n], reads=[Rstg[b]], key=Rstg[b])
            P.barrier()
            sb.release(md)
        for ti, (t0, n) in enumerate(TILES):
            dma("sp", X[:, :, t0:t0 + n], xspill[:, :, t0:t0 + n], reads=[Rspill], writes=[RX[ti]], key=RX[ti])
        m2 = sb.mark()
        wo = sb.alloc("wo", [128, 8, 1024], BF16)
        Rwo = Res("wo")
        dma("pool", wo[:], wout_d[l].rearrange("(kc p) n -> p kc n", p=128), writes=[Rwo], key=Rwo)
        scr = alloc_norm_scratch()
        for half in ([0, 1], [2, 3, 4]):
            t_lo = TILES[half[0]][0]
            t_hi = TILES[half[-1]][0] + TILES[half[-1]][1]
            m3 = sb.mark()
            yacc = sb.alloc("yacc", [128, 8, t_hi - t_lo], F32)
            Ry = {ti: Res(f"y{ti}") for ti in half}
            cnt = 0
            for ti in half:
                t0, n = TILES[ti]
                for oc in range(8):
                    g = cnt % 4
                    cnt += 1
                    mm_group(PS[g][:, :n], [(wo[:, kc, oc * 128:(oc + 1) * 128], ymix[:, kc, t0:t0 + n]) for kc in range(8)],
                             reads=[Rwo] + Rym, writes=[RPS[g]])
                    if oc % 2 == 0:
                        act(yacc[:, oc, t0 - t_lo:t0 - t_lo + n], PS[g][:, :n], AF.Identity, reads=[RPS[g]], writes=[Ry[ti]])
                    else:
                        cp(yacc[:, oc, t0 - t_lo:t0 - t_lo + n], PS[g][:, :n], reads=[RPS[g]], writes=[Ry[ti]])
            post_norm(1, half, yacc, Ry, t_lo, scr)
            P.barrier()
            sb.release(m3)
        P.barrier()
        sb.release(m_arena)

    nsub = 0
    done = False
    m0_ = sb.mark()
    wm0 = [sb.alloc("wm", [128, 8, 512], BF16) for _ in range(2)]
    Rwm0 = [Res("wm0"), Res("wm1")]
    for th in mod_thunks(0, [0, 1], wm0, Rwm0):
        th()
    P.barrier()
    sb.release(m0_)
    for l in range(2):
        CUR[0] = l
        for kind in ("ffn0", "mix", "ffn1"):
            if nsub >= STAGE:
                done = True
                break
            if kind == "ffn0":
                ffn(l, 0, 0)
            elif kind == "mix":
                mix(l)
            else:
                ffn(l, 1, 2)
            nsub += 1
        if done:
            break

    for t, (t0, n) in enumerate(TILES):
        dma("sp", yT_o[:, :, t0:t0 + n], X[:, :, t0:t0 + n], reads=[RX[t]], key=RX[t])
    P.barrier()
    P.op("sp", lambda e: e.nop())
    P.emit(st)
    st.close()
    print(f"[mk] ops={len(P.ops)} sems={P.n_sems} sbuf_peak={sb.peak}")
    return nc


def _prep_core(core, I):
    f = np.float32
    b = core
    s0, s1 = 2 * core, 2 * core + 2
    xcat = np.concatenate([I["x_prompt"][b], I["x_sample"][s0], I["x_sample"][s0 + 1]], axis=0)
    xT = np.ascontiguousarray(xcat.reshape(T, 8, 128).transpose(2, 1, 0)).astype(f)
    ccat = np.concatenate([I["c_prompt"][b:b + 1], I["c_sample"][s0:s1]], axis=0)
    cT = np.ascontiguousarray(ccat.reshape(3, 8, 128).transpose(2, 1, 0)).astype(f)
    m = {"xT": xT, "cT": cT}
    m["ckT"] = np.ascontiguousarray(I["cache_fox_k"][:, s0:s1].transpose(0, 1, 3, 4, 2))
    a = I["cache_fox_v"][:, s0:s1].reshape(2, 2, 32, 128, 8, 64)
    m["cvh"] = np.ascontiguousarray(a.transpose(0, 1, 4, 3, 2, 5)).reshape(2, 2, 8, 128, 2048)
    m["clfT"] = np.ascontiguousarray(I["cache_fox_logf"][:, s0:s1].transpose(0, 1, 3, 2))
    a = I["state_lru_conv"][:, s0:s1].reshape(2, 2, 3, 2, 128)
    m["lconvT"] = np.ascontiguousarray(a.transpose(0, 4, 1, 3, 2))
    a = I["state_lru_h"][:, s0:s1].reshape(2, 2, 2, 128)
    m["lhT"] = np.ascontiguousarray(a.transpose(0, 3, 1, 2))
    a = I["state_ssd_conv"][:, s0:s1].reshape(2, 2, 3, 6, 128)
    m["sconvT"] = np.ascontiguousarray(a.transpose(0, 4, 1, 3, 2))
    m["sh0T"] = np.ascontiguousarray(I["state_ssd_h"][:, s0:s1].transpose(0, 1, 2, 4, 3))
    m["sh0f"] = np.ascontiguousarray(I["state_ssd_h"][:, s0:s1])
    return m


def _prep_shared(I):
    f = np.float32
    sm = np.zeros((2, 128, NSMALL), f)

    def fm(a, nch):
        sh = a.shape[:-1]
        a = a.reshape(sh + (nch, 128))
        return np.moveaxis(a, -1, 0)

    for l in range(2):
        sm[l, :, 0:24] = fm(I["norm_pre"][l], 8).reshape(128, 24)
        sm[l, :, 24:48] = fm(I["norm_post"][l], 8).reshape(128, 24)
        sm[l, :, 48:120] = I["b_mod"][l].reshape(72, 128).T
        sm[l, :, 120:128] = fm(I["lru_conv_w"][l], 2).transpose(0, 2, 1).reshape(128, 8)
        sm[l, :, 128:130] = fm(I["lru_conv_b"][l], 2)
        sm[l, :, 130:132] = fm(I["lru_ba"][l], 2)
        sm[l, :, 132:134] = fm(I["lru_bx"][l], 2)
        sm[l, :, 134:136] = fm(I["lru_lambda"][l], 2)
        sm[l, :, 136:160] = fm(I["ssd_conv_w"][l], 6).transpose(0, 2, 1).reshape(128, 24)
        sm[l, :, 160:166] = fm(I["ssd_conv_b"][l], 6)
        sm[l, :, 166:168] = fm(np.repeat(I["ssd_d"][l], 64), 2)
        sm[l, :, 168:170] = fm(I["ssd_norm_w"][l], 2)
    hp = np.zeros((2, 8, 4), f)
    hp[:, :, 0] = I["fox_f_bias"]
    hp[:, 0:4, 1] = I["ssd_dt_bias"]
    hp[:, 0:4, 2] = I["ssd_a_log"]
    sh = {"smallp": sm, "headp": hp}
    for k in ["w_mod", "ffn_w_gate", "ffn_w_up", "ffn_w_down", "w_in", "w_out", "lru_wa", "lru_wx"]:
        sh[k] = np.ascontiguousarray(I[k], dtype=f)
    return sh


_NC_CACHE = {}


def kernel(**inputs):
    I = {k: np.asarray(v) for k, v in inputs.items()}
    if "nc" not in _NC_CACHE:
        _NC_CACHE["nc"] = build_program()
    nc = _NC_CACHE["nc"]
    shared = _prep_shared(I)
    in_maps = []
    for c in range(NRUN):
        m = dict(shared)
        m.update(_prep_core(c, I))
        in_maps.append(m)
    res = run_bass_kernel_spmd(nc, in_maps, core_ids=list(range(NRUN)))
    R = res.results
    _NC_CACHE["last"] = R
    f = np.float32
    nb = 2 * NRUN
    y_p = np.zeros((8, NPR, D), f)
    y_s = np.zeros((16, NSQ, D), f)
    pk = np.zeros((2, 8, NPR, 8, 64), f); pv = np.zeros((2, 8, NPR, 8, 64), f); plf = np.zeros((2, 8, NPR, 8), f)
    plc = np.zeros((2, 8, 3, 256), f); plh = np.zeros((2, 8, 256), f); psc = np.zeros((2, 8, 3, 768), f)
    psh = np.zeros((2, 8, 4, 64, 128), f)
    sk = np.zeros((2, 16, NSQ, 8, 64), f); sv = np.zeros((2, 16, NSQ, 8, 64), f); slf = np.zeros((2, 16, NSQ, 8), f)
    slc = np.zeros((2, 16, 3, 256), f); slh = np.zeros((2, 16, 256), f); ssc = np.zeros((2, 16, 3, 768), f)
    ssh = np.zeros((2, 16, 4, 64, 128), f)
    for c in range(NRUN):
        r = R[c]
        y = r["yT"].transpose(2, 1, 0).reshape(T, D)
        y_p[c] = y[:NPR]
        y_s[2 * c] = y[NPR:NPR + NSQ]
        y_s[2 * c + 1] = y[NPR + NSQ:]
        k = r["fk_out"].transpose(0, 3, 2, 1).reshape(2, T, 8, 64)
        v = r["fv_out"].reshape(2, T, 8, 64)
        lf = r["logfT"].transpose(0, 2, 1)
        pk[:, c] = k[:, :NPR]; pv[:, c] = v[:, :NPR]; plf[:, c] = lf[:, :NPR]
        for si in range(2):
            sl = slice(NPR + si * NSQ, NPR + (si + 1) * NSQ)
            sk[:, 2 * c + si] = k[:, sl]; sv[:, 2 * c + si] = v[:, sl]; slf[:, 2 * c + si] = lf[:, sl]
        lc = r["lconv_o"].transpose(0, 2, 4, 3, 1).reshape(2, 3, 3, 256)
        lh = r["lh_o"].transpose(0, 2, 3, 1).reshape(2, 3, 256)
        sc = r["sconv_o"].transpose(0, 2, 4, 3, 1).reshape(2, 3, 3, 768)
        sh = r["sh_o"]
        plc[:, c] = lc[:, 0]; plh[:, c] = lh[:, 0]; psc[:, c] = sc[:, 0]; psh[:, c] = sh[:, 0]
        for si in range(2):
            slc[:, 2 * c + si] = lc[:, 1 + si]; slh[:, 2 * c + si] = lh[:, 1 + si]
            ssc[:, 2 * c + si] = sc[:, 1 + si]; ssh[:, 2 * c + si] = sh[:, 1 + si]
    return (y_p, y_s, pk, pv, plf, plc, plh, psc, psh, sk, sv, slf, slc, slh, ssc, ssh)
```
